# Optimizing a Trainium2 kernel written in Bass

```python
import math, functools
import jax, jax.numpy as jnp
from jax import lax
import numpy as np

D_MODEL = 1024
BATCH = 8
SEQ = 2048
DEPTH = 1
DEC_BATCH = 128
DEC_SEQ = 8
PAST_LEN = 16384
PAGE_SIZE = 128

D_MIX = D_MODEL
D_LRU = D_MIX // 2
D_POOL = D_MIX - D_LRU
N_LRU_HEADS = 8
LRU_HEAD_DIM = D_LRU // N_LRU_HEADS
LRU_CONV = 4
LRU_C = 8.0
POOL_WINDOWS = (2, 4, 8, 16)
N_POOL_GROUPS = len(POOL_WINDOWS)
POOL_GROUP_DIM = D_POOL // N_POOL_GROUPS
POOL_BUF = max(POOL_WINDOWS) - 1
D_IN = 2 * D_LRU + D_POOL
D_FF = 3 * D_MODEL
FFN_CONV = 3
EPS = 1e-6

kernel_name = "hymba_style_rglru_pool_convffn_step"


def rmsnorm(x, g):
    xf = x.astype(jnp.float32)
    y = xf * lax.rsqrt(jnp.mean(xf * xf, axis=-1, keepdims=True) + EPS)
    return (y * g.astype(jnp.float32)).astype(x.dtype)


def causal_dwconv(x, buf, w, b):
    k = w.shape[0]
    s = x.shape[1]
    xp = jnp.concatenate([buf.astype(x.dtype), x], axis=1)
    y = xp[:, 0:s] * w[0]
    for j in range(1, k):
        y = y + xp[:, j:j + s] * w[j]
    y = y + b
    new_buf = xp[:, xp.shape[1] - (k - 1):]
    return y, new_buf.astype(buf.dtype)


def rg_lru(x, h0, wa, ba, wx, bx, lam):
    bsz, s, _ = x.shape
    xh = x.reshape(bsz, s, N_LRU_HEADS, LRU_HEAD_DIM)
    r = jax.nn.sigmoid(jnp.einsum('bshi,hij->bshj', xh, wa).reshape(bsz, s, D_LRU) + ba)
    i = jax.nn.sigmoid(jnp.einsum('bshi,hij->bshj', xh, wx).reshape(bsz, s, D_LRU) + bx)
    log_a = -LRU_C * r.astype(jnp.float32) * jax.nn.softplus(-lam.astype(jnp.float32))
    a = jnp.exp(log_a)
    mult = jnp.sqrt(-jnp.expm1(2.0 * log_a))
    bterm = mult * (i * x).astype(jnp.float32)
    bterm = bterm.at[:, 0].add(a[:, 0] * h0.astype(jnp.float32))

    def combine(left, right):
        al, bl = left
        ar, br = right
        return al * ar, ar * bl + br

    _, h = lax.associative_scan(combine, (a, bterm), axis=1)
    return h.astype(x.dtype), h[:, -1].astype(h0.dtype)


def pool_mixer(x, buf, wg, scale, start):
    bsz, s, _ = x.shape
    xp = jnp.concatenate([buf.astype(x.dtype), x], axis=1).astype(jnp.float32)
    cs = jnp.concatenate([jnp.zeros((bsz, 1, D_POOL), jnp.float32), jnp.cumsum(xp, axis=1)], axis=1)
    pos = start + jnp.arange(s)
    xt = xp[:, POOL_BUF:]
    outs = []
    for g, w in enumerate(POOL_WINDOWS):
        c0, c1 = g * POOL_GROUP_DIM, (g + 1) * POOL_GROUP_DIM
        hi = cs[:, POOL_BUF + 1:POOL_BUF + 1 + s, c0:c1]
        lo = cs[:, POOL_BUF + 1 - w:POOL_BUF + 1 - w + s, c0:c1]
        cnt = jnp.minimum(w, pos + 1).astype(jnp.float32)[None, :, None]
        outs.append((hi - lo) / cnt - xt[:, :, c0:c1])
    pooled = jnp.stack(outs, axis=2)
    y = jnp.einsum('bsgi,gij->bsgj', pooled, wg.astype(jnp.float32)).reshape(bsz, s, D_POOL)
    y = y * scale.astype(jnp.float32)
    new_buf = xp[:, xp.shape[1] - POOL_BUF:]
    return y.astype(x.dtype), new_buf.astype(buf.dtype)


def layer(x, lru_buf, h0, pool_buf, ffn_buf, start,
          norm1_g, w_in, lru_conv_w, lru_conv_b, lru_wa, lru_ba, lru_wx, lru_bx, lru_lambda,
          pool_w, pool_scale, w_out, norm2_g, ffn_up, ffn_conv_w, ffn_conv_b, ffn_down):
    h = rmsnorm(x, norm1_g)
    z = jnp.einsum('bsd,de->bse', h, w_in)
    zx = z[..., :D_LRU]
    zg = z[..., D_LRU:2 * D_LRU]
    zp = z[..., 2 * D_LRU:]
    xc, new_lru_buf = causal_dwconv(zx, lru_buf, lru_conv_w, lru_conv_b)
    hl, h_last = rg_lru(xc, h0, lru_wa, lru_ba, lru_wx, lru_bx, lru_lambda)
    lru_out = hl * jax.nn.gelu(zg)
    pool_out, new_pool_buf = pool_mixer(zp, pool_buf, pool_w, pool_scale, start)
    mixed = jnp.concatenate([lru_out, pool_out], axis=-1)
    x = x + jnp.einsum('bse,ed->bsd', mixed, w_out)
    h2 = rmsnorm(x, norm2_g)
    u = jnp.einsum('bsd,df->bsf', h2, ffn_up)
    uc, new_ffn_buf = causal_dwconv(u, ffn_buf, ffn_conv_w, ffn_conv_b)
    gate = uc[..., :D_FF]
    val = uc[..., D_FF:]
    x = x + jnp.einsum('bsf,fd->bsd', jax.nn.gelu(gate) * val, ffn_down)
    return x, new_lru_buf, h_last, new_pool_buf, new_ffn_buf


def setup_inputs(seed: int = 0) -> dict:
    key = jax.random.key(seed)
    ks = jax.random.split(key, 24)
    f32 = jnp.float32
    nrm = lambda k, shape, sc: jax.random.normal(k, shape, f32) * sc
    a0 = jax.random.uniform(ks[13], (DEPTH, D_LRU), f32, minval=0.9, maxval=0.999)
    return {
        "x_prompt": nrm(ks[0], (BATCH, SEQ, D_MODEL), 1.0),
        "x_sample": nrm(ks[1], (DEC_BATCH, DEC_SEQ, D_MODEL), 1.0),
        "state_lru_conv": nrm(ks[2], (DEPTH, DEC_BATCH, LRU_CONV - 1, D_LRU), 1.0),
        "state_lru_h": nrm(ks[3], (DEPTH, DEC_BATCH, D_LRU), 0.5),
        "state_pool": nrm(ks[4], (DEPTH, DEC_BATCH, POOL_BUF, D_POOL), 1.0),
        "state_ffn_conv": nrm(ks[5], (DEPTH, DEC_BATCH, FFN_CONV - 1, 2 * D_FF), 1.0),
        "norm1_g": 1.0 + nrm(ks[6], (DEPTH, D_MODEL), 0.05),
        "w_in": nrm(ks[7], (DEPTH, D_MODEL, D_IN), D_MODEL ** -0.5),
        "lru_conv_w": nrm(ks[8], (DEPTH, LRU_CONV, D_LRU), LRU_CONV ** -0.5),
        "lru_conv_b": nrm(ks[9], (DEPTH, D_LRU), 0.01),
        "lru_wa": nrm(ks[10], (DEPTH, N_LRU_HEADS, LRU_HEAD_DIM, LRU_HEAD_DIM), LRU_HEAD_DIM ** -0.5),
        "lru_ba": nrm(ks[11], (DEPTH, D_LRU), 0.01),
        "lru_wx": nrm(ks[12], (DEPTH, N_LRU_HEADS, LRU_HEAD_DIM, LRU_HEAD_DIM), LRU_HEAD_DIM ** -0.5),
        "lru_bx": nrm(ks[14], (DEPTH, D_LRU), 0.01),
        "lru_lambda": jnp.log(a0) - jnp.log1p(-a0),
        "pool_w": nrm(ks[15], (DEPTH, N_POOL_GROUPS, POOL_GROUP_DIM, POOL_GROUP_DIM), POOL_GROUP_DIM ** -0.5),
        "pool_scale": 0.5 + nrm(ks[16], (DEPTH, D_POOL), 0.05),
        "w_out": nrm(ks[17], (DEPTH, D_MIX, D_MODEL), D_MIX ** -0.5),
        "norm2_g": 1.0 + nrm(ks[18], (DEPTH, D_MODEL), 0.05),
        "ffn_up": nrm(ks[19], (DEPTH, D_MODEL, 2 * D_FF), D_MODEL ** -0.5),
        "ffn_conv_w": nrm(ks[20], (DEPTH, FFN_CONV, 2 * D_FF), FFN_CONV ** -0.5),
        "ffn_conv_b": nrm(ks[21], (DEPTH, 2 * D_FF), 0.01),
        "ffn_down": nrm(ks[22], (DEPTH, D_FF, D_MODEL), D_FF ** -0.5),
        "final_g": 1.0 + nrm(ks[23], (D_MODEL,), 0.05),
    }


def reference(x_prompt, x_sample, state_lru_conv, state_lru_h, state_pool, state_ffn_conv,
              norm1_g, w_in, lru_conv_w, lru_conv_b, lru_wa, lru_ba, lru_wx, lru_bx, lru_lambda,
              pool_w, pool_scale, w_out, norm2_g, ffn_up, ffn_conv_w, ffn_conv_b, ffn_down, final_g):
    dt = x_prompt.dtype
    xp, xs = x_prompt, x_sample
    p_lc, p_h, p_pb, p_fb = [], [], [], []
    s_lc, s_h, s_pb, s_fb = [], [], [], []
    for l in range(DEPTH):
        params = (norm1_g[l], w_in[l], lru_conv_w[l], lru_conv_b[l], lru_wa[l], lru_ba[l],
                  lru_wx[l], lru_bx[l], lru_lambda[l], pool_w[l], pool_scale[l], w_out[l],
                  norm2_g[l], ffn_up[l], ffn_conv_w[l], ffn_conv_b[l], ffn_down[l])
        xp, a1, a2, a3, a4 = layer(
            xp,
            jnp.zeros((BATCH, LRU_CONV - 1, D_LRU), dt),
            jnp.zeros((BATCH, D_LRU), dt),
            jnp.zeros((BATCH, POOL_BUF, D_POOL), dt),
            jnp.zeros((BATCH, FFN_CONV - 1, 2 * D_FF), dt),
            0, *params)
        p_lc.append(a1); p_h.append(a2); p_pb.append(a3); p_fb.append(a4)
        xs, b1, b2, b3, b4 = layer(
            xs, state_lru_conv[l], state_lru_h[l], state_pool[l], state_ffn_conv[l],
            PAST_LEN, *params)
        s_lc.append(b1); s_h.append(b2); s_pb.append(b3); s_fb.append(b4)
    y_prompt = rmsnorm(xp, final_g)
    y_sample = rmsnorm(xs, final_g)
    return (y_prompt, y_sample,
            jnp.stack(p_lc), jnp.stack(p_h), jnp.stack(p_pb), jnp.stack(p_fb),
            jnp.stack(s_lc), jnp.stack(s_h), jnp.stack(s_pb), jnp.stack(s_fb))
```

```python
from contextlib import ExitStack

import numpy as np
import concourse.bass as bass
import concourse.mybir as mybir
from concourse.bass_utils import run_bass_kernel_spmd

F32 = mybir.dt.float32
BF16 = mybir.dt.bfloat16
AF = mybir.ActivationFunctionType
ALU = mybir.AluOpType

NCORES = 8
D = 1024
SEQ = 2048
NS = 16
LS = 8
DFF = 3072
EPS = 1e-6
POOL_W = (2, 4, 8, 16)

V_CW, V_CB, V_BA, V_BX, V_LAM, V_PS, V_FCW, V_FCB, V_G1, NV = 0, 16, 20, 24, 28, 32, 36, 180, 228, 236

GROUPS = [(0, 768, False), (768, 768, False), (1536, 512, True)]
GMAX = 768
ARENA_WORDS = 33920


class T:
    def __init__(self, ap, excl=False):
        self.ap = ap
        self.lastw = []
        self.readers = []
        self.excl = excl


class Q:
    def __init__(self, name, sem_key):
        self.name = name
        self.sem_key = sem_key
        self.cnt = 0
        self.waited = {}
        self.ops = []


class Prog:
    def __init__(self):
        self.q = {n: Q(n, "s_" + n) for n in ("pe", "act", "dve", "pool", "sp")}
        self.dma_cnt = {}
        self.ring = {}

    def _wait(self, q, deps):
        for (k, v) in deps:
            if q.name == "pe" and k == q.sem_key:
                continue
            if q.waited.get(k, 0) < v:
                q.ops.append(("wait", k, v))
                q.waited[k] = v

    @staticmethod
    def _deps(reads, writes, extra):
        deps = set(extra)
        for b in reads:
            deps.update(b.lastw)
            if b.excl:
                deps.update(b.readers)
        for b in writes:
            deps.update(b.lastw)
            deps.update(b.readers)
        return deps

    @staticmethod
    def _update(ev, reads, writes):
        for b in reads:
            b.readers.append(ev)
        for b in writes:
            b.lastw = [ev]
            b.readers = []

    def op(self, eng, fn, reads=(), writes=(), extra=()):
        q = self.q[eng]
        self._wait(q, self._deps(reads, writes, extra))
        q.cnt += 1
        ev = (q.sem_key, q.cnt)
        q.ops.append(("op", fn))
        self._update(ev, reads, writes)
        return ev

    NRING = 12

    def dma(self, eng, dsem, out, in_, reads=(), writes=(), extra=()):
        q = self.q[eng]
        i = self.ring.get(eng, 0)
        self.ring[eng] = i + 1
        dsem = "r_%s_%d" % (eng, i % self.NRING)
        prev = self.dma_cnt.get(dsem, 0)
        if prev:
            self._wait(q, [(dsem, prev)])
        self._wait(q, self._deps(reads, writes, extra))
        self.dma_cnt[dsem] = self.dma_cnt.get(dsem, 0) + 16
        ev = (dsem, self.dma_cnt[dsem])
        q.ops.append(("dma", out, in_, dsem))
        self._update(ev, reads, writes)
        return ev

    def barrier(self):
        evs = [(q.sem_key, q.cnt) for q in self.q.values() if q.cnt > 0]
        for q in self.q.values():
            self._wait(q, evs)

    def final_wait(self, eng):
        evs = [(q.sem_key, q.cnt) for q in self.q.values() if q.cnt > 0]
        evs += [(k, v) for k, v in self.dma_cnt.items()]
        self._wait(self.q[eng], evs)


def build_program():
    nc = bass.Bass("TRN2", target_bir_lowering=False)
    P = Prog()

    def din(name, shape):
        return nc.dram_tensor(name, list(shape), F32, kind="ExternalInput").ap()

    def dout(name, shape):
        return nc.dram_tensor(name, list(shape), F32, kind="ExternalOutput").ap()

    xp = din("xp", [SEQ, D])
    xs = din("xs", [NS * LS, D])
    vecs_d = din("vecs", [128, NV])
    g1r_d = din("g1r", [128, D])
    g2r_d = din("g2r", [128, D])
    gfr_d = din("gfr", [128, D])
    ident_d = din("ident", [128, 128])
    w_in_d = din("w_in", [D, 1536])
    w_out_d = din("w_out", [D, D])
    up_d = din("ffn_up", [D, 2 * DFF])
    down_d = din("ffn_down", [DFF, D])
    wa_d = din("wa", [8, 64, 64])
    wx_d = din("wx", [8, 64, 64])
    pw_d = din("pw", [4, 128, 128])
    stlc_d = din("stlc", [128, 4 * NS * 3])
    sth_d = din("sth", [128, 4 * NS])
    stpool_d = din("stpool", [128, 4 * NS * 15])
    stffn_d = din("stffn", [128, 48 * NS * 2])

    y_d = dout("y", [SEQ + NS * LS, D])
    olc_d = dout("o_lc", [128, 4 * 17 * 3])
    oh_d = dout("o_h", [128, 4 * 17])
    opool_d = dout("o_pool", [128, 4 * 17 * 15])
    offn_d = dout("o_ffn", [128, 48 * 17 * 2])

    w_in_v = w_in_d.rearrange("(kc p) n -> p kc n", p=128)
    w_out_v = w_out_d.rearrange("(kc p) n -> p kc n", p=128)
    up_v = up_d.rearrange("(kc p) n -> p kc n", p=128)
    down_v = down_d.rearrange("(c p) n -> p c n", p=128)

    es = ExitStack()
    with es:
        def sb(name, shape, dt=F32):
            return es.enter_context(nc.sbuf_tensor("sb_" + name, list(shape), dt))

        sems = {}
        for k in ("s_pe", "s_act", "s_dve", "s_pool", "s_sp"):
            sems[k] = es.enter_context(nc.semaphore(k))
        for eng_ in ("sp", "pool"):
            for i_ in range(Prog.NRING):
                k = "r_%s_%d" % (eng_, i_)
                sems[k] = es.enter_context(nc.semaphore(k))

        X_t = sb("X", [128, 6, D])
        X = [T(X_t[:, i, :]) for i in range(6)]
        h2T_t = sb("h2T", [128, 8, 2 + GMAX], BF16)
        h2T_hist = T(h2T_t[:, :, 0:2])
        h2Tb = [T(h2T_t[:, :, 2 + i * 128: 2 + (i + 1) * 128]) for i in range(6)]
        grep = T(sb("grep", [128, D])[:, :])
        vecs_t = sb("vecs", [128, NV])
        vecs = T(vecs_t[:, :])
        der_t = sb("der", [128, 16])
        der = T(der_t[:, :])
        ident_t = sb("ident", [128, 128], BF16)
        ident = T(ident_t[:, :])
        identf_t = sb("identf", [128, 128], F32)
        identf = T(identf_t[:, :])
        olc_t = sb("olc", [128, 4, 17, 3])
        oh_t = sb("oh", [128, 4, 17])
        opool_t = sb("opool", [128, 4, 17, 15])
        offn_t = sb("offn", [128, 48, 17, 2])
        olc, oh, opool, offn = T(olc_t), T(oh_t), T(opool_t), T(offn_t)
        zxs_t = sb("zxs", [128, 4, NS, 3 + LS])
        zxs = [T(zxs_t[:, c, :, :]) for c in range(4)]
        zps_t = sb("zps", [128, 4, NS, 15 + LS], BF16)
        zps = [T(zps_t[:, g, :, :]) for g in range(4)]
        sth_t = sb("sth", [128, 4, NS])
        stffn_t = sb("stffn", [128, 48, NS, 2])
        sth, stffn = T(sth_t), T(stffn_t)
        stats_t = sb("stats", [128, 192])
        h1hist_t = sb("h1hist", [128, 8, 4], BF16)
        h1hist = T(h1hist_t)
        histx_t = sb("histx", [128, 4, 3])
        histx = [T(histx_t[:, c, :]) for c in range(4)]
        histp_t = sb("histp", [128, 4, 15], BF16)
        histp = [T(histp_t[:, g, :]) for g in range(4)]
        dvec_t = sb("dvec", [128, 4, 16])
        dvec = T(dvec_t)
        ones16_t = sb("ones16", [128, 16])
        ones16 = T(ones16_t)
        wa_bd = sb("wa_bd", [128, 4, 128], BF16)
        wx_bd = sb("wx_bd", [128, 4, 128], BF16)
        wab_T = T(wa_bd)
        wxb_T = T(wx_bd)
        pwb = sb("pwb", [128, 4, 128], BF16)
        pws = sb("pws", [128, 4, 128], BF16)
        pwn = sb("pwn", [128, 4, 128], BF16)
        pwb_T, pws_T, pwn_T = T(pwb), T(pws), T(pwn)
        arena_t = sb("arena", [128, ARENA_WORDS])
        stpool_t = arena_t[:, 31760:31760 + 4 * NS * 15].rearrange("p (a b c) -> p a b c", a=4, b=NS)
        stlc_t = arena_t[:, 20548:20548 + 4 * NS * 3].rearrange("p (a b c) -> p a b c", a=4, b=NS)
        stlc, stpool = T(stlc_t), T(stpool_t)

        ps_t = es.enter_context(nc.psum_tensor("ps", [128, 8, 512], F32))
        banks = [T(ps_t[:, i, :], excl=True) for i in range(8)]
        bank_rr = [0]

        def newbank():
            b = banks[bank_rr[0] % 8]
            bank_rr[0] += 1
            return b

        def newbank_pair():
            if bank_rr[0] % 2:
                bank_rr[0] += 1
            i = bank_rr[0] % 8
            bank_rr[0] += 2
            return i, banks[i], banks[i + 1]

        stat_i = [0]

        def newstat():
            i = stat_i[0]
            stat_i[0] += 1
            return T(stats_t[:, i:i + 1])

        class Carver:
            def __init__(self):
                self.off = 0

            def f32(self, shape):
                n = int(np.prod(shape[1:]))
                a = arena_t[:, self.off:self.off + n]
                self.off += n
                assert self.off <= ARENA_WORDS, self.off
                return self._shape(a, shape)

            def bf16(self, shape):
                n = int(np.prod(shape[1:]))
                nw = (n + 1) // 2
                a = arena_t[:, self.off:self.off + nw].bitcast(BF16)[:, 0:n]
                self.off += nw
                assert self.off <= ARENA_WORDS, self.off
                return self._shape(a, shape)

            @staticmethod
            def _shape(a, shape):
                if len(shape) == 2:
                    return a
                if len(shape) == 3:
                    return a.rearrange("p (a b) -> p a b", a=shape[1])
                if len(shape) == 4:
                    return a.rearrange("p (a b c) -> p a b c", a=shape[1], b=shape[2])
                if len(shape) == 5:
                    return a.rearrange("p (a b c d) -> p a b c d", a=shape[1], b=shape[2], c=shape[3])
                raise ValueError(shape)

        DZ0, DZ1 = 21504, 31760
        cm = Carver()
        cm.off = DZ0
        w_in_bf = cm.bf16([128, 8, 1536])
        w_in_T = [T(w_in_bf[:, :, 0:768]), T(w_in_bf[:, :, 768:1536])]
        HX = 3
        h1T_a = cm.bf16([128, 8, HX + GMAX + 1])
        h1T_hist = T(h1T_a[:, :, 0:HX])
        h1Tb = [T(h1T_a[:, :, HX + i * 128: HX + (i + 1) * 128]) for i in range(6)]
        xn1 = [T(cm.f32([128, D])) for _ in range(1)]
        assert cm.off <= DZ1, cm.off
        cm.off = 0
        w_out_bf = cm.bf16([128, 8, D])
        w_out_T = T(w_out_bf)
        mixedT_a = cm.bf16([128, 8, GMAX])
        mixedTb = [T(mixedT_a[:, :, i * 128:(i + 1) * 128]) for i in range(6)]
        LSEG = 384
        lru_sets = []
        for s_ in range(4):
            d = {}
            d["ext"] = T(cm.f32([128, 1, 3 + LSEG]))
            d["acc"] = T(cm.f32([128, LSEG]))
            d["xcb"] = T(cm.bf16([128, LSEG]))
            d["tr"] = T(cm.f32([128, LSEG]))
            d["ti"] = T(cm.f32([128, LSEG]))
            d["a"] = T(cm.f32([128, LSEG]))
            d["a2"] = T(cm.f32([128, LSEG]))
            lru_sets.append(d)
        gz_one = [T(cm.f32([128, LSEG])) for _ in range(4)]
        gz2 = [gz_one, gz_one]
        zpe = [T(cm.bf16([128, 1, 15 + LSEG])) for _ in range(4)]
        xn_m = [T(cm.bf16([128, D])) for _ in range(3)]
        fixS = T(cm.f32([128, 16]))
        fixSd = T(cm.bf16([128, 16]))
        assert cm.off <= DZ0, cm.off
        mixer_words = DZ1

        cf = Carver()
        actT_a = cf.bf16([128, 24, GMAX])
        actTb = [T(actT_a[:, :, i * 128:(i + 1) * 128]) for i in range(6)]
        wd_bf = cf.bf16([128, 24, D])
        wd_T = [T(wd_bf[:, i * 2:(i + 1) * 2, :]) for i in range(12)]
        NST = 3
        upb = [cf.bf16([128, 2, 8, 256]) for _ in range(NST)]
        upb_T = [(T(u[:, 0, :, :]), T(u[:, 1, :, :])) for u in upb]
        NACC = 2
        accA = [cf.f32([128, 2, 2, 384]) for _ in range(NACC)]
        accT = [(T(a[:, 0, :, :]), T(a[:, 1, :, :])) for a in accA]
        gbuf = [T(cf.f32([128, 2, 384])) for _ in range(NACC)]
        acc_rr = [0]
        uexts = [T(cf.f32([128, 2, NS, 2 + LS])) for _ in range(2)]
        uext_halves = [(T(u.ap[:, 0, :, :]), T(u.ap[:, 1, :, :])) for u in uexts]
        ybuf = [T(cf.f32([128, D])) for _ in range(1)]
        ffn_words = cf.off
        assert max(mixer_words, ffn_words) <= ARENA_WORDS

        def act_fn(out, in_, func, **kw):
            return lambda e: e.activation(out=out, in_=in_, func=func, **kw)

        def vcol(off, c):
            return vecs_t[:, off + c: off + c + 1]

        def dcol(off, c):
            return der_t[:, off + c: off + c + 1]

        D_HBA, D_HBX, D_HC, D_C2 = 0, 4, 8, 12

        P.dma("sp", "d_misc", vecs_t[:, :], vecs_d[:, :], writes=[vecs])
        P.dma("sp", "d_misc", grep.ap, g1r_d[:, :], writes=[grep])
        P.dma("sp", "d_x", X[0].ap, xp[0:128, :], writes=[X[0]])
        P.dma("pool", "d_w", ident_t[:, :], ident_d[:, :], writes=[ident])
        P.dma("pool", "d_w", pwb[:, :, :], pw_d.rearrange("g i j -> i g j"), writes=[pwb_T])
        P.dma("pool", "d_w", w_in_bf[:, :, 0:768], w_in_v[:, :, 0:768], writes=[w_in_T[0]])
        P.dma("pool", "d_w", w_in_bf[:, :, 768:1536], w_in_v[:, :, 768:1536], writes=[w_in_T[1]])
        P.dma("sp", "d_misc", identf_t[:, :], ident_d[:, :], writes=[identf])
        P.dma("sp", "d_misc", stlc_t[:, :, :, :].rearrange("p a b c -> p (a b c)"), stlc_d[:, :], writes=[stlc])
        P.dma("sp", "d_misc", sth_t[:, :, :].rearrange("p a b -> p (a b)"), sth_d[:, :], writes=[sth])
        P.dma("sp", "d_misc", stpool_t[:, :, :, :].rearrange("p a b c -> p (a b c)"), stpool_d[:, :], writes=[stpool])
        P.dma("sp", "d_misc", stffn_t[:, :, :, :].rearrange("p a b c -> p (a b c)"), stffn_d[:, :], writes=[stffn])

        P.op("dve", lambda e: e.tensor_scalar(out=der_t[:, 0:8], in0=vecs_t[:, V_BA:V_BA + 8], scalar1=0.5,
                                              scalar2=None, op0=ALU.mult), reads=[vecs], writes=[der])
        tmp_e4 = T(stats_t[:, 176:180])
        tmp_s4 = T(stats_t[:, 180:184])
        P.op("act", act_fn(stats_t[:, 176:180], vecs_t[:, V_LAM:V_LAM + 4], AF.Exp, scale=-1.0),
             reads=[vecs], writes=[tmp_e4])
        P.op("act", act_fn(stats_t[:, 180:184], stats_t[:, 176:180], AF.Ln, bias=1.0),
             reads=[tmp_e4], writes=[tmp_s4])
        P.op("dve", lambda e: e.tensor_scalar(out=der_t[:, 8:12], in0=stats_t[:, 180:184], scalar1=-4.0,
                                              scalar2=None, op0=ALU.mult), reads=[tmp_s4], writes=[der])
        P.op("dve", lambda e: e.tensor_scalar(out=der_t[:, 12:16], in0=stats_t[:, 180:184], scalar1=-8.0,
                                              scalar2=None, op0=ALU.mult), reads=[tmp_s4], writes=[der])
        P.op("dve", lambda e: e.memset(histx_t[:, :, :], 0.0), writes=histx)
        P.op("dve", lambda e: e.memset(histp_t[:, :, :], 0.0), writes=histp)
        P.op("dve", lambda e: e.memset(ones16_t[:, :], 1.0), writes=[ones16])
        P.op("dve", lambda e: e.memset(dvec_t[:, :, :], 0.0), writes=[dvec])
        for g, w in enumerate(POOL_W):
            for t in range(w - 1):
                val = 1.0 / (t + 1) - 1.0 / w
                P.op("dve", lambda e, g=g, t=t, val=val: e.memset(dvec_t[:, g, t:t + 1], val), writes=[dvec])
        P.op("dve", lambda e: e.memset(h2T_t[:, :, 0:2], 0.0), writes=[h2T_hist])
        for c in range(4):
            P.op("dve", lambda e, c=c: e.tensor_copy(out=zxs_t[:, c, :, 0:3], in_=stlc_t[:, c, :, :]),
                 reads=[stlc], writes=[zxs[c]])
            P.op("dve", lambda e, c=c: e.tensor_copy(out=zps_t[:, c, :, 0:15], in_=stpool_t[:, c, :, :]),
                 reads=[stpool], writes=[zps[c]])
            P.op("dve", lambda e, c=c: e.tensor_copy(out=opool_t[:, c, 1:17, 0:7], in_=stpool_t[:, c, :, 8:15]),
                 reads=[stpool], writes=[opool])

        P.op("dve", lambda e: e.memset(wa_bd[:, :, :], 0.0), writes=[wab_T])
        P.op("dve", lambda e: e.memset(wx_bd[:, :, :], 0.0), writes=[wxb_T])
        for h in range(8):
            r0 = (h % 2) * 64
            P.dma("pool", "d_w", wa_bd[r0:r0 + 64, h // 2, r0:r0 + 64], wa_d[h, :, :], writes=[wab_T])
            P.dma("pool", "d_w", wx_bd[r0:r0 + 64, h // 2, r0:r0 + 64], wx_d[h, :, :], writes=[wxb_T])
        def scale_pool_weights():
            for g, w in enumerate(POOL_W):
                P.op("dve", lambda e, g=g, w=w: e.tensor_scalar(out=pws[:, g, :], in0=pwb[:, g, :], scalar1=1.0 / w,
                                                                scalar2=None, op0=ALU.mult), reads=[pwb_T], writes=[pws_T])
                P.op("dve", lambda e, g=g: e.tensor_scalar(out=pwn[:, g, :], in0=pwb[:, g, :], scalar1=-1.0,
                                                           scalar2=None, op0=ALU.mult), reads=[pwb_T], writes=[pwn_T])

        def load_x(gi):
            p0, npt, has_s = GROUPS[gi]
            nt = npt // 128
            for t in range(1 if gi == 0 else 0, nt):
                P.dma("sp", "d_x", X[t].ap, xp[p0 + t * 128: p0 + (t + 1) * 128, :], writes=[X[t]])
            if has_s:
                P.dma("sp", "d_x", X[nt].ap, xs[:, :], writes=[X[nt]])

        def norm_A(xt, grep_T, xn):
            ssq, sq, rs = newstat(), newstat(), newstat()
            P.op("act", act_fn(xn.ap, xt.ap, AF.Square, accum_out=ssq.ap), reads=[xt], writes=[xn, ssq])
            P.op("act", act_fn(sq.ap, ssq.ap, AF.Sqrt, scale=1.0 / D, bias=EPS), reads=[ssq], writes=[sq])
            P.op("dve", lambda e: e.reciprocal(out=rs.ap, in_=sq.ap), reads=[sq], writes=[rs])
            P.op("dve", lambda e: e.scalar_tensor_tensor(out=xn.ap, in0=xt.ap, scalar=rs.ap, in1=grep_T.ap,
                                                         op0=ALU.mult, op1=ALU.mult),
                 reads=[xt, rs, grep_T], writes=[xn])

        def norm1_A(t, extra=()):
            xt, xn = X[t], xn1[0]
            ssq, sq, rs = newstat(), newstat(), newstat()
            P.op("act", act_fn(xn.ap, xt.ap, AF.Square, accum_out=ssq.ap), reads=[xt], writes=[xn, ssq], extra=extra)
            P.op("act", act_fn(sq.ap, ssq.ap, AF.Sqrt, scale=1.0 / D, bias=EPS), reads=[ssq], writes=[sq])
            P.op("dve", lambda e: e.reciprocal(out=rs.ap, in_=sq.ap), reads=[sq], writes=[rs], extra=extra)
            P.op("act", act_fn(xn.ap, xt.ap, AF.Identity, scale=rs.ap), reads=[xt, rs], writes=[xn])

        def norm1_B(t, extra=()):
            xn = xn1[0]
            for half in range(2):
                bk = newbank()

                def tr(e, half=half, bk=bk):
                    ins = None
                    for j in range(4):
                        kc = half * 4 + j
                        ins = e.transpose(out=bk.ap[:, j * 128:(j + 1) * 128], in_=xn.ap[:, kc * 128:(kc + 1) * 128],
                                          identity=identf_t[:, :])
                    return ins
                P.op("pe", tr, reads=[xn, identf], writes=[bk], extra=extra)
                for j in range(4):
                    kc = half * 4 + j
                    P.op("act", act_fn(h1T_a[:, kc, HX + t * 128: HX + (t + 1) * 128], bk.ap[:, j * 128:(j + 1) * 128], AF.Identity,
                                       scale=vcol(V_G1, kc)), reads=[bk, vecs], writes=[h1Tb[t]])

        def norm_B(xn, dstT_blocks_ap, dst_T):
            bk = newbank()
            bkb = bk.ap[:, :].bitcast(BF16)

            def tr(e):
                ins = None
                for kc in range(8):
                    ins = e.transpose(out=bkb[:, kc * 128:(kc + 1) * 128], in_=xn.ap[:, kc * 128:(kc + 1) * 128],
                                      identity=ident_t[:, :])
                return ins
            P.op("pe", tr, reads=[xn, ident], writes=[bk])
            P.op("act", act_fn(dstT_blocks_ap, bkb.rearrange("p (k n) -> p k n", k=8), AF.Copy),
                 reads=[bk], writes=[dst_T])

        seg_ctr = [0]

        def mixer_segment(gi, col0, L, S, Ls, tiles, kind, first_seq, last_prompt):
            is_s = kind == "s"
            gz = gz2[seg_ctr[0] % 2]
            seg_ctr[0] += 1

            def as3(ap2):
                return ap2.rearrange("p (s l) -> p s l", s=S)

            h1 = [h1Tb[t] for t in tiles]
            mT = [mixedTb[t] for t in tiles]

            H = 0 if (is_s or first_seq) else HX

            def w_in_mm(m, hist=0):
                bk = newbank()

                def mm(e, m=m, bk=bk):
                    ins = None
                    for kc in range(8):
                        ins = e.matmul(bk.ap[:, 0:L + hist], lhsT=w_in_bf[:, kc, m * 128:(m + 1) * 128],
                                       rhs=h1T_a[:, kc, HX + col0 - hist: HX + col0 + L], start=(kc == 0), stop=(kc == 7))
                    return ins
                rd = h1 + [w_in_T[m // 6]]
                if hist:
                    rd.append(h1T_hist if col0 == 0 else h1Tb[col0 // 128 - 1])
                P.op("pe", mm, reads=rd, writes=[bk])
                return bk

            ext3 = {}
            extT = {}

            def fronta():
              if is_s:
                  for c in range(4):
                      bk = w_in_mm(c)
                      P.op("dve", lambda e, c=c, bk=bk: e.tensor_copy(out=zxs_t[:, c, :, 3:3 + LS], in_=as3(bk.ap[:, 0:L])),
                           reads=[bk], writes=[zxs[c]])
                      ext3[c], extT[c] = zxs_t[:, c, :, :], zxs[c]
              else:
                  zb = {}
                  for c in range(4):
                      bk = w_in_mm(c, hist=H)
                      zb[c] = bk
                      st = lru_sets[c]
                      P.op("act", act_fn(st["acc"].ap[:, 0:L], bk.ap[:, H:H + L], AF.Identity, scale=vcol(V_CW, 3 * 4 + c),
                                         bias=vcol(V_CB, c)), reads=[bk, vecs], writes=[st["acc"]])
                      if last_prompt:
                          P.op("act", act_fn(olc_t[:, c, 0, :], bk.ap[:, H + L - 3:H + L], AF.Copy), reads=[bk], writes=[olc])
                  for j in (2, 1, 0):
                      sh = 3 - j
                      for c in range(4):
                          st = lru_sets[c]
                          bk = zb[c]
                          if H:
                              o_, i_ = st["acc"].ap[:, 0:L], bk.ap[:, H - sh:H - sh + L]
                          else:
                              o_, i_ = st["acc"].ap[:, sh:L], bk.ap[:, 0:L - sh]
                          P.op("dve", lambda e, o_=o_, i_=i_, j=j, c=c: e.scalar_tensor_tensor(
                              out=o_, in0=i_, scalar=vcol(V_CW, j * 4 + c), in1=o_, op0=ALU.mult, op1=ALU.add),
                              reads=[bk, vecs], writes=[st["acc"]])
              for g, w in enumerate(POOL_W):
                  bk = w_in_mm(8 + g)
                  if is_s:
                      P.op("act", act_fn(zps_t[:, g, :, 15:15 + LS], as3(bk.ap[:, 0:L]), AF.Copy), reads=[bk], writes=[zps[g]])
                      P.op("act", act_fn(opool_t[:, g, 1:17, 7:15], as3(bk.ap[:, 0:L]), AF.Copy), reads=[bk], writes=[opool])
                  else:
                      P.op("pool", lambda e, g=g: e.tensor_copy(out=zpe[g].ap[:, 0, 0:15], in_=histp_t[:, g, :]),
                           reads=[histp[g]], writes=[zpe[g]])
                      P.op("act", act_fn(zpe[g].ap[:, :, 15:15 + L], as3(bk.ap[:, 0:L]), AF.Copy), reads=[bk], writes=[zpe[g]])
                      P.op("pool", lambda e, g=g: e.tensor_copy(out=histp_t[:, g, :], in_=zpe[g].ap[:, 0, L:L + 15]),
                           reads=[zpe[g]], writes=[histp[g]])
                      if last_prompt:
                          P.op("act", act_fn(opool_t[:, g, 0, :], bk.ap[:, L - 15:L], AF.Copy), reads=[bk], writes=[opool])
            acc3 = {c: lru_sets[c]["acc"].ap[:, 0:L].rearrange("p (s l) -> p s l", s=S) for c in range(4)}

            def stageC(cs):
                if not is_s:
                    return
                for c in cs:
                    st = lru_sets[c]
                    P.op("dve", lambda e, c=c: e.tensor_scalar(out=acc3[c], in0=ext3[c][:, :, 3:3 + Ls],
                                                               scalar1=vcol(V_CW, 3 * 4 + c), scalar2=vcol(V_CB, c),
                                                               op0=ALU.mult, op1=ALU.add),
                         reads=[extT[c], vecs], writes=[st["acc"]])
                for j in (2, 1, 0):
                    for c in cs:
                        st = lru_sets[c]
                        P.op("dve", lambda e, j=j, c=c: e.scalar_tensor_tensor(out=acc3[c], in0=ext3[c][:, :, j:j + Ls],
                                                                               scalar=vcol(V_CW, j * 4 + c), in1=acc3[c],
                                                                               op0=ALU.mult, op1=ALU.add),
                             reads=[extT[c], vecs], writes=[st["acc"]])
                if is_s:
                    for c in cs:
                        P.op("pool", lambda e, c=c: e.tensor_copy(out=olc_t[:, c, 1:17, :], in_=zxs_t[:, c, :, LS:LS + 3]),
                             reads=[zxs[c]], writes=[olc])

            def stageD(cs):
                banks_ax = {}
                for c in cs:
                    st = lru_sets[c]
                    P.op("dve", lambda e, st=st: e.tensor_copy(out=st["xcb"].ap[:, 0:L], in_=st["acc"].ap[:, 0:L]),
                         reads=[st["acc"]], writes=[st["xcb"]])
                    bka, bkx = newbank(), newbank()
                    P.op("pe", lambda e, st=st, c=c, bka=bka: e.matmul(bka.ap[:, 0:L], lhsT=wa_bd[:, c, :], rhs=st["xcb"].ap[:, 0:L],
                                                                       start=True, stop=True),
                         reads=[st["xcb"], wab_T], writes=[bka])
                    P.op("pe", lambda e, st=st, c=c, bkx=bkx: e.matmul(bkx.ap[:, 0:L], lhsT=wx_bd[:, c, :], rhs=st["xcb"].ap[:, 0:L],
                                                                       start=True, stop=True),
                         reads=[st["xcb"], wxb_T], writes=[bkx])
                    banks_ax[c] = (bka, bkx)
                for c in cs:
                    st = lru_sets[c]
                    ba_, bx_ = banks_ax[c]
                    P.op("act", act_fn(st["ti"].ap[:, 0:L], bx_.ap[:, 0:L], AF.Tanh, scale=0.5, bias=dcol(D_HBX, c)),
                         reads=[bx_, der], writes=[st["ti"]])
                for c in cs:
                    st = lru_sets[c]
                    ba_, bx_ = banks_ax[c]
                    P.op("act", act_fn(st["tr"].ap[:, 0:L], ba_.ap[:, 0:L], AF.Tanh, scale=0.5, bias=dcol(D_HBA, c)),
                         reads=[ba_, der], writes=[st["tr"]])

            def stageD2(cs):
                for c in cs:
                    st = lru_sets[c]
                    P.op("act", act_fn(st["a"].ap[:, 0:L], st["tr"].ap[:, 0:L], AF.Exp, scale=dcol(D_HC, c), bias=dcol(D_HC, c)),
                         reads=[st["tr"], der], writes=[st["a"]])
                    P.op("pool", lambda e, st=st: e.tensor_tensor(out=st["a2"].ap[:, 0:L], in0=st["a"].ap[:, 0:L],
                                                                  in1=st["a"].ap[:, 0:L], op=ALU.mult),
                         reads=[st["a"]], writes=[st["a2"]])

            def stageE(cs):
                for c in cs:
                    st = lru_sets[c]
                    P.op("act", act_fn(st["a2"].ap[:, 0:L], st["a2"].ap[:, 0:L], AF.Sqrt, scale=-1.0, bias=1.0),
                         reads=[], writes=[st["a2"]])

            def stageF1(cs):
                for c in cs:
                    st = lru_sets[c]
                    P.op("dve", lambda e, st=st: e.scalar_tensor_tensor(out=st["ti"].ap[:, 0:L], in0=st["ti"].ap[:, 0:L], scalar=1.0,
                                                                        in1=st["acc"].ap[:, 0:L], op0=ALU.add, op1=ALU.mult),
                         reads=[st["acc"]], writes=[st["ti"]])

            def stageF(cs):
                for c in cs:
                    st = lru_sets[c]
                    P.op("dve", lambda e, st=st: e.scalar_tensor_tensor(out=st["ti"].ap[:, 0:L], in0=st["ti"].ap[:, 0:L], scalar=0.5,
                                                                        in1=st["a2"].ap[:, 0:L], op0=ALU.mult, op1=ALU.mult),
                         reads=[st["a2"]], writes=[st["ti"]])
                for c in cs:
                    st = lru_sets[c]
                    hb = st["tr"]
                    if is_s:
                        a3s = st["a"].ap[:, 0:L].rearrange("p (s l) -> p s l", s=NS)
                        b3s = st["ti"].ap[:, 0:L].rearrange("p (s l) -> p s l", s=NS)
                        P.op("dve", lambda e, a3s=a3s, c=c: e.tensor_tensor(out=fixS.ap, in0=a3s[:, :, 0], in1=sth_t[:, c, :],
                                                                           op=ALU.mult),
                             reads=[st["a"], sth], writes=[fixS])
                        P.op("dve", lambda e, b3s=b3s: e.tensor_tensor(out=b3s[:, :, 0], in0=b3s[:, :, 0], in1=fixS.ap, op=ALU.add),
                             reads=[fixS], writes=[st["ti"]])
                        P.op("dve", lambda e, a3s=a3s: e.memset(a3s[:, :, 0:1], 0.0), writes=[st["a"]])
                        P.op("dve", lambda e, st=st, hb=hb: e.tensor_tensor_scan(
                            out=hb.ap[:, 0:L], data0=st["a"].ap[:, 0:L], data1=st["ti"].ap[:, 0:L], initial=0.0,
                            op0=ALU.mult, op1=ALU.add),
                            reads=[st["a"], st["ti"]], writes=[hb])
                        P.op("dve", lambda e, hb=hb, c=c: e.tensor_copy(
                            out=oh_t[:, c, 1:17], in_=hb.ap[:, 0:L].rearrange("p (s l) -> p s l", s=NS)[:, :, LS - 1]),
                            reads=[hb], writes=[oh])
                    else:
                        init = 0.0 if first_seq else oh_t[:, c, 0:1]
                        P.op("dve", lambda e, st=st, hb=hb, init=init: e.tensor_tensor_scan(
                            out=hb.ap[:, 0:L], data0=st["a"].ap[:, 0:L], data1=st["ti"].ap[:, 0:L], initial=init,
                            op0=ALU.mult, op1=ALU.add),
                            reads=[st["a"], st["ti"], oh], writes=[hb])
                        P.op("dve", lambda e, hb=hb, c=c: e.tensor_copy(out=oh_t[:, c, 0:1], in_=hb.ap[:, L - 1:L]),
                             reads=[hb], writes=[oh])
                    P.op("pool", lambda e, hb=hb, c=c: e.tensor_tensor(out=mixedT_a[:, c, col0:col0 + L], in0=hb.ap[:, 0:L],
                                                                       in1=gz[c].ap[:, 0:L], op=ALU.mult),
                         reads=[hb, gz[c]], writes=mT)

            def frontb():
                for c in range(4):
                    bk = w_in_mm(4 + c)
                    P.op("act", act_fn(gz[c].ap[:, 0:L], bk.ap[:, 0:L], AF.Gelu_apprx_tanh), reads=[bk], writes=[gz[c]])

            def conv():
                stageC([0, 1])
                stageC([2, 3])

            def poolG():
              for g, w in enumerate(POOL_W):
                  bk = newbank()
                  src = zps_t[:, g, :, :] if is_s else zpe[g].ap
                  srcT = zps[g] if is_s else zpe[g]
                  do_fix = first_seq and not is_s
                  if do_fix:
                      P.op("dve", lambda e, g=g: e.tensor_tensor_scan(out=fixS.ap, data0=ones16_t[:, :],
                                                                      data1=zpe[g].ap[:, 0, 15:31], initial=0.0,
                                                                      op0=ALU.mult, op1=ALU.add),
                           reads=[zpe[g], ones16], writes=[fixS])
                      P.op("dve", lambda e, g=g: e.tensor_tensor(out=fixSd.ap, in0=fixS.ap, in1=dvec_t[:, g, :], op=ALU.mult),
                           reads=[fixS, dvec], writes=[fixSd])

                  def pm(e, g=g, w=w, bk=bk, src=src, do_fix=do_fix):
                      ins = None
                      out3 = as3(bk.ap[:, 0:L])
                      for k in range(w):
                          ins = e.matmul(out3, lhsT=pws[:, g, :], rhs=src[:, :, 15 - k:15 - k + Ls],
                                         start=(k == 0), stop=False)
                      ins = e.matmul(out3, lhsT=pwn[:, g, :], rhs=src[:, :, 15:15 + Ls], start=False, stop=(not do_fix))
                      if do_fix:
                          ins = e.matmul(bk.ap[:, 0:16], lhsT=pwb[:, g, :], rhs=fixSd.ap, start=False, stop=True)
                      return ins
                  rd = [srcT, pws_T, pwn_T] + ([fixSd, pwb_T] if do_fix else [])
                  P.op("pe", pm, reads=rd, writes=[bk])
                  P.op("act", act_fn(mixedT_a[:, 4 + g, col0:col0 + L], bk.ap[:, 0:L], AF.Identity, scale=vcol(V_PS, g)),
                       reads=[bk, vecs], writes=mT)
            def S12():
                stageD([0, 1, 2, 3])
                stageF1([0, 1, 2, 3])

            def S34():
                stageD2([0, 1, 2, 3])
                stageE([0, 1, 2, 3])
                stageF([0, 1])
                stageF([2, 3])

            return {"fronta": fronta, "frontb": frontb, "conv": conv, "poolG": poolG, "S12": S12, "S34": S34}

        def wout_A(t):
            for half in range(2):
                bk = newbank()

                def mm(e, half=half, bk=bk):
                    ins = None
                    for kc in range(8):
                        ins = e.matmul(bk.ap[:, :], lhsT=mixedT_a[:, kc, t * 128:(t + 1) * 128],
                                       rhs=w_out_bf[:, kc, half * 512:(half + 1) * 512], start=(kc == 0), stop=(kc == 7))
                    return ins
                P.op("pe", mm, reads=[mixedTb[t], w_out_T], writes=[bk])
                xs_ = X[t].ap[:, half * 512:(half + 1) * 512]
                P.op("dve", lambda e, bk=bk, xs_=xs_: e.tensor_tensor(out=xs_, in0=bk.ap[:, :], in1=xs_, op=ALU.add),
                     reads=[bk, X[t]], writes=[X[t]])
            norm_A(X[t], grep, xn_m[t % 3])

        def wout_B(t):
            norm_B(xn_m[t % 3], h2T_t[:, :, 2 + t * 128: 2 + (t + 1) * 128], h2Tb[t])

        tail_pending = []

        def flush_tail():
            while tail_pending:
                tail_pending.pop(0)()

        def up_pair(gi, pr, stg, j, segs_p, has_s, nt_p, first_group, last_group, on_last_mm=None):
            cg, cv = pr, 24 + pr
            L = segs_p[0][1]
            N = L + 2
            ig, bg0, bg1 = newbank_pair()
            iv, bv0, bv1 = newbank_pair()
            bks = {0: (bg0, bg1), 1: (bv0, bv1)}
            ib = {0: ig, 1: iv}
            all_tiles = list(range(0, (2 * L) // 128))
            for si, (c0, L_) in enumerate(segs_p):
                tiles = list(range(c0 // 128, (c0 + L) // 128))
                rd = [h2Tb[t] for t in tiles] + list(upb_T[stg])
                rd.append(h2T_hist if c0 == 0 else h2Tb[c0 // 128 - 1])
                for gv in (0, 1):
                    bk = bks[gv][si]

                    def mm(e, gv=gv, bk=bk, c0=c0, N=N, L=L):
                        ins = None
                        for kc in range(8):
                            ins = e.matmul(bk.ap[:, 0:N], lhsT=upb[stg][:, gv, kc, j * 128:(j + 1) * 128],
                                           rhs=h2T_t[:, kc, c0: 2 + c0 + L], start=(kc == 0), stop=(kc == 7))
                        return ins
                    P.op("pe", mm, reads=rd, writes=[bk])
            if on_last_mm is not None and not has_s:
                on_last_mm()
            k_ = acc_rr[0] % NACC
            acc_rr[0] += 1
            acc = accA[k_]
            aT = accT[k_]
            gb = gbuf[k_]
            for gv, ch in ((0, cg), (1, cv)):
                P.op("act", act_fn(acc[:, gv, :, 0:L], ps_t[:, ib[gv]:ib[gv] + 2, 2:2 + L], AF.Identity,
                                   scale=vcol(V_FCW, 2 * 48 + ch), bias=vcol(V_FCB, ch)),
                     reads=list(bks[gv]) + [vecs], writes=[aT[gv]])
            for tap, sh in ((1, 1), (0, 2)):
                for gv, ch in ((0, cg), (1, cv)):
                    o_ = acc[:, gv, :, 0:L]
                    i_ = ps_t[:, ib[gv]:ib[gv] + 2, 2 - sh:2 - sh + L]
                    P.op("dve", lambda e, o_=o_, i_=i_, tap=tap, ch=ch: e.scalar_tensor_tensor(
                        out=o_, in0=i_, scalar=vcol(V_FCW, tap * 48 + ch), in1=o_, op0=ALU.mult, op1=ALU.add),
                        reads=list(bks[gv]) + [vecs], writes=[aT[gv]])
            if last_group:
                for gv, ch in ((0, cg), (1, cv)):
                    bk = bks[gv][1]
                    P.op("dve", lambda e, bk=bk, ch=ch, N=N: e.tensor_copy(out=offn_t[:, ch, 0, :], in_=bk.ap[:, N - 2:N]),
                         reads=[bk], writes=[offn])
            flush_tail()

            def tail(gb=gb, acc=acc, aT=aT, L=L, all_tiles=all_tiles):
                P.op("act", act_fn(gb.ap[:, :, 0:L], acc[:, 0, :, 0:L], AF.Gelu_apprx_tanh), reads=[aT[0]], writes=[gb])
                P.op("dve", lambda e: e.tensor_tensor(
                    out=actT_a[:, pr, 0:2 * L].rearrange("p (s l) -> p s l", s=2), in0=gb.ap[:, :, 0:L],
                    in1=acc[:, 1, :, 0:L], op=ALU.mult),
                    reads=[gb, aT[1]], writes=[actTb[t] for t in all_tiles])
            tail_pending.append(tail)
            if has_s:
                c0 = nt_p * 128
                L = NS * LS
                t = nt_p
                bg, bv = newbank(), newbank()
                ue = uexts[pr % 2]
                k_ = acc_rr[0] % NACC
                acc_rr[0] += 1
                acc = accA[k_][:, :, 0, :]
                aT = accT[k_]
                gb = T(gbuf[k_].ap[:, 0, :])
                gb_full = gbuf[k_]
                for gv, bk in ((0, bg), (1, bv)):
                    def mm(e, gv=gv, bk=bk, c0=c0, L=L):
                        ins = None
                        for kc in range(8):
                            ins = e.matmul(bk.ap[:, 0:L], lhsT=upb[stg][:, gv, kc, j * 128:(j + 1) * 128],
                                           rhs=h2T_t[:, kc, 2 + c0: 2 + c0 + L], start=(kc == 0), stop=(kc == 7))
                        return ins
                    P.op("pe", mm, reads=[h2Tb[t]] + list(upb_T[stg]), writes=[bk])
                if on_last_mm is not None:
                    on_last_mm()
                ueT = uext_halves[pr % 2]
                for gv, bk, ch in ((0, bg, cg), (1, bv, cv)):
                    P.op("act", act_fn(ue.ap[:, gv, :, 0:2], stffn_t[:, ch, :, :], AF.Copy), reads=[stffn], writes=[ueT[gv]])
                    b3 = bk.ap[:, 0:L].rearrange("p (s l) -> p s l", s=NS)
                    P.op("act", act_fn(ue.ap[:, gv, :, 2:2 + LS], b3, AF.Copy), reads=[bk], writes=[ueT[gv]])
                for gv, bk, ch in ((0, bg, cg), (1, bv, cv)):
                    a3 = acc[:, gv, 0:L].rearrange("p (s l) -> p s l", s=NS)
                    if gv == 0:
                        b3 = bk.ap[:, 0:L].rearrange("p (s l) -> p s l", s=NS)
                        P.op("act", act_fn(a3, b3, AF.Identity, scale=vcol(V_FCW, 2 * 48 + ch), bias=vcol(V_FCB, ch)),
                             reads=[bk, vecs], writes=[aT[gv]])
                    else:
                        P.op("dve", lambda e, gv=gv, ch=ch, a3=a3: e.tensor_scalar(
                            out=a3, in0=ue.ap[:, gv, :, 2:2 + LS], scalar1=vcol(V_FCW, 2 * 48 + ch), scalar2=vcol(V_FCB, ch),
                            op0=ALU.mult, op1=ALU.add), reads=[ueT[gv], vecs], writes=[aT[gv]])
                for tap, sh in ((1, 1), (0, 2)):
                    for gv, bk, ch in ((0, bg, cg), (1, bv, cv)):
                        a3 = acc[:, gv, 0:L].rearrange("p (s l) -> p s l", s=NS)
                        P.op("dve", lambda e, gv=gv, ch=ch, a3=a3, tap=tap, sh=sh: e.scalar_tensor_tensor(
                            out=a3, in0=ue.ap[:, gv, :, 2 - sh:2 - sh + LS], scalar=vcol(V_FCW, tap * 48 + ch), in1=a3,
                            op0=ALU.mult, op1=ALU.add), reads=[ueT[gv], vecs], writes=[aT[gv]])
                for gv, bk, ch in ((0, bg, cg), (1, bv, cv)):
                    P.op("act", act_fn(offn_t[:, ch, 1:17, :], ue.ap[:, gv, :, LS:LS + 2], AF.Copy), reads=[ueT[gv]], writes=[offn])
                flush_tail()

                def tail(gb=gb, gb_full=gb_full, acc=acc, aT=aT, c0=c0, L=L, t=t):
                    P.op("act", act_fn(gb.ap[:, 0:L], acc[:, 0, 0:L], AF.Gelu_apprx_tanh), reads=[aT[0]], writes=[gb_full])
                    P.op("dve", lambda e: e.tensor_tensor(
                        out=actT_a[:, pr, c0:c0 + L], in0=gb.ap[:, 0:L], in1=acc[:, 1, 0:L], op=ALU.mult),
                        reads=[gb_full, aT[1]], writes=[actTb[t]])
                tail_pending.append(tail)

        def down_final(gi, t, yrow0):
            for half in range(2):
                bk = newbank()

                def mm(e, half=half, bk=bk):
                    ins = None
                    for c in range(24):
                        ins = e.matmul(bk.ap[:, :], lhsT=actT_a[:, c, t * 128:(t + 1) * 128],
                                       rhs=wd_bf[:, c, half * 512:(half + 1) * 512], start=(c == 0), stop=(c == 23))
                    return ins
                P.op("pe", mm, reads=[actTb[t]] + wd_T, writes=[bk])
                xs_ = X[t].ap[:, half * 512:(half + 1) * 512]
                P.op("dve", lambda e, bk=bk, xs_=xs_: e.tensor_tensor(out=xs_, in0=bk.ap[:, :], in1=xs_, op=ALU.add),
                     reads=[bk, X[t]], writes=[X[t]])
            ssq, sq, rs = newstat(), newstat(), newstat()
            yb = ybuf[0]
            P.op("act", act_fn(yb.ap, X[t].ap, AF.Square, accum_out=ssq.ap), reads=[X[t]], writes=[yb, ssq])
            P.op("act", act_fn(sq.ap, ssq.ap, AF.Sqrt, scale=1.0 / D, bias=EPS), reads=[ssq], writes=[sq])
            P.op("dve", lambda e: e.reciprocal(out=rs.ap, in_=sq.ap), reads=[sq], writes=[rs])
            P.op("dve", lambda e: e.scalar_tensor_tensor(out=yb.ap, in0=X[t].ap, scalar=rs.ap, in1=grep.ap,
                                                         op0=ALU.mult, op1=ALU.mult),
                 reads=[X[t], rs, grep], writes=[yb])
            P.dma("sp", "d_y", y_d[yrow0:yrow0 + 128, :], yb.ap, reads=[yb])

        deferred_norm1 = []
        load_x(0)
        for gi, (p0, npt, has_s) in enumerate(GROUPS):
            nt_p = npt // 128
            nt = nt_p + (1 if has_s else 0)
            first_group = gi == 0
            last_group = gi == len(GROUPS) - 1
            if gi > 0:
                P.barrier()
            else:
                def p1_tiles(t0, t1):
                    for t in range(t0, t1 + 1):
                        if t < t1:
                            norm_A(X[t], grep, xn_m[t % 2])
                        if t >= t0 + 1:
                            norm_B(xn_m[(t - 1) % 2], h1T_a[:, :, HX + (t - 1) * 128: HX + t * 128], h1Tb[t - 1])
                p1_tiles(0, 3)
                deferred_norm1.append(lambda nt=nt: p1_tiles(3, nt))
            P.dma("pool", "d_w", w_out_bf[:, :, :], w_out_v[:, :, :], writes=[w_out_T])
            if gi > 0:
                P.op("pool", lambda e: e.tensor_copy(out=h1T_a[:, :, 0:HX], in_=h1hist_t[:, :, 0:HX]),
                     reads=[h1hist], writes=[h1T_hist])
            if gi == 0:
                scale_pool_weights()
            segs = []
            c0 = 0
            while c0 < npt:
                L = min(384 if npt % 384 == 0 else 256, npt - c0)
                segs.append((c0, L))
                c0 += L
            sg = []
            for si, (c0, L) in enumerate(segs):
                sg.append(mixer_segment(gi, c0, L, 1, L, list(range(c0 // 128, (c0 + L) // 128)), "p",
                                        first_seq=(first_group and si == 0),
                                        last_prompt=(last_group and si == len(segs) - 1)))
            if has_s:
                sg.append(mixer_segment(gi, nt_p * 128, NS * LS, NS, LS, [nt_p], "s", first_seq=False, last_prompt=False))
            sg[0]["fronta"]()
            sg[0]["frontb"]()
            sg[0]["conv"]()
            sg[0]["poolG"]()
            while deferred_norm1:
                deferred_norm1.pop(0)()
            for k in range(len(sg)):
                sg[k]["S12"]()
                if k + 1 < len(sg):
                    sg[k + 1]["fronta"]()
                    sg[k + 1]["conv"]()
                sg[k]["S34"]()
                if k + 1 < len(sg):
                    sg[k + 1]["frontb"]()
                    sg[k + 1]["poolG"]()
            if not last_group:
                P.op("pool", lambda e, npt=npt: e.tensor_copy(out=h1hist_t[:, :, 0:HX], in_=h1T_a[:, :, npt:npt + HX]),
                     reads=[h1Tb[nt_p - 1]], writes=[h1hist])
            half_p = npt // 2
            segs_p = [(0, half_p), (half_p, half_p)]
            nstages = 12

            def load_stage(s):
                stg = s % NST
                pr0 = s * 2
                P.dma("pool", "d_up", upb[stg][:, 0, :, :], up_v[:, :, pr0 * 128: pr0 * 128 + 256], writes=[upb_T[stg][0]])
                P.dma("pool", "d_up", upb[stg][:, 1, :, :], up_v[:, :, DFF + pr0 * 128: DFF + pr0 * 128 + 256],
                      writes=[upb_T[stg][1]])
            P._wait(P.q["pool"], [(q_.sem_key, q_.cnt) for q_ in P.q.values() if q_.cnt > 0 and q_.name != "pool"])
            for s0_ in range(NST):
                load_stage(s0_)
            P.dma("sp", "d_misc", grep.ap, g2r_d[:, :], writes=[grep])
            for t in range(nt + 2):
                if t < nt:
                    wout_A(t)
                if t >= 2:
                    wout_B(t - 2)
            P.barrier()
            P.dma("sp", "d_misc", grep.ap, gfr_d[:, :], writes=[grep])
            for s in range(nstages):
                def prefetch(s=s):
                    if s + NST < nstages:
                        load_stage(s + NST)
                    P.dma("pool", "d_wd", wd_bf[:, s * 2:(s + 1) * 2, :], down_v[:, s * 2:(s + 1) * 2, :], writes=[wd_T[s]])
                up_pair(gi, s * 2, s % NST, 0, segs_p, has_s, nt_p, first_group, last_group)
                up_pair(gi, s * 2 + 1, s % NST, 1, segs_p, has_s, nt_p, first_group, last_group, on_last_mm=prefetch)
            flush_tail()
            if not last_group:
                P.op("dve", lambda e, npt=npt: e.tensor_copy(out=h2T_t[:, :, 0:2], in_=h2T_t[:, :, npt:npt + 2]),
                     reads=[h2Tb[nt_p - 1]], writes=[h2T_hist])
            evs_p4 = [(q_.sem_key, q_.cnt) for q_ in P.q.values() if q_.cnt > 0 and q_.name != "sp"]
            if not last_group:
                P.dma("pool", "d_w", w_in_bf[:, :, 0:768], w_in_v[:, :, 0:768], writes=[w_in_T[0]], extra=evs_p4)
                P.dma("pool", "d_w", w_in_bf[:, :, 768:1536], w_in_v[:, :, 768:1536], writes=[w_in_T[1]], extra=evs_p4)
                np0, nnpt, nhs = GROUPS[gi + 1]
                nnt_p = nnpt // 128
                nnt = nnt_p + (1 if nhs else 0)
            else:
                nnt = 0
            doneA = doneB = 0

            def load_next(tt):
                if tt < nnt_p:
                    P.dma("sp", "d_x", X[tt].ap, xp[np0 + tt * 128: np0 + (tt + 1) * 128, :], writes=[X[tt]])
                else:
                    P.dma("sp", "d_x", X[tt].ap, xs[:, :], writes=[X[tt]])

            for t in range(nt):
                yrow0 = (p0 + t * 128) if t < nt_p else SEQ
                down_final(gi, t, yrow0)
                if t < nnt:
                    load_next(t)
                if doneB < doneA and doneB <= t - 2:
                    norm1_B(doneB, extra=evs_p4)
                    doneB += 1
                if doneA < nnt and doneA <= t - 1:
                    norm1_A(doneA, extra=evs_p4)
                    doneA += 1
            for tt in range(nt, nnt):
                load_next(tt)
            def norm1_tail(doneA=doneA, doneB=doneB, nnt=nnt, evs_p4=evs_p4):
                while doneB < nnt:
                    if doneA == doneB:
                        norm1_A(doneA, extra=evs_p4)
                        doneA += 1
                    norm1_B(doneB, extra=evs_p4)
                    doneB += 1
            deferred_norm1.append(norm1_tail)

        P.dma("sp", "d_y", olc_d[:, :], olc_t[:, :, :, :].rearrange("p a b c -> p (a b c)"), reads=[olc])
        P.dma("sp", "d_y", oh_d[:, :], oh_t[:, :, :].rearrange("p a b -> p (a b)"), reads=[oh])
        P.dma("sp", "d_y", opool_d[:, :], opool_t[:, :, :, :].rearrange("p a b c -> p (a b c)"), reads=[opool])
        P.dma("sp", "d_y", offn_d[:, :], offn_t[:, :, :, :].rearrange("p a b c -> p (a b c)"), reads=[offn])
        P.final_wait("sp")

        with nc.Block() as block:
            def play(eng, q):
                own = sems[q.sem_key]
                for item in q.ops:
                    if item[0] == "wait":
                        eng.wait_ge(sems[item[1]], item[2])
                    elif item[0] == "op":
                        item[1](eng).then_inc(own, 1)
                    else:
                        eng.dma_start(out=item[1], in_=item[2]).then_inc(sems[item[3]], 16)

            @block.sync
            def _(e):
                play(e, P.q["sp"])

            @block.scalar
            def _(e):
                play(e, P.q["act"])

            @block.vector
            def _(e):
                play(e, P.q["dve"])

            @block.gpsimd
            def _(e):
                play(e, P.q["pool"])

            @block.tensor
            def _(e):
                play(e, P.q["pe"])
    build_program.last_prog = P
    return nc


_NC_CACHE = {}


def _layout_vec(v, nchunk):
    return np.ascontiguousarray(np.asarray(v, np.float32).reshape(nchunk, 128).T)


def kernel(x_prompt, x_sample, state_lru_conv, state_lru_h, state_pool, state_ffn_conv,
           norm1_g, w_in, lru_conv_w, lru_conv_b, lru_wa, lru_ba, lru_wx, lru_bx, lru_lambda,
           pool_w, pool_scale, w_out, norm2_g, ffn_up, ffn_conv_w, ffn_conv_b, ffn_down, final_g):
    f = lambda a: np.ascontiguousarray(np.asarray(a, dtype=np.float32))
    x_prompt, x_sample = f(x_prompt), f(x_sample)
    vecs = np.zeros((128, NV), np.float32)
    cw = f(lru_conv_w)[0]
    for j in range(4):
        vecs[:, V_CW + j * 4: V_CW + (j + 1) * 4] = _layout_vec(cw[j], 4)
    vecs[:, V_CB:V_CB + 4] = _layout_vec(f(lru_conv_b)[0], 4)
    vecs[:, V_BA:V_BA + 4] = _layout_vec(f(lru_ba)[0], 4)
    vecs[:, V_BX:V_BX + 4] = _layout_vec(f(lru_bx)[0], 4)
    vecs[:, V_LAM:V_LAM + 4] = _layout_vec(f(lru_lambda)[0], 4)
    vecs[:, V_PS:V_PS + 4] = _layout_vec(f(pool_scale)[0], 4)
    fcw = f(ffn_conv_w)[0]
    for j in range(3):
        vecs[:, V_FCW + j * 48: V_FCW + (j + 1) * 48] = _layout_vec(fcw[j], 48)
    vecs[:, V_FCB:V_FCB + 48] = _layout_vec(f(ffn_conv_b)[0], 48)
    vecs[:, V_G1:V_G1 + 8] = _layout_vec(f(norm1_g)[0], 8)
    rep = lambda g: np.ascontiguousarray(np.broadcast_to(f(g).reshape(1, D), (128, D)))
    common = {
        "vecs": vecs, "g1r": rep(norm1_g[0]), "g2r": rep(norm2_g[0]), "gfr": rep(final_g),
        "ident": np.eye(128, dtype=np.float32),
        "w_in": f(w_in)[0], "w_out": f(w_out)[0], "ffn_up": f(ffn_up)[0], "ffn_down": f(ffn_down)[0],
        "wa": f(lru_wa)[0], "wx": f(lru_wx)[0], "pw": f(pool_w)[0],
    }
    slc, slh, spl, sff = f(state_lru_conv)[0], f(state_lru_h)[0], f(state_pool)[0], f(state_ffn_conv)[0]
    in_maps = []
    for c in range(NCORES):
        s0, s1 = c * NS, (c + 1) * NS
        m = dict(common)
        m["xp"] = x_prompt[c]
        m["xs"] = x_sample[s0:s1].reshape(NS * LS, D)
        m["stlc"] = np.ascontiguousarray(slc[s0:s1].reshape(NS, 3, 4, 128).transpose(3, 2, 0, 1)).reshape(128, -1)
        m["sth"] = np.ascontiguousarray(slh[s0:s1].reshape(NS, 4, 128).transpose(2, 1, 0)).reshape(128, -1)
        m["stpool"] = np.ascontiguousarray(spl[s0:s1].reshape(NS, 15, 4, 128).transpose(3, 2, 0, 1)).reshape(128, -1)
        m["stffn"] = np.ascontiguousarray(sff[s0:s1].reshape(NS, 2, 48, 128).transpose(3, 2, 0, 1)).reshape(128, -1)
        in_maps.append(m)

    if "nc" not in _NC_CACHE:
        _NC_CACHE["nc"] = build_program()
    nc = _NC_CACHE["nc"]
    res = run_bass_kernel_spmd(nc, in_maps, core_ids=list(range(NCORES)))
    R = res.results

    y_prompt = np.empty((8, SEQ, D), np.float32)
    y_sample = np.empty((128, LS, D), np.float32)
    p_lc = np.empty((1, 8, 3, 512), np.float32)
    p_h = np.empty((1, 8, 512), np.float32)
    p_pool = np.empty((1, 8, 15, 512), np.float32)
    p_ffn = np.empty((1, 8, 2, 6144), np.float32)
    s_lc = np.empty((1, 128, 3, 512), np.float32)
    s_h = np.empty((1, 128, 512), np.float32)
    s_pool = np.empty((1, 128, 15, 512), np.float32)
    s_ffn = np.empty((1, 128, 2, 6144), np.float32)
    for c in range(NCORES):
        s0, s1 = c * NS, (c + 1) * NS
        y = np.asarray(R[c]["y"], np.float32)
        y_prompt[c] = y[:SEQ]
        y_sample[s0:s1] = y[SEQ:].reshape(NS, LS, D)
        olc = np.asarray(R[c]["o_lc"], np.float32).reshape(128, 4, 17, 3).transpose(2, 3, 1, 0).reshape(17, 3, 512)
        ohh = np.asarray(R[c]["o_h"], np.float32).reshape(128, 4, 17).transpose(2, 1, 0).reshape(17, 512)
        opl = np.asarray(R[c]["o_pool"], np.float32).reshape(128, 4, 17, 15).transpose(2, 3, 1, 0).reshape(17, 15, 512)
        off = np.asarray(R[c]["o_ffn"], np.float32).reshape(128, 48, 17, 2).transpose(2, 3, 1, 0).reshape(17, 2, 6144)
        p_lc[0, c], s_lc[0, s0:s1] = olc[0], olc[1:]
        p_h[0, c], s_h[0, s0:s1] = ohh[0], ohh[1:]
        p_pool[0, c], s_pool[0, s0:s1] = opl[0], opl[1:]
        p_ffn[0, c], s_ffn[0, s0:s1] = off[0], off[1:]
    return (y_prompt, y_sample, p_lc, p_h, p_pool, p_ffn, s_lc, s_h, s_pool, s_ffn)
```

```python
from contextlib import ExitStack

import numpy as np
import concourse.bass as bass
import concourse.mybir as mybir
from concourse.bass_utils import run_bass_kernel_spmd

F32 = mybir.dt.float32
BF16 = mybir.dt.bfloat16
AF = mybir.ActivationFunctionType
ALU = mybir.AluOpType

NCORES = 8
D = 1024
SEQ = 2048
NS = 16
LS = 8
DFF = 3072
EPS = 1e-6
POOL_W = (2, 4, 8, 16)

V_CW, V_CB, V_BA, V_BX, V_LAM, V_PS, V_FCW, V_FCB, V_G1, NV = 0, 16, 20, 24, 28, 32, 36, 180, 228, 236

GROUPS = [(0, 768, False), (768, 768, False), (1536, 512, True)]
GMAX = 768
ARENA_WORDS = 33920


class T:
    def __init__(self, ap, excl=False):
        self.ap = ap
        self.lastw = []
        self.readers = []
        self.excl = excl


class Q:
    def __init__(self, name, sem_key):
        self.name = name
        self.sem_key = sem_key
        self.cnt = 0
        self.waited = {}
        self.ops = []


class Prog:
    def __init__(self):
        self.q = {n: Q(n, "s_" + n) for n in ("pe", "act", "dve", "pool", "sp")}
        self.dma_cnt = {}
        self.ring = {}

    def _wait(self, q, deps):
        for (k, v) in deps:
            if q.name == "pe" and k == q.sem_key:
                continue
            if q.waited.get(k, 0) < v:
                q.ops.append(("wait", k, v))
                q.waited[k] = v

    @staticmethod
    def _deps(reads, writes, extra):
        deps = set(extra)
        for b in reads:
            deps.update(b.lastw)
            if b.excl:
                deps.update(b.readers)
        for b in writes:
            deps.update(b.lastw)
            deps.update(b.readers)
        return deps

    @staticmethod
    def _update(ev, reads, writes):
        for b in reads:
            b.readers.append(ev)
        for b in writes:
            b.lastw = [ev]
            b.readers = []

    def op(self, eng, fn, reads=(), writes=(), extra=()):
        q = self.q[eng]
        self._wait(q, self._deps(reads, writes, extra))
        q.cnt += 1
        ev = (q.sem_key, q.cnt)
        q.ops.append(("op", fn))
        self._update(ev, reads, writes)
        return ev

    NRING = 12

    def dma(self, eng, dsem, out, in_, reads=(), writes=(), extra=()):
        q = self.q[eng]
        i = self.ring.get(eng, 0)
        self.ring[eng] = i + 1
        dsem = "r_%s_%d" % (eng, i % self.NRING)
        prev = self.dma_cnt.get(dsem, 0)
        if prev:
            self._wait(q, [(dsem, prev)])
        self._wait(q, self._deps(reads, writes, extra))
        self.dma_cnt[dsem] = self.dma_cnt.get(dsem, 0) + 16
        ev = (dsem, self.dma_cnt[dsem])
        q.ops.append(("dma", out, in_, dsem))
        self._update(ev, reads, writes)
        return ev

    def barrier(self):
        evs = [(q.sem_key, q.cnt) for q in self.q.values() if q.cnt > 0]
        for q in self.q.values():
            self._wait(q, evs)

    def final_wait(self, eng):
        evs = [(q.sem_key, q.cnt) for q in self.q.values() if q.cnt > 0]
        evs += [(k, v) for k, v in self.dma_cnt.items()]
        self._wait(self.q[eng], evs)


def build_program():
    nc = bass.Bass("TRN2", target_bir_lowering=False)
    P = Prog()

    def din(name, shape):
        return nc.dram_tensor(name, list(shape), F32, kind="ExternalInput").ap()

    def dout(name, shape):
        return nc.dram_tensor(name, list(shape), F32, kind="ExternalOutput").ap()

    xp = din("xp", [SEQ, D])
    xs = din("xs", [NS * LS, D])
    vecs_d = din("vecs", [128, NV])
    g1r_d = din("g1r", [128, D])
    g2r_d = din("g2r", [128, D])
    gfr_d = din("gfr", [128, D])
    ident_d = din("ident", [128, 128])
    w_in_d = din("w_in", [D, 1536])
    w_out_d = din("w_out", [D, D])
    up_d = din("ffn_up", [D, 2 * DFF])
    down_d = din("ffn_down", [DFF, D])
    wa_d = din("wa", [8, 64, 64])
    wx_d = din("wx", [8, 64, 64])
    pw_d = din("pw", [4, 128, 128])
    stlc_d = din("stlc", [128, 4 * NS * 3])
    sth_d = din("sth", [128, 4 * NS])
    stpool_d = din("stpool", [128, 4 * NS * 15])
    stffn_d = din("stffn", [128, 48 * NS * 2])

    y_d = dout("y", [SEQ + NS * LS, D])
    olc_d = dout("o_lc", [128, 4 * 17 * 3])
    oh_d = dout("o_h", [128, 4 * 17])
    opool_d = dout("o_pool", [128, 4 * 17 * 15])
    offn_d = dout("o_ffn", [128, 48 * 17 * 2])

    w_in_v = w_in_d.rearrange("(kc p) n -> p kc n", p=128)
    w_out_v = w_out_d.rearrange("(kc p) n -> p kc n", p=128)
    up_v = up_d.rearrange("(kc p) n -> p kc n", p=128)
    down_v = down_d.rearrange("(c p) n -> p c n", p=128)

    es = ExitStack()
    with es:
        def sb(name, shape, dt=F32):
            return es.enter_context(nc.sbuf_tensor("sb_" + name, list(shape), dt))

        sems = {}
        for k in ("s_pe", "s_act", "s_dve", "s_pool", "s_sp"):
            sems[k] = es.enter_context(nc.semaphore(k))
        for eng_ in ("sp", "pool"):
            for i_ in range(Prog.NRING):
                k = "r_%s_%d" % (eng_, i_)
                sems[k] = es.enter_context(nc.semaphore(k))

        X_t = sb("X", [128, 6, D])
        X = [T(X_t[:, i, :]) for i in range(6)]
        h2T_t = sb("h2T", [128, 8, 2 + GMAX], BF16)
        h2T_hist = T(h2T_t[:, :, 0:2])
        h2Tb = [T(h2T_t[:, :, 2 + i * 128: 2 + (i + 1) * 128]) for i in range(6)]
        grep = T(sb("grep", [128, D])[:, :])
        vecs_t = sb("vecs", [128, NV])
        vecs = T(vecs_t[:, :])
        der_t = sb("der", [128, 16])
        der = T(der_t[:, :])
        ident_t = sb("ident", [128, 128], BF16)
        ident = T(ident_t[:, :])
        identf_t = sb("identf", [128, 128], F32)
        identf = T(identf_t[:, :])
        olc_t = sb("olc", [128, 4, 17, 3])
        oh_t = sb("oh", [128, 4, 17])
        opool_t = sb("opool", [128, 4, 17, 15])
        offn_t = sb("offn", [128, 48, 17, 2])
        olc, oh, opool, offn = T(olc_t), T(oh_t), T(opool_t), T(offn_t)
        zxs_t = sb("zxs", [128, 4, NS, 3 + LS])
        zxs = [T(zxs_t[:, c, :, :]) for c in range(4)]
        zps_t = sb("zps", [128, 4, NS, 15 + LS], BF16)
        zps = [T(zps_t[:, g, :, :]) for g in range(4)]
        sth_t = sb("sth", [128, 4, NS])
        stffn_t = sb("stffn", [128, 48, NS, 2])
        sth, stffn = T(sth_t), T(stffn_t)
        stats_t = sb("stats", [128, 192])
        h1hist_t = sb("h1hist", [128, 8, 4], BF16)
        h1hist = T(h1hist_t)
        histx_t = sb("histx", [128, 4, 3])
        histx = [T(histx_t[:, c, :]) for c in range(4)]
        histp_t = sb("histp", [128, 4, 15], BF16)
        histp = [T(histp_t[:, g, :]) for g in range(4)]
        dvec_t = sb("dvec", [128, 4, 16])
        dvec = T(dvec_t)
        ones16_t = sb("ones16", [128, 16])
        ones16 = T(ones16_t)
        wa_bd = sb("wa_bd", [128, 4, 128], BF16)
        wx_bd = sb("wx_bd", [128, 4, 128], BF16)
        wab_T = T(wa_bd)
        wxb_T = T(wx_bd)
        pwb = sb("pwb", [128, 4, 128], BF16)
        pws = sb("pws", [128, 4, 128], BF16)
        pwn = sb("pwn", [128, 4, 128], BF16)
        pwb_T, pws_T, pwn_T = T(pwb), T(pws), T(pwn)
        arena_t = sb("arena", [128, ARENA_WORDS])
        stpool_t = arena_t[:, 31760:31760 + 4 * NS * 15].rearrange("p (a b c) -> p a b c", a=4, b=NS)
        stlc_t = arena_t[:, 20548:20548 + 4 * NS * 3].rearrange("p (a b c) -> p a b c", a=4, b=NS)
        stlc, stpool = T(stlc_t), T(stpool_t)

        ps_t = es.enter_context(nc.psum_tensor("ps", [128, 8, 512], F32))
        banks = [T(ps_t[:, i, :], excl=True) for i in range(8)]
        bank_rr = [0]

        def newbank():
            b = banks[bank_rr[0] % 8]
            bank_rr[0] += 1
            return b

        def newbank_pair():
            if bank_rr[0] % 2:
                bank_rr[0] += 1
            i = bank_rr[0] % 8
            bank_rr[0] += 2
            return i, banks[i], banks[i + 1]

        stat_i = [0]

        def newstat():
            i = stat_i[0]
            stat_i[0] += 1
            return T(stats_t[:, i:i + 1])

        class Carver:
            def __init__(self):
                self.off = 0

            def f32(self, shape):
                n = int(np.prod(shape[1:]))
                a = arena_t[:, self.off:self.off + n]
                self.off += n
                assert self.off <= ARENA_WORDS, self.off
                return self._shape(a, shape)

            def bf16(self, shape):
                n = int(np.prod(shape[1:]))
                nw = (n + 1) // 2
                a = arena_t[:, self.off:self.off + nw].bitcast(BF16)[:, 0:n]
                self.off += nw
                assert self.off <= ARENA_WORDS, self.off
                return self._shape(a, shape)

            @staticmethod
            def _shape(a, shape):
                if len(shape) == 2:
                    return a
                if len(shape) == 3:
                    return a.rearrange("p (a b) -> p a b", a=shape[1])
                if len(shape) == 4:
                    return a.rearrange("p (a b c) -> p a b c", a=shape[1], b=shape[2])
                if len(shape) == 5:
                    return a.rearrange("p (a b c d) -> p a b c d", a=shape[1], b=shape[2], c=shape[3])
                raise ValueError(shape)

        DZ0, DZ1 = 21504, 31760
        cm = Carver()
        cm.off = DZ0
        w_in_bf = cm.bf16([128, 8, 1536])
        w_in_T = [T(w_in_bf[:, :, 0:768]), T(w_in_bf[:, :, 768:1536])]
        HX = 3
        h1T_a = cm.bf16([128, 8, HX + GMAX + 1])
        h1T_hist = T(h1T_a[:, :, 0:HX])
        h1Tb = [T(h1T_a[:, :, HX + i * 128: HX + (i + 1) * 128]) for i in range(6)]
        xn1 = [T(cm.f32([128, D])) for _ in range(1)]
        assert cm.off <= DZ1, cm.off
        cm.off = 0
        w_out_bf = cm.bf16([128, 8, D])
        w_out_T = T(w_out_bf)
        mixedT_a = cm.bf16([128, 8, GMAX])
        mixedTb = [T(mixedT_a[:, :, i * 128:(i + 1) * 128]) for i in range(6)]
        LSEG = 384
        lru_sets = []
        for s_ in range(4):
            d = {}
            d["ext"] = T(cm.f32([128, 1, 3 + LSEG]))
            d["acc"] = T(cm.f32([128, LSEG]))
            d["xcb"] = T(cm.bf16([128, LSEG]))
            d["tr"] = T(cm.f32([128, LSEG]))
            d["ti"] = T(cm.f32([128, LSEG]))
            d["a"] = T(cm.f32([128, LSEG]))
            d["a2"] = T(cm.f32([128, LSEG]))
            lru_sets.append(d)
        gz_one = [T(cm.f32([128, LSEG])) for _ in range(4)]
        gz2 = [gz_one, gz_one]
        zpe = [T(cm.bf16([128, 1, 15 + LSEG])) for _ in range(4)]
        xn_m = [T(cm.bf16([128, D])) for _ in range(3)]
        fixS = T(cm.f32([128, 16]))
        fixSd = T(cm.bf16([128, 16]))
        assert cm.off <= DZ0, cm.off
        mixer_words = DZ1

        cf = Carver()
        actT_a = cf.bf16([128, 24, GMAX])
        actTb = [T(actT_a[:, :, i * 128:(i + 1) * 128]) for i in range(6)]
        wd_bf = cf.bf16([128, 24, D])
        wd_T = [T(wd_bf[:, i * 2:(i + 1) * 2, :]) for i in range(12)]
        NST = 3
        upb = [cf.bf16([128, 2, 8, 256]) for _ in range(NST)]
        upb_T = [(T(u[:, 0, :, :]), T(u[:, 1, :, :])) for u in upb]
        NACC = 2
        accA = [cf.f32([128, 2, 2, 384]) for _ in range(NACC)]
        accT = [(T(a[:, 0, :, :]), T(a[:, 1, :, :])) for a in accA]
        gbuf = [T(cf.f32([128, 2, 384])) for _ in range(NACC)]
        acc_rr = [0]
        uexts = [T(cf.f32([128, 2, NS, 2 + LS])) for _ in range(2)]
        uext_halves = [(T(u.ap[:, 0, :, :]), T(u.ap[:, 1, :, :])) for u in uexts]
        ybuf = [T(cf.f32([128, D])) for _ in range(1)]
        ffn_words = cf.off
        assert max(mixer_words, ffn_words) <= ARENA_WORDS

        def act_fn(out, in_, func, **kw):
            return lambda e: e.activation(out=out, in_=in_, func=func, **kw)

        def vcol(off, c):
            return vecs_t[:, off + c: off + c + 1]

        def dcol(off, c):
            return der_t[:, off + c: off + c + 1]

        D_HBA, D_HBX, D_HC, D_C2 = 0, 4, 8, 12

        P.dma("sp", "d_misc", vecs_t[:, :], vecs_d[:, :], writes=[vecs])
        P.dma("sp", "d_misc", grep.ap, g1r_d[:, :], writes=[grep])
        P.dma("sp", "d_x", X[0].ap, xp[0:128, :], writes=[X[0]])
        P.dma("pool", "d_w", ident_t[:, :], ident_d[:, :], writes=[ident])
        P.dma("pool", "d_w", pwb[:, :, :], pw_d.rearrange("g i j -> i g j"), writes=[pwb_T])
        P.dma("pool", "d_w", w_in_bf[:, :, 0:768], w_in_v[:, :, 0:768], writes=[w_in_T[0]])
        P.dma("pool", "d_w", w_in_bf[:, :, 768:1536], w_in_v[:, :, 768:1536], writes=[w_in_T[1]])
        P.dma("sp", "d_misc", identf_t[:, :], ident_d[:, :], writes=[identf])
        P.dma("sp", "d_misc", stlc_t[:, :, :, :].rearrange("p a b c -> p (a b c)"), stlc_d[:, :], writes=[stlc])
        P.dma("sp", "d_misc", sth_t[:, :, :].rearrange("p a b -> p (a b)"), sth_d[:, :], writes=[sth])
        P.dma("sp", "d_misc", stpool_t[:, :, :, :].rearrange("p a b c -> p (a b c)"), stpool_d[:, :], writes=[stpool])
        P.dma("sp", "d_misc", stffn_t[:, :, :, :].rearrange("p a b c -> p (a b c)"), stffn_d[:, :], writes=[stffn])

        P.op("dve", lambda e: e.tensor_scalar(out=der_t[:, 0:8], in0=vecs_t[:, V_BA:V_BA + 8], scalar1=0.5,
                                              scalar2=None, op0=ALU.mult), reads=[vecs], writes=[der])
        tmp_e4 = T(stats_t[:, 176:180])
        tmp_s4 = T(stats_t[:, 180:184])
        P.op("act", act_fn(stats_t[:, 176:180], vecs_t[:, V_LAM:V_LAM + 4], AF.Exp, scale=-1.0),
             reads=[vecs], writes=[tmp_e4])
        P.op("act", act_fn(stats_t[:, 180:184], stats_t[:, 176:180], AF.Ln, bias=1.0),
             reads=[tmp_e4], writes=[tmp_s4])
        P.op("dve", lambda e: e.tensor_scalar(out=der_t[:, 8:12], in0=stats_t[:, 180:184], scalar1=-4.0,
                                              scalar2=None, op0=ALU.mult), reads=[tmp_s4], writes=[der])
        P.op("dve", lambda e: e.tensor_scalar(out=der_t[:, 12:16], in0=stats_t[:, 180:184], scalar1=-8.0,
                                              scalar2=None, op0=ALU.mult), reads=[tmp_s4], writes=[der])
        P.op("dve", lambda e: e.memset(histx_t[:, :, :], 0.0), writes=histx)
        P.op("dve", lambda e: e.memset(histp_t[:, :, :], 0.0), writes=histp)
        P.op("dve", lambda e: e.memset(ones16_t[:, :], 1.0), writes=[ones16])
        P.op("dve", lambda e: e.memset(dvec_t[:, :, :], 0.0), writes=[dvec])
        for g, w in enumerate(POOL_W):
            for t in range(w - 1):
                val = 1.0 / (t + 1) - 1.0 / w
                P.op("dve", lambda e, g=g, t=t, val=val: e.memset(dvec_t[:, g, t:t + 1], val), writes=[dvec])
        P.op("dve", lambda e: e.memset(h2T_t[:, :, 0:2], 0.0), writes=[h2T_hist])
        for c in range(4):
            P.op("dve", lambda e, c=c: e.tensor_copy(out=zxs_t[:, c, :, 0:3], in_=stlc_t[:, c, :, :]),
                 reads=[stlc], writes=[zxs[c]])
            P.op("dve", lambda e, c=c: e.tensor_copy(out=zps_t[:, c, :, 0:15], in_=stpool_t[:, c, :, :]),
                 reads=[stpool], writes=[zps[c]])
            P.op("dve", lambda e, c=c: e.tensor_copy(out=opool_t[:, c, 1:17, 0:7], in_=stpool_t[:, c, :, 8:15]),
                 reads=[stpool], writes=[opool])

        P.op("dve", lambda e: e.memset(wa_bd[:, :, :], 0.0), writes=[wab_T])
        P.op("dve", lambda e: e.memset(wx_bd[:, :, :], 0.0), writes=[wxb_T])
        for h in range(8):
            r0 = (h % 2) * 64
            P.dma("pool", "d_w", wa_bd[r0:r0 + 64, h // 2, r0:r0 + 64], wa_d[h, :, :], writes=[wab_T])
            P.dma("pool", "d_w", wx_bd[r0:r0 + 64, h // 2, r0:r0 + 64], wx_d[h, :, :], writes=[wxb_T])
        def scale_pool_weights():
            for g, w in enumerate(POOL_W):
                P.op("dve", lambda e, g=g, w=w: e.tensor_scalar(out=pws[:, g, :], in0=pwb[:, g, :], scalar1=1.0 / w,
                                                                scalar2=None, op0=ALU.mult), reads=[pwb_T], writes=[pws_T])
                P.op("dve", lambda e, g=g: e.tensor_scalar(out=pwn[:, g, :], in0=pwb[:, g, :], scalar1=-1.0,
                                                           scalar2=None, op0=ALU.mult), reads=[pwb_T], writes=[pwn_T])

        def load_x(gi):
            p0, npt, has_s = GROUPS[gi]
            nt = npt // 128
            for t in range(1 if gi == 0 else 0, nt):
                P.dma("sp", "d_x", X[t].ap, xp[p0 + t * 128: p0 + (t + 1) * 128, :], writes=[X[t]])
            if has_s:
                P.dma("sp", "d_x", X[nt].ap, xs[:, :], writes=[X[nt]])

        def norm_A(xt, grep_T, xn):
            ssq, sq, rs = newstat(), newstat(), newstat()
            P.op("act", act_fn(xn.ap, xt.ap, AF.Square, accum_out=ssq.ap), reads=[xt], writes=[xn, ssq])
            P.op("act", act_fn(sq.ap, ssq.ap, AF.Sqrt, scale=1.0 / D, bias=EPS), reads=[ssq], writes=[sq])
            P.op("dve", lambda e: e.reciprocal(out=rs.ap, in_=sq.ap), reads=[sq], writes=[rs])
            P.op("dve", lambda e: e.scalar_tensor_tensor(out=xn.ap, in0=xt.ap, scalar=rs.ap, in1=grep_T.ap,
                                                         op0=ALU.mult, op1=ALU.mult),
                 reads=[xt, rs, grep_T], writes=[xn])

        def norm1_A(t, extra=()):
            xt, xn = X[t], xn1[0]
            ssq, sq, rs = newstat(), newstat(), newstat()
            P.op("act", act_fn(xn.ap, xt.ap, AF.Square, accum_out=ssq.ap), reads=[xt], writes=[xn, ssq], extra=extra)
            P.op("act", act_fn(sq.ap, ssq.ap, AF.Sqrt, scale=1.0 / D, bias=EPS), reads=[ssq], writes=[sq])
            P.op("dve", lambda e: e.reciprocal(out=rs.ap, in_=sq.ap), reads=[sq], writes=[rs], extra=extra)
            P.op("act", act_fn(xn.ap, xt.ap, AF.Identity, scale=rs.ap), reads=[xt, rs], writes=[xn])

        def norm1_B(t, extra=()):
            xn = xn1[0]
            for half in range(2):
                bk = newbank()

                def tr(e, half=half, bk=bk):
                    ins = None
                    for j in range(4):
                        kc = half * 4 + j
                        ins = e.transpose(out=bk.ap[:, j * 128:(j + 1) * 128], in_=xn.ap[:, kc * 128:(kc + 1) * 128],
                                          identity=identf_t[:, :])
                    return ins
                P.op("pe", tr, reads=[xn, identf], writes=[bk], extra=extra)
                for j in range(4):
                    kc = half * 4 + j
                    P.op("act", act_fn(h1T_a[:, kc, HX + t * 128: HX + (t + 1) * 128], bk.ap[:, j * 128:(j + 1) * 128], AF.Identity,
                                       scale=vcol(V_G1, kc)), reads=[bk, vecs], writes=[h1Tb[t]])

        def norm_B(xn, dstT_blocks_ap, dst_T):
            bk = newbank()
            bkb = bk.ap[:, :].bitcast(BF16)

            def tr(e):
                ins = None
                for kc in range(8):
                    ins = e.transpose(out=bkb[:, kc * 128:(kc + 1) * 128], in_=xn.ap[:, kc * 128:(kc + 1) * 128],
                                      identity=ident_t[:, :])
                return ins
            P.op("pe", tr, reads=[xn, ident], writes=[bk])
            P.op("act", act_fn(dstT_blocks_ap, bkb.rearrange("p (k n) -> p k n", k=8), AF.Copy),
                 reads=[bk], writes=[dst_T])

        seg_ctr = [0]

        def mixer_segment(gi, col0, L, S, Ls, tiles, kind, first_seq, last_prompt):
            is_s = kind == "s"
            gz = gz2[seg_ctr[0] % 2]
            seg_ctr[0] += 1

            def as3(ap2):
                return ap2.rearrange("p (s l) -> p s l", s=S)

            h1 = [h1Tb[t] for t in tiles]
            mT = [mixedTb[t] for t in tiles]

            H = 0 if (is_s or first_seq) else HX

            def w_in_mm(m, hist=0):
                bk = newbank()

                def mm(e, m=m, bk=bk):
                    ins = None
                    for kc in range(8):
                        ins = e.matmul(bk.ap[:, 0:L + hist], lhsT=w_in_bf[:, kc, m * 128:(m + 1) * 128],
                                       rhs=h1T_a[:, kc, HX + col0 - hist: HX + col0 + L], start=(kc == 0), stop=(kc == 7))
                    return ins
                rd = h1 + [w_in_T[m // 6]]
                if hist:
                    rd.append(h1T_hist if col0 == 0 else h1Tb[col0 // 128 - 1])
                P.op("pe", mm, reads=rd, writes=[bk])
                return bk

            ext3 = {}
            extT = {}

            def fronta():
              if is_s:
                  for c in range(4):
                      bk = w_in_mm(c)
                      P.op("dve", lambda e, c=c, bk=bk: e.tensor_copy(out=zxs_t[:, c, :, 3:3 + LS], in_=as3(bk.ap[:, 0:L])),
                           reads=[bk], writes=[zxs[c]])
                      ext3[c], extT[c] = zxs_t[:, c, :, :], zxs[c]
              else:
                  zb = {}
                  for c in range(4):
                      bk = w_in_mm(c, hist=H)
                      zb[c] = bk
                      st = lru_sets[c]
                      P.op("act", act_fn(st["acc"].ap[:, 0:L], bk.ap[:, H:H + L], AF.Identity, scale=vcol(V_CW, 3 * 4 + c),
                                         bias=vcol(V_CB, c)), reads=[bk, vecs], writes=[st["acc"]])
                      if last_prompt:
                          P.op("act", act_fn(olc_t[:, c, 0, :], bk.ap[:, H + L - 3:H + L], AF.Copy), reads=[bk], writes=[olc])
                  for j in (2, 1, 0):
                      sh = 3 - j
                      for c in range(4):
                          st = lru_sets[c]
                          bk = zb[c]
                          if H:
                              o_, i_ = st["acc"].ap[:, 0:L], bk.ap[:, H - sh:H - sh + L]
                          else:
                              o_, i_ = st["acc"].ap[:, sh:L], bk.ap[:, 0:L - sh]
                          P.op("dve", lambda e, o_=o_, i_=i_, j=j, c=c: e.scalar_tensor_tensor(
                              out=o_, in0=i_, scalar=vcol(V_CW, j * 4 + c), in1=o_, op0=ALU.mult, op1=ALU.add),
                              reads=[bk, vecs], writes=[st["acc"]])
              for g, w in enumerate(POOL_W):
                  bk = w_in_mm(8 + g)
                  if is_s:
                      P.op("act", act_fn(zps_t[:, g, :, 15:15 + LS], as3(bk.ap[:, 0:L]), AF.Copy), reads=[bk], writes=[zps[g]])
                      P.op("act", act_fn(opool_t[:, g, 1:17, 7:15], as3(bk.ap[:, 0:L]), AF.Copy), reads=[bk], writes=[opool])
                  else:
                      P.op("pool", lambda e, g=g: e.tensor_copy(out=zpe[g].ap[:, 0, 0:15], in_=histp_t[:, g, :]),
                           reads=[histp[g]], writes=[zpe[g]])
                      P.op("act", act_fn(zpe[g].ap[:, :, 15:15 + L], as3(bk.ap[:, 0:L]), AF.Copy), reads=[bk], writes=[zpe[g]])
                      P.op("pool", lambda e, g=g: e.tensor_copy(out=histp_t[:, g, :], in_=zpe[g].ap[:, 0, L:L + 15]),
                           reads=[zpe[g]], writes=[histp[g]])
                      if last_prompt:
                          P.op("act", act_fn(opool_t[:, g, 0, :], bk.ap[:, L - 15:L], AF.Copy), reads=[bk], writes=[opool])
            acc3 = {c: lru_sets[c]["acc"].ap[:, 0:L].rearrange("p (s l) -> p s l", s=S) for c in range(4)}

            def stageC(cs):
                if not is_s:
                    return
                for c in cs:
                    st = lru_sets[c]
                    P.op("dve", lambda e, c=c: e.tensor_scalar(out=acc3[c], in0=ext3[c][:, :, 3:3 + Ls],
                                                               scalar1=vcol(V_CW, 3 * 4 + c), scalar2=vcol(V_CB, c),
                                                               op0=ALU.mult, op1=ALU.add),
                         reads=[extT[c], vecs], writes=[st["acc"]])
                for j in (2, 1, 0):
                    for c in cs:
                        st = lru_sets[c]
                        P.op("dve", lambda e, j=j, c=c: e.scalar_tensor_tensor(out=acc3[c], in0=ext3[c][:, :, j:j + Ls],
                                                                               scalar=vcol(V_CW, j * 4 + c), in1=acc3[c],
                                                                               op0=ALU.mult, op1=ALU.add),
                             reads=[extT[c], vecs], writes=[st["acc"]])
                if is_s:
                    for c in cs:
                        P.op("pool", lambda e, c=c: e.tensor_copy(out=olc_t[:, c, 1:17, :], in_=zxs_t[:, c, :, LS:LS + 3]),
                             reads=[zxs[c]], writes=[olc])

            def stageD(cs):
                banks_ax = {}
                for c in cs:
                    st = lru_sets[c]
                    P.op("dve", lambda e, st=st: e.tensor_copy(out=st["xcb"].ap[:, 0:L], in_=st["acc"].ap[:, 0:L]),
                         reads=[st["acc"]], writes=[st["xcb"]])
                    bka, bkx = newbank(), newbank()
                    P.op("pe", lambda e, st=st, c=c, bka=bka: e.matmul(bka.ap[:, 0:L], lhsT=wa_bd[:, c, :], rhs=st["xcb"].ap[:, 0:L],
                                                                       start=True, stop=True),
                         reads=[st["xcb"], wab_T], writes=[bka])
                    P.op("pe", lambda e, st=st, c=c, bkx=bkx: e.matmul(bkx.ap[:, 0:L], lhsT=wx_bd[:, c, :], rhs=st["xcb"].ap[:, 0:L],
                                                                       start=True, stop=True),
                         reads=[st["xcb"], wxb_T], writes=[bkx])
                    banks_ax[c] = (bka, bkx)
                for c in cs:
                    st = lru_sets[c]
                    ba_, bx_ = banks_ax[c]
                    P.op("act", act_fn(st["ti"].ap[:, 0:L], bx_.ap[:, 0:L], AF.Tanh, scale=0.5, bias=dcol(D_HBX, c)),
                         reads=[bx_, der], writes=[st["ti"]])
                for c in cs:
                    st = lru_sets[c]
                    ba_, bx_ = banks_ax[c]
                    P.op("act", act_fn(st["tr"].ap[:, 0:L], ba_.ap[:, 0:L], AF.Tanh, scale=0.5, bias=dcol(D_HBA, c)),
                         reads=[ba_, der], writes=[st["tr"]])

            def stageD2(cs):
                for c in cs:
                    st = lru_sets[c]
                    P.op("act", act_fn(st["a"].ap[:, 0:L], st["tr"].ap[:, 0:L], AF.Exp, scale=dcol(D_HC, c), bias=dcol(D_HC, c)),
                         reads=[st["tr"], der], writes=[st["a"]])
                    P.op("pool", lambda e, st=st: e.tensor_tensor(out=st["a2"].ap[:, 0:L], in0=st["a"].ap[:, 0:L],
                                                                  in1=st["a"].ap[:, 0:L], op=ALU.mult),
                         reads=[st["a"]], writes=[st["a2"]])

            def stageE(cs):
                for c in cs:
                    st = lru_sets[c]
                    P.op("act", act_fn(st["a2"].ap[:, 0:L], st["a2"].ap[:, 0:L], AF.Sqrt, scale=-1.0, bias=1.0),
                         reads=[], writes=[st["a2"]])

            def stageF1(cs):
                for c in cs:
                    st = lru_sets[c]
                    P.op("dve", lambda e, st=st: e.scalar_tensor_tensor(out=st["ti"].ap[:, 0:L], in0=st["ti"].ap[:, 0:L], scalar=1.0,
                                                                        in1=st["acc"].ap[:, 0:L], op0=ALU.add, op1=ALU.mult),
                         reads=[st["acc"]], writes=[st["ti"]])

            def stageF(cs):
                for c in cs:
                    st = lru_sets[c]
                    P.op("dve", lambda e, st=st: e.scalar_tensor_tensor(out=st["ti"].ap[:, 0:L], in0=st["ti"].ap[:, 0:L], scalar=0.5,
                                                                        in1=st["a2"].ap[:, 0:L], op0=ALU.mult, op1=ALU.mult),
                         reads=[st["a2"]], writes=[st["ti"]])
                for c in cs:
                    st = lru_sets[c]
                    hb = st["tr"]
                    if is_s:
                        a3s = st["a"].ap[:, 0:L].rearrange("p (s l) -> p s l", s=NS)
                        b3s = st["ti"].ap[:, 0:L].rearrange("p (s l) -> p s l", s=NS)
                        P.op("dve", lambda e, a3s=a3s, c=c: e.tensor_tensor(out=fixS.ap, in0=a3s[:, :, 0], in1=sth_t[:, c, :],
                                                                           op=ALU.mult),
                             reads=[st["a"], sth], writes=[fixS])
                        P.op("dve", lambda e, b3s=b3s: e.tensor_tensor(out=b3s[:, :, 0], in0=b3s[:, :, 0], in1=fixS.ap, op=ALU.add),
                             reads=[fixS], writes=[st["ti"]])
                        P.op("dve", lambda e, a3s=a3s: e.memset(a3s[:, :, 0:1], 0.0), writes=[st["a"]])
                        P.op("dve", lambda e, st=st, hb=hb: e.tensor_tensor_scan(
                            out=hb.ap[:, 0:L], data0=st["a"].ap[:, 0:L], data1=st["ti"].ap[:, 0:L], initial=0.0,
                            op0=ALU.mult, op1=ALU.add),
                            reads=[st["a"], st["ti"]], writes=[hb])
                        P.op("dve", lambda e, hb=hb, c=c: e.tensor_copy(
                            out=oh_t[:, c, 1:17], in_=hb.ap[:, 0:L].rearrange("p (s l) -> p s l", s=NS)[:, :, LS - 1]),
                            reads=[hb], writes=[oh])
                    else:
                        init = 0.0 if first_seq else oh_t[:, c, 0:1]
                        P.op("dve", lambda e, st=st, hb=hb, init=init: e.tensor_tensor_scan(
                            out=hb.ap[:, 0:L], data0=st["a"].ap[:, 0:L], data1=st["ti"].ap[:, 0:L], initial=init,
                            op0=ALU.mult, op1=ALU.add),
                            reads=[st["a"], st["ti"], oh], writes=[hb])
                        P.op("dve", lambda e, hb=hb, c=c: e.tensor_copy(out=oh_t[:, c, 0:1], in_=hb.ap[:, L - 1:L]),
                             reads=[hb], writes=[oh])
                    P.op("pool", lambda e, hb=hb, c=c: e.tensor_tensor(out=mixedT_a[:, c, col0:col0 + L], in0=hb.ap[:, 0:L],
                                                                       in1=gz[c].ap[:, 0:L], op=ALU.mult),
                         reads=[hb, gz[c]], writes=mT)

            def frontb():
                for c in range(4):
                    bk = w_in_mm(4 + c)
                    P.op("act", act_fn(gz[c].ap[:, 0:L], bk.ap[:, 0:L], AF.Gelu_apprx_tanh), reads=[bk], writes=[gz[c]])

            def conv():
                stageC([0, 1])
                stageC([2, 3])

            def poolG():
              for g, w in enumerate(POOL_W):
                  bk = newbank()
                  src = zps_t[:, g, :, :] if is_s else zpe[g].ap
                  srcT = zps[g] if is_s else zpe[g]
                  do_fix = first_seq and not is_s
                  if do_fix:
                      P.op("dve", lambda e, g=g: e.tensor_tensor_scan(out=fixS.ap, data0=ones16_t[:, :],
                                                                      data1=zpe[g].ap[:, 0, 15:31], initial=0.0,
                                                                      op0=ALU.mult, op1=ALU.add),
                           reads=[zpe[g], ones16], writes=[fixS])
                      P.op("dve", lambda e, g=g: e.tensor_tensor(out=fixSd.ap, in0=fixS.ap, in1=dvec_t[:, g, :], op=ALU.mult),
                           reads=[fixS, dvec], writes=[fixSd])

                  def pm(e, g=g, w=w, bk=bk, src=src, do_fix=do_fix):
                      ins = None
                      out3 = as3(bk.ap[:, 0:L])
                      for k in range(w):
                          ins = e.matmul(out3, lhsT=pws[:, g, :], rhs=src[:, :, 15 - k:15 - k + Ls],
                                         start=(k == 0), stop=False)
                      ins = e.matmul(out3, lhsT=pwn[:, g, :], rhs=src[:, :, 15:15 + Ls], start=False, stop=(not do_fix))
                      if do_fix:
                          ins = e.matmul(bk.ap[:, 0:16], lhsT=pwb[:, g, :], rhs=fixSd.ap, start=False, stop=True)
                      return ins
                  rd = [srcT, pws_T, pwn_T] + ([fixSd, pwb_T] if do_fix else [])
                  P.op("pe", pm, reads=rd, writes=[bk])
                  P.op("act", act_fn(mixedT_a[:, 4 + g, col0:col0 + L], bk.ap[:, 0:L], AF.Identity, scale=vcol(V_PS, g)),
                       reads=[bk, vecs], writes=mT)
            def S12():
                stageD([0, 1, 2, 3])
                stageF1([0, 1, 2, 3])

            def S34():
                stageD2([0, 1, 2, 3])
                stageE([0, 1, 2, 3])
                stageF([0, 1])
                stageF([2, 3])

            return {"fronta": fronta, "frontb": frontb, "conv": conv, "poolG": poolG, "S12": S12, "S34": S34}

        def wout_A(t):
            for half in range(2):
                bk = newbank()

                def mm(e, half=half, bk=bk):
                    ins = None
                    for kc in range(8):
                        ins = e.matmul(bk.ap[:, :], lhsT=mixedT_a[:, kc, t * 128:(t + 1) * 128],
                                       rhs=w_out_bf[:, kc, half * 512:(half + 1) * 512], start=(kc == 0), stop=(kc == 7))
                    return ins
                P.op("pe", mm, reads=[mixedTb[t], w_out_T], writes=[bk])
                xs_ = X[t].ap[:, half * 512:(half + 1) * 512]
                P.op("dve", lambda e, bk=bk, xs_=xs_: e.tensor_tensor(out=xs_, in0=bk.ap[:, :], in1=xs_, op=ALU.add),
                     reads=[bk, X[t]], writes=[X[t]])
            norm_A(X[t], grep, xn_m[t % 3])

        def wout_B(t):
            norm_B(xn_m[t % 3], h2T_t[:, :, 2 + t * 128: 2 + (t + 1) * 128], h2Tb[t])

        tail_pending = []
        mult_evs = {}

        def flush_tail():
            while tail_pending:
                tail_pending.pop(0)()

        def up_pair(gi, pr, stg, j, segs_p, has_s, nt_p, first_group, last_group, on_last_mm=None):
            cg, cv = pr, 24 + pr
            L = segs_p[0][1]
            N = L + 2
            ig, bg0, bg1 = newbank_pair()
            iv, bv0, bv1 = newbank_pair()
            bks = {0: (bg0, bg1), 1: (bv0, bv1)}
            ib = {0: ig, 1: iv}
            all_tiles = list(range(0, (2 * L) // 128))
            for si, (c0, L_) in enumerate(segs_p):
                tiles = list(range(c0 // 128, (c0 + L) // 128))
                rd = [h2Tb[t] for t in tiles] + list(upb_T[stg])
                rd.append(h2T_hist if c0 == 0 else h2Tb[c0 // 128 - 1])
                for gv in (0, 1):
                    bk = bks[gv][si]

                    def mm(e, gv=gv, bk=bk, c0=c0, N=N, L=L):
                        ins = None
                        for kc in range(8):
                            ins = e.matmul(bk.ap[:, 0:N], lhsT=upb[stg][:, gv, kc, j * 128:(j + 1) * 128],
                                           rhs=h2T_t[:, kc, c0: 2 + c0 + L], start=(kc == 0), stop=(kc == 7))
                        return ins
                    P.op("pe", mm, reads=rd, writes=[bk])
            if on_last_mm is not None and not has_s:
                on_last_mm()
            k_ = acc_rr[0] % NACC
            acc_rr[0] += 1
            acc = accA[k_]
            aT = accT[k_]
            gb = gbuf[k_]
            for gv, ch in ((0, cg), (1, cv)):
                P.op("act", act_fn(acc[:, gv, :, 0:L], ps_t[:, ib[gv]:ib[gv] + 2, 2:2 + L], AF.Identity,
                                   scale=vcol(V_FCW, 2 * 48 + ch), bias=vcol(V_FCB, ch)),
                     reads=list(bks[gv]) + [vecs], writes=[aT[gv]])
            for tap, sh in ((1, 1), (0, 2)):
                for gv, ch in ((0, cg), (1, cv)):
                    o_ = acc[:, gv, :, 0:L]
                    i_ = ps_t[:, ib[gv]:ib[gv] + 2, 2 - sh:2 - sh + L]
                    P.op("dve", lambda e, o_=o_, i_=i_, tap=tap, ch=ch: e.scalar_tensor_tensor(
                        out=o_, in0=i_, scalar=vcol(V_FCW, tap * 48 + ch), in1=o_, op0=ALU.mult, op1=ALU.add),
                        reads=list(bks[gv]) + [vecs], writes=[aT[gv]])
            if last_group:
                for gv, ch in ((0, cg), (1, cv)):
                    bk = bks[gv][1]
                    P.op("dve", lambda e, bk=bk, ch=ch, N=N: e.tensor_copy(out=offn_t[:, ch, 0, :], in_=bk.ap[:, N - 2:N]),
                         reads=[bk], writes=[offn])
            flush_tail()

            def tail(gb=gb, acc=acc, aT=aT, L=L, all_tiles=all_tiles):
                P.op("act", act_fn(gb.ap[:, :, 0:L], acc[:, 0, :, 0:L], AF.Gelu_apprx_tanh), reads=[aT[0]], writes=[gb])
                mult_evs[pr] = P.op("dve", lambda e: e.tensor_tensor(
                    out=actT_a[:, pr, 0:2 * L].rearrange("p (s l) -> p s l", s=2), in0=gb.ap[:, :, 0:L],
                    in1=acc[:, 1, :, 0:L], op=ALU.mult),
                    reads=[gb, aT[1]], writes=[actTb[t] for t in all_tiles])
            tail_pending.append(tail)
            if has_s:
                c0 = nt_p * 128
                L = NS * LS
                t = nt_p
                bg, bv = newbank(), newbank()
                ue = uexts[pr % 2]
                k_ = acc_rr[0] % NACC
                acc_rr[0] += 1
                acc = accA[k_][:, :, 0, :]
                aT = accT[k_]
                gb = T(gbuf[k_].ap[:, 0, :])
                gb_full = gbuf[k_]
                for gv, bk in ((0, bg), (1, bv)):
                    def mm(e, gv=gv, bk=bk, c0=c0, L=L):
                        ins = None
                        for kc in range(8):
                            ins = e.matmul(bk.ap[:, 0:L], lhsT=upb[stg][:, gv, kc, j * 128:(j + 1) * 128],
                                           rhs=h2T_t[:, kc, 2 + c0: 2 + c0 + L], start=(kc == 0), stop=(kc == 7))
                        return ins
                    P.op("pe", mm, reads=[h2Tb[t]] + list(upb_T[stg]), writes=[bk])
                if on_last_mm is not None:
                    on_last_mm()
                ueT = uext_halves[pr % 2]
                for gv, bk, ch in ((0, bg, cg), (1, bv, cv)):
                    P.op("act", act_fn(ue.ap[:, gv, :, 0:2], stffn_t[:, ch, :, :], AF.Copy), reads=[stffn], writes=[ueT[gv]])
                    b3 = bk.ap[:, 0:L].rearrange("p (s l) -> p s l", s=NS)
                    P.op("act", act_fn(ue.ap[:, gv, :, 2:2 + LS], b3, AF.Copy), reads=[bk], writes=[ueT[gv]])
                for gv, bk, ch in ((0, bg, cg), (1, bv, cv)):
                    a3 = acc[:, gv, 0:L].rearrange("p (s l) -> p s l", s=NS)
                    if gv == 0:
                        b3 = bk.ap[:, 0:L].rearrange("p (s l) -> p s l", s=NS)
                        P.op("act", act_fn(a3, b3, AF.Identity, scale=vcol(V_FCW, 2 * 48 + ch), bias=vcol(V_FCB, ch)),
                             reads=[bk, vecs], writes=[aT[gv]])
                    else:
                        P.op("dve", lambda e, gv=gv, ch=ch, a3=a3: e.tensor_scalar(
                            out=a3, in0=ue.ap[:, gv, :, 2:2 + LS], scalar1=vcol(V_FCW, 2 * 48 + ch), scalar2=vcol(V_FCB, ch),
                            op0=ALU.mult, op1=ALU.add), reads=[ueT[gv], vecs], writes=[aT[gv]])
                for tap, sh in ((1, 1), (0, 2)):
                    for gv, bk, ch in ((0, bg, cg), (1, bv, cv)):
                        a3 = acc[:, gv, 0:L].rearrange("p (s l) -> p s l", s=NS)
                        P.op("dve", lambda e, gv=gv, ch=ch, a3=a3, tap=tap, sh=sh: e.scalar_tensor_tensor(
                            out=a3, in0=ue.ap[:, gv, :, 2 - sh:2 - sh + LS], scalar=vcol(V_FCW, tap * 48 + ch), in1=a3,
                            op0=ALU.mult, op1=ALU.add), reads=[ueT[gv], vecs], writes=[aT[gv]])
                for gv, bk, ch in ((0, bg, cg), (1, bv, cv)):
                    P.op("act", act_fn(offn_t[:, ch, 1:17, :], ue.ap[:, gv, :, LS:LS + 2], AF.Copy), reads=[ueT[gv]], writes=[offn])
                flush_tail()

                def tail(gb=gb, gb_full=gb_full, acc=acc, aT=aT, c0=c0, L=L, t=t):
                    P.op("act", act_fn(gb.ap[:, 0:L], acc[:, 0, 0:L], AF.Gelu_apprx_tanh), reads=[aT[0]], writes=[gb_full])
                    P.op("dve", lambda e: e.tensor_tensor(
                        out=actT_a[:, pr, c0:c0 + L], in0=gb.ap[:, 0:L], in1=acc[:, 1, 0:L], op=ALU.mult),
                        reads=[gb_full, aT[1]], writes=[actTb[t]])
                tail_pending.append(tail)

        def down_final(gi, t, yrow0):
            CS = 20
            split = (t == 0 and (CS - 1) in mult_evs)
            bks2 = [newbank(), newbank()]

            def mk(half, bk):
                def mm(e, c_lo=0, c_hi=24):
                    ins = None
                    for c in range(c_lo, c_hi):
                        ins = e.matmul(bk.ap[:, :], lhsT=actT_a[:, c, t * 128:(t + 1) * 128],
                                       rhs=wd_bf[:, c, half * 512:(half + 1) * 512], start=(c == 0), stop=(c == 23))
                    return ins
                return mm
            mms = [mk(0, bks2[0]), mk(1, bks2[1])]
            if split:
                for half in range(2):
                    ev_ = P.op("pe", lambda e, mm=mms[half]: mm(e, c_lo=0, c_hi=CS), reads=wd_T[:CS // 2],
                               writes=[bks2[half]], extra=[mult_evs[CS - 1]])
                    actTb[t].readers.append(ev_)
                for half in range(2):
                    P.op("pe", lambda e, mm=mms[half]: mm(e, c_lo=CS, c_hi=24), reads=[actTb[t]] + wd_T[CS // 2:],
                         writes=[bks2[half]])
            else:
                for half in range(2):
                    P.op("pe", mms[half], reads=[actTb[t]] + wd_T, writes=[bks2[half]])
            for half in range(2):
                bk = bks2[half]
                xs_ = X[t].ap[:, half * 512:(half + 1) * 512]
                P.op("dve", lambda e, bk=bk, xs_=xs_: e.tensor_tensor(out=xs_, in0=bk.ap[:, :], in1=xs_, op=ALU.add),
                     reads=[bk, X[t]], writes=[X[t]])
            ssq, sq, rs = newstat(), newstat(), newstat()
            yb = ybuf[0]
            P.op("act", act_fn(yb.ap, X[t].ap, AF.Square, accum_out=ssq.ap), reads=[X[t]], writes=[yb, ssq])
            P.op("act", act_fn(sq.ap, ssq.ap, AF.Sqrt, scale=1.0 / D, bias=EPS), reads=[ssq], writes=[sq])
            P.op("dve", lambda e: e.reciprocal(out=rs.ap, in_=sq.ap), reads=[sq], writes=[rs])
            P.op("dve", lambda e: e.scalar_tensor_tensor(out=yb.ap, in0=X[t].ap, scalar=rs.ap, in1=grep.ap,
                                                         op0=ALU.mult, op1=ALU.mult),
                 reads=[X[t], rs, grep], writes=[yb])
            P.dma("sp", "d_y", y_d[yrow0:yrow0 + 128, :], yb.ap, reads=[yb])

        deferred_norm1 = []
        load_x(0)
        for gi, (p0, npt, has_s) in enumerate(GROUPS):
            nt_p = npt // 128
            nt = nt_p + (1 if has_s else 0)
            first_group = gi == 0
            last_group = gi == len(GROUPS) - 1
            if gi > 0:
                P.barrier()
            else:
                def p1_tiles(t0, t1):
                    for t in range(t0, t1 + 1):
                        if t < t1:
                            norm_A(X[t], grep, xn_m[t % 2])
                        if t >= t0 + 1:
                            norm_B(xn_m[(t - 1) % 2], h1T_a[:, :, HX + (t - 1) * 128: HX + t * 128], h1Tb[t - 1])
                p1_tiles(0, 3)
                deferred_norm1.append(lambda nt=nt: p1_tiles(3, nt))
            P.dma("pool", "d_w", w_out_bf[:, :, :], w_out_v[:, :, :], writes=[w_out_T])
            if gi > 0:
                P.op("pool", lambda e: e.tensor_copy(out=h1T_a[:, :, 0:HX], in_=h1hist_t[:, :, 0:HX]),
                     reads=[h1hist], writes=[h1T_hist])
            if gi == 0:
                scale_pool_weights()
            segs = []
            c0 = 0
            while c0 < npt:
                L = min(384 if npt % 384 == 0 else 256, npt - c0)
                segs.append((c0, L))
                c0 += L
            sg = []
            for si, (c0, L) in enumerate(segs):
                sg.append(mixer_segment(gi, c0, L, 1, L, list(range(c0 // 128, (c0 + L) // 128)), "p",
                                        first_seq=(first_group and si == 0),
                                        last_prompt=(last_group and si == len(segs) - 1)))
            if has_s:
                sg.append(mixer_segment(gi, nt_p * 128, NS * LS, NS, LS, [nt_p], "s", first_seq=False, last_prompt=False))
            sg[0]["fronta"]()
            sg[0]["frontb"]()
            sg[0]["conv"]()
            sg[0]["poolG"]()
            while deferred_norm1:
                deferred_norm1.pop(0)()
            for k in range(len(sg)):
                sg[k]["S12"]()
                if k + 1 < len(sg):
                    sg[k + 1]["fronta"]()
                    sg[k + 1]["conv"]()
                sg[k]["S34"]()
                if k + 1 < len(sg):
                    sg[k + 1]["frontb"]()
                    sg[k + 1]["poolG"]()
            if not last_group:
                P.op("pool", lambda e, npt=npt: e.tensor_copy(out=h1hist_t[:, :, 0:HX], in_=h1T_a[:, :, npt:npt + HX]),
                     reads=[h1Tb[nt_p - 1]], writes=[h1hist])
            half_p = npt // 2
            segs_p = [(0, half_p), (half_p, half_p)]
            nstages = 12

            def load_stage(s):
                stg = s % NST
                pr0 = s * 2
                P.dma("pool", "d_up", upb[stg][:, 0, :, :], up_v[:, :, pr0 * 128: pr0 * 128 + 256], writes=[upb_T[stg][0]])
                P.dma("pool", "d_up", upb[stg][:, 1, :, :], up_v[:, :, DFF + pr0 * 128: DFF + pr0 * 128 + 256],
                      writes=[upb_T[stg][1]])
            P._wait(P.q["pool"], [(q_.sem_key, q_.cnt) for q_ in P.q.values() if q_.cnt > 0 and q_.name != "pool"])
            for s0_ in range(NST):
                load_stage(s0_)
            P.dma("sp", "d_misc", grep.ap, g2r_d[:, :], writes=[grep])
            for t in range(nt + 2):
                if t < nt:
                    wout_A(t)
                if t >= 2:
                    wout_B(t - 2)
            P.barrier()
            P.dma("sp", "d_misc", grep.ap, gfr_d[:, :], writes=[grep])
            mult_evs.clear()
            for s in range(nstages):
                def prefetch(s=s):
                    if s + NST < nstages:
                        load_stage(s + NST)
                    P.dma("pool", "d_wd", wd_bf[:, s * 2:(s + 1) * 2, :], down_v[:, s * 2:(s + 1) * 2, :], writes=[wd_T[s]])
                up_pair(gi, s * 2, s % NST, 0, segs_p, has_s, nt_p, first_group, last_group)
                up_pair(gi, s * 2 + 1, s % NST, 1, segs_p, has_s, nt_p, first_group, last_group, on_last_mm=prefetch)
            flush_tail()
            if not last_group:
                P.op("dve", lambda e, npt=npt: e.tensor_copy(out=h2T_t[:, :, 0:2], in_=h2T_t[:, :, npt:npt + 2]),
                     reads=[h2Tb[nt_p - 1]], writes=[h2T_hist])
            evs_p4 = [(q_.sem_key, q_.cnt) for q_ in P.q.values() if q_.cnt > 0 and q_.name != "sp"]
            if not last_group:
                P.dma("pool", "d_w", w_in_bf[:, :, 0:768], w_in_v[:, :, 0:768], writes=[w_in_T[0]], extra=evs_p4)
                P.dma("pool", "d_w", w_in_bf[:, :, 768:1536], w_in_v[:, :, 768:1536], writes=[w_in_T[1]], extra=evs_p4)
                np0, nnpt, nhs = GROUPS[gi + 1]
                nnt_p = nnpt // 128
                nnt = nnt_p + (1 if nhs else 0)
            else:
                nnt = 0
            doneA = doneB = 0

            def load_next(tt):
                if tt < nnt_p:
                    P.dma("sp", "d_x", X[tt].ap, xp[np0 + tt * 128: np0 + (tt + 1) * 128, :], writes=[X[tt]])
                else:
                    P.dma("sp", "d_x", X[tt].ap, xs[:, :], writes=[X[tt]])

            for t in range(nt):
                yrow0 = (p0 + t * 128) if t < nt_p else SEQ
                down_final(gi, t, yrow0)
                if t < nnt:
                    load_next(t)
                if doneB < doneA and doneB <= t - 2:
                    norm1_B(doneB, extra=evs_p4)
                    doneB += 1
                if doneA < nnt and doneA <= t - 1:
                    norm1_A(doneA, extra=evs_p4)
                    doneA += 1
            for tt in range(nt, nnt):
                load_next(tt)
            def norm1_tail(doneA=doneA, doneB=doneB, nnt=nnt, evs_p4=evs_p4):
                while doneB < nnt:
                    if doneA == doneB:
                        norm1_A(doneA, extra=evs_p4)
                        doneA += 1
                    norm1_B(doneB, extra=evs_p4)
                    doneB += 1
            deferred_norm1.append(norm1_tail)

        P.dma("sp", "d_y", olc_d[:, :], olc_t[:, :, :, :].rearrange("p a b c -> p (a b c)"), reads=[olc])
        P.dma("sp", "d_y", oh_d[:, :], oh_t[:, :, :].rearrange("p a b -> p (a b)"), reads=[oh])
        P.dma("sp", "d_y", opool_d[:, :], opool_t[:, :, :, :].rearrange("p a b c -> p (a b c)"), reads=[opool])
        P.dma("sp", "d_y", offn_d[:, :], offn_t[:, :, :, :].rearrange("p a b c -> p (a b c)"), reads=[offn])
        P.final_wait("sp")

        with nc.Block() as block:
            def play(eng, q):
                own = sems[q.sem_key]
                for item in q.ops:
                    if item[0] == "wait":
                        eng.wait_ge(sems[item[1]], item[2])
                    elif item[0] == "op":
                        item[1](eng).then_inc(own, 1)
                    else:
                        eng.dma_start(out=item[1], in_=item[2]).then_inc(sems[item[3]], 16)

            @block.sync
            def _(e):
                play(e, P.q["sp"])

            @block.scalar
            def _(e):
                play(e, P.q["act"])

            @block.vector
            def _(e):
                play(e, P.q["dve"])

            @block.gpsimd
            def _(e):
                play(e, P.q["pool"])

            @block.tensor
            def _(e):
                play(e, P.q["pe"])
    build_program.last_prog = P
    return nc


_NC_CACHE = {}


def _layout_vec(v, nchunk):
    return np.ascontiguousarray(np.asarray(v, np.float32).reshape(nchunk, 128).T)


def kernel(x_prompt, x_sample, state_lru_conv, state_lru_h, state_pool, state_ffn_conv,
           norm1_g, w_in, lru_conv_w, lru_conv_b, lru_wa, lru_ba, lru_wx, lru_bx, lru_lambda,
           pool_w, pool_scale, w_out, norm2_g, ffn_up, ffn_conv_w, ffn_conv_b, ffn_down, final_g):
    f = lambda a: np.ascontiguousarray(np.asarray(a, dtype=np.float32))
    x_prompt, x_sample = f(x_prompt), f(x_sample)
    vecs = np.zeros((128, NV), np.float32)
    cw = f(lru_conv_w)[0]
    for j in range(4):
        vecs[:, V_CW + j * 4: V_CW + (j + 1) * 4] = _layout_vec(cw[j], 4)
    vecs[:, V_CB:V_CB + 4] = _layout_vec(f(lru_conv_b)[0], 4)
    vecs[:, V_BA:V_BA + 4] = _layout_vec(f(lru_ba)[0], 4)
    vecs[:, V_BX:V_BX + 4] = _layout_vec(f(lru_bx)[0], 4)
    vecs[:, V_LAM:V_LAM + 4] = _layout_vec(f(lru_lambda)[0], 4)
    vecs[:, V_PS:V_PS + 4] = _layout_vec(f(pool_scale)[0], 4)
    fcw = f(ffn_conv_w)[0]
    for j in range(3):
        vecs[:, V_FCW + j * 48: V_FCW + (j + 1) * 48] = _layout_vec(fcw[j], 48)
    vecs[:, V_FCB:V_FCB + 48] = _layout_vec(f(ffn_conv_b)[0], 48)
    vecs[:, V_G1:V_G1 + 8] = _layout_vec(f(norm1_g)[0], 8)
    rep = lambda g: np.ascontiguousarray(np.broadcast_to(f(g).reshape(1, D), (128, D)))
    common = {
        "vecs": vecs, "g1r": rep(norm1_g[0]), "g2r": rep(norm2_g[0]), "gfr": rep(final_g),
        "ident": np.eye(128, dtype=np.float32),
        "w_in": f(w_in)[0], "w_out": f(w_out)[0], "ffn_up": f(ffn_up)[0], "ffn_down": f(ffn_down)[0],
        "wa": f(lru_wa)[0], "wx": f(lru_wx)[0], "pw": f(pool_w)[0],
    }
    slc, slh, spl, sff = f(state_lru_conv)[0], f(state_lru_h)[0], f(state_pool)[0], f(state_ffn_conv)[0]
    in_maps = []
    for c in range(NCORES):
        s0, s1 = c * NS, (c + 1) * NS
        m = dict(common)
        m["xp"] = x_prompt[c]
        m["xs"] = x_sample[s0:s1].reshape(NS * LS, D)
        m["stlc"] = np.ascontiguousarray(slc[s0:s1].reshape(NS, 3, 4, 128).transpose(3, 2, 0, 1)).reshape(128, -1)
        m["sth"] = np.ascontiguousarray(slh[s0:s1].reshape(NS, 4, 128).transpose(2, 1, 0)).reshape(128, -1)
        m["stpool"] = np.ascontiguousarray(spl[s0:s1].reshape(NS, 15, 4, 128).transpose(3, 2, 0, 1)).reshape(128, -1)
        m["stffn"] = np.ascontiguousarray(sff[s0:s1].reshape(NS, 2, 48, 128).transpose(3, 2, 0, 1)).reshape(128, -1)
        in_maps.append(m)

    if "nc" not in _NC_CACHE:
        _NC_CACHE["nc"] = build_program()
    nc = _NC_CACHE["nc"]
    res = run_bass_kernel_spmd(nc, in_maps, core_ids=list(range(NCORES)))
    R = res.results

    y_prompt = np.empty((8, SEQ, D), np.float32)
    y_sample = np.empty((128, LS, D), np.float32)
    p_lc = np.empty((1, 8, 3, 512), np.float32)
    p_h = np.empty((1, 8, 512), np.float32)
    p_pool = np.empty((1, 8, 15, 512), np.float32)
    p_ffn = np.empty((1, 8, 2, 6144), np.float32)
    s_lc = np.empty((1, 128, 3, 512), np.float32)
    s_h = np.empty((1, 128, 512), np.float32)
    s_pool = np.empty((1, 128, 15, 512), np.float32)
    s_ffn = np.empty((1, 128, 2, 6144), np.float32)
    for c in range(NCORES):
        s0, s1 = c * NS, (c + 1) * NS
        y = np.asarray(R[c]["y"], np.float32)
        y_prompt[c] = y[:SEQ]
        y_sample[s0:s1] = y[SEQ:].reshape(NS, LS, D)
        olc = np.asarray(R[c]["o_lc"], np.float32).reshape(128, 4, 17, 3).transpose(2, 3, 1, 0).reshape(17, 3, 512)
        ohh = np.asarray(R[c]["o_h"], np.float32).reshape(128, 4, 17).transpose(2, 1, 0).reshape(17, 512)
        opl = np.asarray(R[c]["o_pool"], np.float32).reshape(128, 4, 17, 15).transpose(2, 3, 1, 0).reshape(17, 15, 512)
        off = np.asarray(R[c]["o_ffn"], np.float32).reshape(128, 48, 17, 2).transpose(2, 3, 1, 0).reshape(17, 2, 6144)
        p_lc[0, c], s_lc[0, s0:s1] = olc[0], olc[1:]
        p_h[0, c], s_h[0, s0:s1] = ohh[0], ohh[1:]
        p_pool[0, c], s_pool[0, s0:s1] = opl[0], opl[1:]
        p_ffn[0, c], s_ffn[0, s0:s1] = off[0], off[1:]
    return (y_prompt, y_sample, p_lc, p_h, p_pool, p_ffn, s_lc, s_h, s_pool, s_ffn)
```

```python
from contextlib import ExitStack

import numpy as np
import concourse.bass as bass
import concourse.mybir as mybir
from concourse.bass_utils import run_bass_kernel_spmd

F32 = mybir.dt.float32
BF16 = mybir.dt.bfloat16
AF = mybir.ActivationFunctionType
ALU = mybir.AluOpType

NCORES = 8
D = 1024
SEQ = 2048
NS = 16
LS = 8
DFF = 3072
EPS = 1e-6
POOL_W = (2, 4, 8, 16)

V_CW, V_CB, V_BA, V_BX, V_LAM, V_PS, V_FCW, V_FCB, V_G1, NV = 0, 16, 20, 24, 28, 32, 36, 180, 228, 236

GROUPS = [(0, 768, False), (768, 768, False), (1536, 512, True)]
GMAX = 768
ARENA_WORDS = 33920


class T:
    def __init__(self, ap, excl=False):
        self.ap = ap
        self.lastw = []
        self.readers = []
        self.excl = excl


class Q:
    def __init__(self, name, sem_key):
        self.name = name
        self.sem_key = sem_key
        self.cnt = 0
        self.waited = {}
        self.ops = []


class Prog:
    def __init__(self):
        self.q = {n: Q(n, "s_" + n) for n in ("pe", "act", "dve", "pool", "sp")}
        self.dma_cnt = {}
        self.ring = {}

    def _wait(self, q, deps):
        for (k, v) in deps:
            if q.name == "pe" and k == q.sem_key:
                continue
            if q.waited.get(k, 0) < v:
                q.ops.append(("wait", k, v))
                q.waited[k] = v

    @staticmethod
    def _deps(reads, writes, extra):
        deps = set(extra)
        for b in reads:
            deps.update(b.lastw)
            if b.excl:
                deps.update(b.readers)
        for b in writes:
            deps.update(b.lastw)
            deps.update(b.readers)
        return deps

    @staticmethod
    def _update(ev, reads, writes):
        for b in reads:
            b.readers.append(ev)
        for b in writes:
            b.lastw = [ev]
            b.readers = []

    def op(self, eng, fn, reads=(), writes=(), extra=()):
        q = self.q[eng]
        self._wait(q, self._deps(reads, writes, extra))
        q.cnt += 1
        ev = (q.sem_key, q.cnt)
        q.ops.append(("op", fn))
        self._update(ev, reads, writes)
        return ev

    NRING = 12

    def dma(self, eng, dsem, out, in_, reads=(), writes=(), extra=()):
        q = self.q[eng]
        i = self.ring.get(eng, 0)
        self.ring[eng] = i + 1
        dsem = "r_%s_%d" % (eng, i % self.NRING)
        prev = self.dma_cnt.get(dsem, 0)
        if prev:
            self._wait(q, [(dsem, prev)])
        self._wait(q, self._deps(reads, writes, extra))
        self.dma_cnt[dsem] = self.dma_cnt.get(dsem, 0) + 16
        ev = (dsem, self.dma_cnt[dsem])
        q.ops.append(("dma", out, in_, dsem))
        self._update(ev, reads, writes)
        return ev

    def barrier(self):
        evs = [(q.sem_key, q.cnt) for q in self.q.values() if q.cnt > 0]
        for q in self.q.values():
            self._wait(q, evs)

    def final_wait(self, eng):
        evs = [(q.sem_key, q.cnt) for q in self.q.values() if q.cnt > 0]
        evs += [(k, v) for k, v in self.dma_cnt.items()]
        self._wait(self.q[eng], evs)


def build_program():
    nc = bass.Bass("TRN2", target_bir_lowering=False)
    P = Prog()

    def din(name, shape):
        return nc.dram_tensor(name, list(shape), F32, kind="ExternalInput").ap()

    def dout(name, shape):
        return nc.dram_tensor(name, list(shape), F32, kind="ExternalOutput").ap()

    xp = din("xp", [SEQ, D])
    xs = din("xs", [NS * LS, D])
    vecs_d = din("vecs", [128, NV])
    g1r_d = din("g1r", [128, D])
    g2r_d = din("g2r", [128, D])
    gfr_d = din("gfr", [128, D])
    ident_d = din("ident", [128, 128])
    w_in_d = din("w_in", [D, 1536])
    w_out_d = din("w_out", [D, D])
    up_d = din("ffn_up", [D, 2 * DFF])
    down_d = din("ffn_down", [DFF, D])
    wa_d = din("wa", [8, 64, 64])
    wx_d = din("wx", [8, 64, 64])
    pw_d = din("pw", [4, 128, 128])
    stlc_d = din("stlc", [128, 4 * NS * 3])
    sth_d = din("sth", [128, 4 * NS])
    stpool_d = din("stpool", [128, 4 * NS * 15])
    stffn_d = din("stffn", [128, 48 * NS * 2])

    y_d = dout("y", [SEQ + NS * LS, D])
    olc_d = dout("o_lc", [128, 4 * 17 * 3])
    oh_d = dout("o_h", [128, 4 * 17])
    opool_d = dout("o_pool", [128, 4 * 17 * 15])
    offn_d = dout("o_ffn", [128, 48 * 17 * 2])

    w_in_v = w_in_d.rearrange("(kc p) n -> p kc n", p=128)
    w_out_v = w_out_d.rearrange("(kc p) n -> p kc n", p=128)
    up_v = up_d.rearrange("(kc p) n -> p kc n", p=128)
    down_v = down_d.rearrange("(c p) n -> p c n", p=128)

    es = ExitStack()
    with es:
        def sb(name, shape, dt=F32):
            return es.enter_context(nc.sbuf_tensor("sb_" + name, list(shape), dt))

        sems = {}
        for k in ("s_pe", "s_act", "s_dve", "s_pool", "s_sp"):
            sems[k] = es.enter_context(nc.semaphore(k))
        for eng_ in ("sp", "pool"):
            for i_ in range(Prog.NRING):
                k = "r_%s_%d" % (eng_, i_)
                sems[k] = es.enter_context(nc.semaphore(k))

        X_t = sb("X", [128, 6, D])
        X = [T(X_t[:, i, :]) for i in range(6)]
        h2T_t = sb("h2T", [128, 8, 2 + GMAX], BF16)
        h2T_hist = T(h2T_t[:, :, 0:2])
        h2Tb = [T(h2T_t[:, :, 2 + i * 128: 2 + (i + 1) * 128]) for i in range(6)]
        grep = T(sb("grep", [128, D])[:, :])
        vecs_t = sb("vecs", [128, NV])
        vecs = T(vecs_t[:, :])
        der_t = sb("der", [128, 16])
        der = T(der_t[:, :])
        ident_t = sb("ident", [128, 128], BF16)
        ident = T(ident_t[:, :])
        identf_t = sb("identf", [128, 128], F32)
        identf = T(identf_t[:, :])
        olc_t = sb("olc", [128, 4, 17, 3])
        oh_t = sb("oh", [128, 4, 17])
        opool_t = sb("opool", [128, 4, 17, 15])
        offn_t = sb("offn", [128, 48, 17, 2])
        olc, oh, opool, offn = T(olc_t), T(oh_t), T(opool_t), T(offn_t)
        zxs_t = sb("zxs", [128, 4, NS, 3 + LS])
        zxs = [T(zxs_t[:, c, :, :]) for c in range(4)]
        zps_t = sb("zps", [128, 4, NS, 15 + LS], BF16)
        zps = [T(zps_t[:, g, :, :]) for g in range(4)]
        sth_t = sb("sth", [128, 4, NS])
        stffn_t = sb("stffn", [128, 48, NS, 2])
        sth, stffn = T(sth_t), T(stffn_t)
        stats_t = sb("stats", [128, 192])
        h1hist_t = sb("h1hist", [128, 8, 4], BF16)
        h1hist = T(h1hist_t)
        histx_t = sb("histx", [128, 4, 3])
        histx = [T(histx_t[:, c, :]) for c in range(4)]
        histp_t = sb("histp", [128, 4, 15], BF16)
        histp = [T(histp_t[:, g, :]) for g in range(4)]
        dvec_t = sb("dvec", [128, 4, 16])
        dvec = T(dvec_t)
        ones16_t = sb("ones16", [128, 16])
        ones16 = T(ones16_t)
        wa_bd = sb("wa_bd", [128, 4, 128], BF16)
        wx_bd = sb("wx_bd", [128, 4, 128], BF16)
        wab_T = T(wa_bd)
        wxb_T = T(wx_bd)
        pwb = sb("pwb", [128, 4, 128], BF16)
        pws = sb("pws", [128, 4, 128], BF16)
        pwn = sb("pwn", [128, 4, 128], BF16)
        pwb_T, pws_T, pwn_T = T(pwb), T(pws), T(pwn)
        arena_t = sb("arena", [128, ARENA_WORDS])
        stpool_t = arena_t[:, 31760:31760 + 4 * NS * 15].rearrange("p (a b c) -> p a b c", a=4, b=NS)
        stlc_t = arena_t[:, 20548:20548 + 4 * NS * 3].rearrange("p (a b c) -> p a b c", a=4, b=NS)
        stlc, stpool = T(stlc_t), T(stpool_t)

        ps_t = es.enter_context(nc.psum_tensor("ps", [128, 8, 512], F32))
        banks = [T(ps_t[:, i, :], excl=True) for i in range(8)]
        bank_rr = [0]

        def newbank():
            b = banks[bank_rr[0] % 8]
            bank_rr[0] += 1
            return b

        def newbank_pair():
            if bank_rr[0] % 2:
                bank_rr[0] += 1
            i = bank_rr[0] % 8
            bank_rr[0] += 2
            return i, banks[i], banks[i + 1]

        stat_i = [0]

        def newstat():
            i = stat_i[0]
            stat_i[0] += 1
            return T(stats_t[:, i:i + 1])

        class Carver:
            def __init__(self):
                self.off = 0

            def f32(self, shape):
                n = int(np.prod(shape[1:]))
                a = arena_t[:, self.off:self.off + n]
                self.off += n
                assert self.off <= ARENA_WORDS, self.off
                return self._shape(a, shape)

            def bf16(self, shape):
                n = int(np.prod(shape[1:]))
                nw = (n + 1) // 2
                a = arena_t[:, self.off:self.off + nw].bitcast(BF16)[:, 0:n]
                self.off += nw
                assert self.off <= ARENA_WORDS, self.off
                return self._shape(a, shape)

            @staticmethod
            def _shape(a, shape):
                if len(shape) == 2:
                    return a
                if len(shape) == 3:
                    return a.rearrange("p (a b) -> p a b", a=shape[1])
                if len(shape) == 4:
                    return a.rearrange("p (a b c) -> p a b c", a=shape[1], b=shape[2])
                if len(shape) == 5:
                    return a.rearrange("p (a b c d) -> p a b c d", a=shape[1], b=shape[2], c=shape[3])
                raise ValueError(shape)

        DZ0, DZ1 = 21504, 31760
        cm = Carver()
        cm.off = DZ0
        w_in_bf = cm.bf16([128, 8, 1536])
        w_in_T = [T(w_in_bf[:, :, 0:512]), T(w_in_bf[:, :, 512:1024]), T(w_in_bf[:, :, 1024:1536])]
        HX = 3
        h1T_a = cm.bf16([128, 8, HX + GMAX + 1])
        h1T_hist = T(h1T_a[:, :, 0:HX])
        h1Tb = [T(h1T_a[:, :, HX + i * 128: HX + (i + 1) * 128]) for i in range(6)]
        xn1 = [T(cm.f32([128, D])) for _ in range(1)]
        assert cm.off <= DZ1, cm.off
        cm.off = 0
        w_out_bf = cm.bf16([128, 8, D])
        w_out_T = T(w_out_bf)
        mixedT_a = cm.bf16([128, 8, GMAX])
        mixedTb = [T(mixedT_a[:, :, i * 128:(i + 1) * 128]) for i in range(6)]
        LSEG = 384
        lru_sets = []
        for s_ in range(4):
            d = {}
            d["ext"] = T(cm.f32([128, 1, 3 + LSEG]))
            d["acc"] = T(cm.f32([128, LSEG]))
            d["xcb"] = T(cm.bf16([128, LSEG]))
            d["tr"] = T(cm.f32([128, LSEG]))
            d["ti"] = T(cm.f32([128, LSEG]))
            d["a"] = T(cm.f32([128, LSEG]))
            d["a2"] = T(cm.f32([128, LSEG]))
            lru_sets.append(d)
        gz_one = [T(cm.f32([128, LSEG])) for _ in range(4)]
        gz2 = [gz_one, gz_one]
        zpe = [T(cm.bf16([128, 1, 15 + LSEG])) for _ in range(4)]
        xn_m = [T(cm.bf16([128, D])) for _ in range(3)]
        fixS = T(cm.f32([128, 16]))
        fixSd = T(cm.bf16([128, 16]))
        assert cm.off <= DZ0, cm.off
        mixer_words = DZ1

        cf = Carver()
        actT_a = cf.bf16([128, 24, GMAX])
        actTb = [T(actT_a[:, :, i * 128:(i + 1) * 128]) for i in range(6)]
        wd_bf = cf.bf16([128, 24, D])
        wd_T = [T(wd_bf[:, i * 2:(i + 1) * 2, :]) for i in range(12)]
        NST = 3
        upb = [cf.bf16([128, 2, 8, 256]) for _ in range(NST)]
        upb_T = [(T(u[:, 0, :, :]), T(u[:, 1, :, :])) for u in upb]
        NACC = 2
        accA = [cf.f32([128, 2, 2, 384]) for _ in range(NACC)]
        accT = [(T(a[:, 0, :, :]), T(a[:, 1, :, :])) for a in accA]
        gbuf = [T(cf.f32([128, 2, 384])) for _ in range(NACC)]
        acc_rr = [0]
        uexts = [T(cf.f32([128, 2, NS, 2 + LS])) for _ in range(2)]
        uext_halves = [(T(u.ap[:, 0, :, :]), T(u.ap[:, 1, :, :])) for u in uexts]
        ybuf = [T(cf.f32([128, D])) for _ in range(1)]
        ffn_words = cf.off
        assert max(mixer_words, ffn_words) <= ARENA_WORDS

        def act_fn(out, in_, func, **kw):
            return lambda e: e.activation(out=out, in_=in_, func=func, **kw)

        def vcol(off, c):
            return vecs_t[:, off + c: off + c + 1]

        def dcol(off, c):
            return der_t[:, off + c: off + c + 1]

        D_HBA, D_HBX, D_HC, D_C2 = 0, 4, 8, 12

        P.dma("sp", "d_misc", vecs_t[:, :], vecs_d[:, :], writes=[vecs])
        P.dma("sp", "d_misc", grep.ap, g1r_d[:, :], writes=[grep])
        P.dma("sp", "d_x", X[0].ap, xp[0:128, :], writes=[X[0]])
        P.dma("pool", "d_w", ident_t[:, :], ident_d[:, :], writes=[ident])
        P.dma("pool", "d_w", pwb[:, :, :], pw_d.rearrange("g i j -> i g j"), writes=[pwb_T])
        for part in (0, 2, 1):
            P.dma("pool", "d_w", w_in_bf[:, :, part * 512:(part + 1) * 512], w_in_v[:, :, part * 512:(part + 1) * 512],
                  writes=[w_in_T[part]])
        P.dma("sp", "d_misc", identf_t[:, :], ident_d[:, :], writes=[identf])
        P.dma("sp", "d_misc", stlc_t[:, :, :, :].rearrange("p a b c -> p (a b c)"), stlc_d[:, :], writes=[stlc])
        P.dma("sp", "d_misc", sth_t[:, :, :].rearrange("p a b -> p (a b)"), sth_d[:, :], writes=[sth])
        P.dma("sp", "d_misc", stpool_t[:, :, :, :].rearrange("p a b c -> p (a b c)"), stpool_d[:, :], writes=[stpool])
        P.dma("sp", "d_misc", stffn_t[:, :, :, :].rearrange("p a b c -> p (a b c)"), stffn_d[:, :], writes=[stffn])

        P.op("dve", lambda e: e.tensor_scalar(out=der_t[:, 0:8], in0=vecs_t[:, V_BA:V_BA + 8], scalar1=0.5,
                                              scalar2=None, op0=ALU.mult), reads=[vecs], writes=[der])
        tmp_e4 = T(stats_t[:, 176:180])
        tmp_s4 = T(stats_t[:, 180:184])
        P.op("act", act_fn(stats_t[:, 176:180], vecs_t[:, V_LAM:V_LAM + 4], AF.Exp, scale=-1.0),
             reads=[vecs], writes=[tmp_e4])
        P.op("act", act_fn(stats_t[:, 180:184], stats_t[:, 176:180], AF.Ln, bias=1.0),
             reads=[tmp_e4], writes=[tmp_s4])
        P.op("dve", lambda e: e.tensor_scalar(out=der_t[:, 8:12], in0=stats_t[:, 180:184], scalar1=-4.0,
                                              scalar2=None, op0=ALU.mult), reads=[tmp_s4], writes=[der])
        P.op("dve", lambda e: e.tensor_scalar(out=der_t[:, 12:16], in0=stats_t[:, 180:184], scalar1=-8.0,
                                              scalar2=None, op0=ALU.mult), reads=[tmp_s4], writes=[der])
        P.op("dve", lambda e: e.memset(histx_t[:, :, :], 0.0), writes=histx)
        P.op("dve", lambda e: e.memset(histp_t[:, :, :], 0.0), writes=histp)
        P.op("dve", lambda e: e.memset(ones16_t[:, :], 1.0), writes=[ones16])
        P.op("dve", lambda e: e.memset(dvec_t[:, :, :], 0.0), writes=[dvec])
        for g, w in enumerate(POOL_W):
            for t in range(w - 1):
                val = 1.0 / (t + 1) - 1.0 / w
                P.op("dve", lambda e, g=g, t=t, val=val: e.memset(dvec_t[:, g, t:t + 1], val), writes=[dvec])
        P.op("dve", lambda e: e.memset(h2T_t[:, :, 0:2], 0.0), writes=[h2T_hist])
        for c in range(4):
            P.op("dve", lambda e, c=c: e.tensor_copy(out=zxs_t[:, c, :, 0:3], in_=stlc_t[:, c, :, :]),
                 reads=[stlc], writes=[zxs[c]])
            P.op("dve", lambda e, c=c: e.tensor_copy(out=zps_t[:, c, :, 0:15], in_=stpool_t[:, c, :, :]),
                 reads=[stpool], writes=[zps[c]])
            P.op("dve", lambda e, c=c: e.tensor_copy(out=opool_t[:, c, 1:17, 0:7], in_=stpool_t[:, c, :, 8:15]),
                 reads=[stpool], writes=[opool])

        P.op("dve", lambda e: e.memset(wa_bd[:, :, :], 0.0), writes=[wab_T])
        P.op("dve", lambda e: e.memset(wx_bd[:, :, :], 0.0), writes=[wxb_T])
        for h in range(8):
            r0 = (h % 2) * 64
            P.dma("pool", "d_w", wa_bd[r0:r0 + 64, h // 2, r0:r0 + 64], wa_d[h, :, :], writes=[wab_T])
            P.dma("pool", "d_w", wx_bd[r0:r0 + 64, h // 2, r0:r0 + 64], wx_d[h, :, :], writes=[wxb_T])
        def scale_pool_weights():
            for g, w in enumerate(POOL_W):
                P.op("dve", lambda e, g=g, w=w: e.tensor_scalar(out=pws[:, g, :], in0=pwb[:, g, :], scalar1=1.0 / w,
                                                                scalar2=None, op0=ALU.mult), reads=[pwb_T], writes=[pws_T])
                P.op("dve", lambda e, g=g: e.tensor_scalar(out=pwn[:, g, :], in0=pwb[:, g, :], scalar1=-1.0,
                                                           scalar2=None, op0=ALU.mult), reads=[pwb_T], writes=[pwn_T])

        def load_x(gi):
            p0, npt, has_s = GROUPS[gi]
            nt = npt // 128
            for t in range(1 if gi == 0 else 0, nt):
                P.dma("sp", "d_x", X[t].ap, xp[p0 + t * 128: p0 + (t + 1) * 128, :], writes=[X[t]])
            if has_s:
                P.dma("sp", "d_x", X[nt].ap, xs[:, :], writes=[X[nt]])

        def norm_A(xt, grep_T, xn):
            ssq, sq, rs = newstat(), newstat(), newstat()
            P.op("act", act_fn(xn.ap, xt.ap, AF.Square, accum_out=ssq.ap), reads=[xt], writes=[xn, ssq])
            P.op("act", act_fn(sq.ap, ssq.ap, AF.Sqrt, scale=1.0 / D, bias=EPS), reads=[ssq], writes=[sq])
            P.op("dve", lambda e: e.reciprocal(out=rs.ap, in_=sq.ap), reads=[sq], writes=[rs])
            P.op("dve", lambda e: e.scalar_tensor_tensor(out=xn.ap, in0=xt.ap, scalar=rs.ap, in1=grep_T.ap,
                                                         op0=ALU.mult, op1=ALU.mult),
                 reads=[xt, rs, grep_T], writes=[xn])

        def norm1_A(t, extra=()):
            xt, xn = X[t], xn1[0]
            ssq, sq, rs = newstat(), newstat(), newstat()
            P.op("act", act_fn(xn.ap, xt.ap, AF.Square, accum_out=ssq.ap), reads=[xt], writes=[xn, ssq], extra=extra)
            P.op("act", act_fn(sq.ap, ssq.ap, AF.Sqrt, scale=1.0 / D, bias=EPS), reads=[ssq], writes=[sq])
            P.op("dve", lambda e: e.reciprocal(out=rs.ap, in_=sq.ap), reads=[sq], writes=[rs], extra=extra)
            P.op("act", act_fn(xn.ap, xt.ap, AF.Identity, scale=rs.ap), reads=[xt, rs], writes=[xn])

        def norm1_B(t, extra=()):
            xn = xn1[0]
            for half in range(2):
                bk = newbank()

                def tr(e, half=half, bk=bk):
                    ins = None
                    for j in range(4):
                        kc = half * 4 + j
                        ins = e.transpose(out=bk.ap[:, j * 128:(j + 1) * 128], in_=xn.ap[:, kc * 128:(kc + 1) * 128],
                                          identity=identf_t[:, :])
                    return ins
                P.op("pe", tr, reads=[xn, identf], writes=[bk], extra=extra)
                for j in range(4):
                    kc = half * 4 + j
                    P.op("act", act_fn(h1T_a[:, kc, HX + t * 128: HX + (t + 1) * 128], bk.ap[:, j * 128:(j + 1) * 128], AF.Identity,
                                       scale=vcol(V_G1, kc)), reads=[bk, vecs], writes=[h1Tb[t]])

        def norm_B(xn, dstT_blocks_ap, dst_T):
            bk = newbank()
            bkb = bk.ap[:, :].bitcast(BF16)

            def tr(e):
                ins = None
                for kc in range(8):
                    ins = e.transpose(out=bkb[:, kc * 128:(kc + 1) * 128], in_=xn.ap[:, kc * 128:(kc + 1) * 128],
                                      identity=ident_t[:, :])
                return ins
            P.op("pe", tr, reads=[xn, ident], writes=[bk])
            P.op("act", act_fn(dstT_blocks_ap, bkb.rearrange("p (k n) -> p k n", k=8), AF.Copy),
                 reads=[bk], writes=[dst_T])

        seg_ctr = [0]

        def mixer_segment(gi, col0, L, S, Ls, tiles, kind, first_seq, last_prompt):
            is_s = kind == "s"
            gz = gz2[seg_ctr[0] % 2]
            seg_ctr[0] += 1

            def as3(ap2):
                return ap2.rearrange("p (s l) -> p s l", s=S)

            h1 = [h1Tb[t] for t in tiles]
            mT = [mixedTb[t] for t in tiles]

            H = 0 if (is_s or first_seq) else HX

            def w_in_mm(m, hist=0):
                bk = newbank()

                def mm(e, m=m, bk=bk):
                    ins = None
                    for kc in range(8):
                        ins = e.matmul(bk.ap[:, 0:L + hist], lhsT=w_in_bf[:, kc, m * 128:(m + 1) * 128],
                                       rhs=h1T_a[:, kc, HX + col0 - hist: HX + col0 + L], start=(kc == 0), stop=(kc == 7))
                    return ins
                rd = h1 + [w_in_T[m // 4]]
                if hist:
                    rd.append(h1T_hist if col0 == 0 else h1Tb[col0 // 128 - 1])
                P.op("pe", mm, reads=rd, writes=[bk])
                return bk

            ext3 = {}
            extT = {}

            def fronta():
              if is_s:
                  for c in range(4):
                      bk = w_in_mm(c)
                      P.op("dve", lambda e, c=c, bk=bk: e.tensor_copy(out=zxs_t[:, c, :, 3:3 + LS], in_=as3(bk.ap[:, 0:L])),
                           reads=[bk], writes=[zxs[c]])
                      ext3[c], extT[c] = zxs_t[:, c, :, :], zxs[c]
              else:
                  zb = {}
                  for c in range(4):
                      bk = w_in_mm(c, hist=H)
                      zb[c] = bk
                      st = lru_sets[c]
                      P.op("act", act_fn(st["acc"].ap[:, 0:L], bk.ap[:, H:H + L], AF.Identity, scale=vcol(V_CW, 3 * 4 + c),
                                         bias=vcol(V_CB, c)), reads=[bk, vecs], writes=[st["acc"]])
                      if last_prompt:
                          P.op("act", act_fn(olc_t[:, c, 0, :], bk.ap[:, H + L - 3:H + L], AF.Copy), reads=[bk], writes=[olc])
                  for j in (2, 1, 0):
                      sh = 3 - j
                      for c in range(4):
                          st = lru_sets[c]
                          bk = zb[c]
                          if H:
                              o_, i_ = st["acc"].ap[:, 0:L], bk.ap[:, H - sh:H - sh + L]
                          else:
                              o_, i_ = st["acc"].ap[:, sh:L], bk.ap[:, 0:L - sh]
                          P.op("dve", lambda e, o_=o_, i_=i_, j=j, c=c: e.scalar_tensor_tensor(
                              out=o_, in0=i_, scalar=vcol(V_CW, j * 4 + c), in1=o_, op0=ALU.mult, op1=ALU.add),
                              reads=[bk, vecs], writes=[st["acc"]])
              for g, w in enumerate(POOL_W):
                  bk = w_in_mm(8 + g)
                  if is_s:
                      P.op("act", act_fn(zps_t[:, g, :, 15:15 + LS], as3(bk.ap[:, 0:L]), AF.Copy), reads=[bk], writes=[zps[g]])
                      P.op("act", act_fn(opool_t[:, g, 1:17, 7:15], as3(bk.ap[:, 0:L]), AF.Copy), reads=[bk], writes=[opool])
                  else:
                      P.op("pool", lambda e, g=g: e.tensor_copy(out=zpe[g].ap[:, 0, 0:15], in_=histp_t[:, g, :]),
                           reads=[histp[g]], writes=[zpe[g]])
                      P.op("act", act_fn(zpe[g].ap[:, :, 15:15 + L], as3(bk.ap[:, 0:L]), AF.Copy), reads=[bk], writes=[zpe[g]])
                      P.op("pool", lambda e, g=g: e.tensor_copy(out=histp_t[:, g, :], in_=zpe[g].ap[:, 0, L:L + 15]),
                           reads=[zpe[g]], writes=[histp[g]])
                      if last_prompt:
                          P.op("act", act_fn(opool_t[:, g, 0, :], bk.ap[:, L - 15:L], AF.Copy), reads=[bk], writes=[opool])
            acc3 = {c: lru_sets[c]["acc"].ap[:, 0:L].rearrange("p (s l) -> p s l", s=S) for c in range(4)}

            def stageC(cs):
                if not is_s:
                    return
                for c in cs:
                    st = lru_sets[c]
                    P.op("dve", lambda e, c=c: e.tensor_scalar(out=acc3[c], in0=ext3[c][:, :, 3:3 + Ls],
                                                               scalar1=vcol(V_CW, 3 * 4 + c), scalar2=vcol(V_CB, c),
                                                               op0=ALU.mult, op1=ALU.add),
                         reads=[extT[c], vecs], writes=[st["acc"]])
                for j in (2, 1, 0):
                    for c in cs:
                        st = lru_sets[c]
                        P.op("dve", lambda e, j=j, c=c: e.scalar_tensor_tensor(out=acc3[c], in0=ext3[c][:, :, j:j + Ls],
                                                                               scalar=vcol(V_CW, j * 4 + c), in1=acc3[c],
                                                                               op0=ALU.mult, op1=ALU.add),
                             reads=[extT[c], vecs], writes=[st["acc"]])
                if is_s:
                    for c in cs:
                        P.op("pool", lambda e, c=c: e.tensor_copy(out=olc_t[:, c, 1:17, :], in_=zxs_t[:, c, :, LS:LS + 3]),
                             reads=[zxs[c]], writes=[olc])

            def stageD(cs):
                banks_ax = {}
                for c in cs:
                    st = lru_sets[c]
                    P.op("dve", lambda e, st=st: e.tensor_copy(out=st["xcb"].ap[:, 0:L], in_=st["acc"].ap[:, 0:L]),
                         reads=[st["acc"]], writes=[st["xcb"]])
                    bka, bkx = newbank(), newbank()
                    P.op("pe", lambda e, st=st, c=c, bka=bka: e.matmul(bka.ap[:, 0:L], lhsT=wa_bd[:, c, :], rhs=st["xcb"].ap[:, 0:L],
                                                                       start=True, stop=True),
                         reads=[st["xcb"], wab_T], writes=[bka])
                    P.op("pe", lambda e, st=st, c=c, bkx=bkx: e.matmul(bkx.ap[:, 0:L], lhsT=wx_bd[:, c, :], rhs=st["xcb"].ap[:, 0:L],
                                                                       start=True, stop=True),
                         reads=[st["xcb"], wxb_T], writes=[bkx])
                    banks_ax[c] = (bka, bkx)
                for c in cs:
                    st = lru_sets[c]
                    ba_, bx_ = banks_ax[c]
                    P.op("act", act_fn(st["ti"].ap[:, 0:L], bx_.ap[:, 0:L], AF.Tanh, scale=0.5, bias=dcol(D_HBX, c)),
                         reads=[bx_, der], writes=[st["ti"]])
                for c in cs:
                    st = lru_sets[c]
                    ba_, bx_ = banks_ax[c]
                    P.op("act", act_fn(st["tr"].ap[:, 0:L], ba_.ap[:, 0:L], AF.Tanh, scale=0.5, bias=dcol(D_HBA, c)),
                         reads=[ba_, der], writes=[st["tr"]])

            def stageD2(cs):
                for c in cs:
                    st = lru_sets[c]
                    P.op("act", act_fn(st["a"].ap[:, 0:L], st["tr"].ap[:, 0:L], AF.Exp, scale=dcol(D_HC, c), bias=dcol(D_HC, c)),
                         reads=[st["tr"], der], writes=[st["a"]])
                    P.op("pool", lambda e, st=st: e.tensor_tensor(out=st["a2"].ap[:, 0:L], in0=st["a"].ap[:, 0:L],
                                                                  in1=st["a"].ap[:, 0:L], op=ALU.mult),
                         reads=[st["a"]], writes=[st["a2"]])

            def stageE(cs):
                for c in cs:
                    st = lru_sets[c]
                    P.op("act", act_fn(st["a2"].ap[:, 0:L], st["a2"].ap[:, 0:L], AF.Sqrt, scale=-1.0, bias=1.0),
                         reads=[], writes=[st["a2"]])

            def stageF1(cs):
                for c in cs:
                    st = lru_sets[c]
                    P.op("dve", lambda e, st=st: e.scalar_tensor_tensor(out=st["ti"].ap[:, 0:L], in0=st["ti"].ap[:, 0:L], scalar=1.0,
                                                                        in1=st["acc"].ap[:, 0:L], op0=ALU.add, op1=ALU.mult),
                         reads=[st["acc"]], writes=[st["ti"]])

            def stageF(cs):
                for c in cs:
                    st = lru_sets[c]
                    P.op("dve", lambda e, st=st: e.scalar_tensor_tensor(out=st["ti"].ap[:, 0:L], in0=st["ti"].ap[:, 0:L], scalar=0.5,
                                                                        in1=st["a2"].ap[:, 0:L], op0=ALU.mult, op1=ALU.mult),
                         reads=[st["a2"]], writes=[st["ti"]])
                for c in cs:
                    st = lru_sets[c]
                    hb = st["tr"]
                    if is_s:
                        a3s = st["a"].ap[:, 0:L].rearrange("p (s l) -> p s l", s=NS)
                        b3s = st["ti"].ap[:, 0:L].rearrange("p (s l) -> p s l", s=NS)
                        P.op("dve", lambda e, a3s=a3s, c=c: e.tensor_tensor(out=fixS.ap, in0=a3s[:, :, 0], in1=sth_t[:, c, :],
                                                                           op=ALU.mult),
                             reads=[st["a"], sth], writes=[fixS])
                        P.op("dve", lambda e, b3s=b3s: e.tensor_tensor(out=b3s[:, :, 0], in0=b3s[:, :, 0], in1=fixS.ap, op=ALU.add),
                             reads=[fixS], writes=[st["ti"]])
                        P.op("dve", lambda e, a3s=a3s: e.memset(a3s[:, :, 0:1], 0.0), writes=[st["a"]])
                        P.op("dve", lambda e, st=st, hb=hb: e.tensor_tensor_scan(
                            out=hb.ap[:, 0:L], data0=st["a"].ap[:, 0:L], data1=st["ti"].ap[:, 0:L], initial=0.0,
                            op0=ALU.mult, op1=ALU.add),
                            reads=[st["a"], st["ti"]], writes=[hb])
                        P.op("dve", lambda e, hb=hb, c=c: e.tensor_copy(
                            out=oh_t[:, c, 1:17], in_=hb.ap[:, 0:L].rearrange("p (s l) -> p s l", s=NS)[:, :, LS - 1]),
                            reads=[hb], writes=[oh])
                    else:
                        init = 0.0 if first_seq else oh_t[:, c, 0:1]
                        P.op("dve", lambda e, st=st, hb=hb, init=init: e.tensor_tensor_scan(
                            out=hb.ap[:, 0:L], data0=st["a"].ap[:, 0:L], data1=st["ti"].ap[:, 0:L], initial=init,
                            op0=ALU.mult, op1=ALU.add),
                            reads=[st["a"], st["ti"], oh], writes=[hb])
                        P.op("dve", lambda e, hb=hb, c=c: e.tensor_copy(out=oh_t[:, c, 0:1], in_=hb.ap[:, L - 1:L]),
                             reads=[hb], writes=[oh])
                    P.op("pool", lambda e, hb=hb, c=c: e.tensor_tensor(out=mixedT_a[:, c, col0:col0 + L], in0=hb.ap[:, 0:L],
                                                                       in1=gz[c].ap[:, 0:L], op=ALU.mult),
                         reads=[hb, gz[c]], writes=mT)

            def frontb():
                for c in range(4):
                    bk = w_in_mm(4 + c)
                    P.op("act", act_fn(gz[c].ap[:, 0:L], bk.ap[:, 0:L], AF.Gelu_apprx_tanh), reads=[bk], writes=[gz[c]])

            def conv():
                stageC([0, 1])
                stageC([2, 3])

            def poolG():
              for g, w in enumerate(POOL_W):
                  bk = newbank()
                  src = zps_t[:, g, :, :] if is_s else zpe[g].ap
                  srcT = zps[g] if is_s else zpe[g]
                  do_fix = first_seq and not is_s
                  if do_fix:
                      P.op("dve", lambda e, g=g: e.tensor_tensor_scan(out=fixS.ap, data0=ones16_t[:, :],
                                                                      data1=zpe[g].ap[:, 0, 15:31], initial=0.0,
                                                                      op0=ALU.mult, op1=ALU.add),
                           reads=[zpe[g], ones16], writes=[fixS])
                      P.op("dve", lambda e, g=g: e.tensor_tensor(out=fixSd.ap, in0=fixS.ap, in1=dvec_t[:, g, :], op=ALU.mult),
                           reads=[fixS, dvec], writes=[fixSd])

                  def pm(e, g=g, w=w, bk=bk, src=src, do_fix=do_fix):
                      ins = None
                      out3 = as3(bk.ap[:, 0:L])
                      for k in range(w):
                          ins = e.matmul(out3, lhsT=pws[:, g, :], rhs=src[:, :, 15 - k:15 - k + Ls],
                                         start=(k == 0), stop=False)
                      ins = e.matmul(out3, lhsT=pwn[:, g, :], rhs=src[:, :, 15:15 + Ls], start=False, stop=(not do_fix))
                      if do_fix:
                          ins = e.matmul(bk.ap[:, 0:16], lhsT=pwb[:, g, :], rhs=fixSd.ap, start=False, stop=True)
                      return ins
                  rd = [srcT, pws_T, pwn_T] + ([fixSd, pwb_T] if do_fix else [])
                  P.op("pe", pm, reads=rd, writes=[bk])
                  P.op("act", act_fn(mixedT_a[:, 4 + g, col0:col0 + L], bk.ap[:, 0:L], AF.Identity, scale=vcol(V_PS, g)),
                       reads=[bk, vecs], writes=mT)
            def S12():
                stageD([0, 1, 2, 3])
                stageF1([0, 1, 2, 3])

            def S34():
                stageD2([0, 1, 2, 3])
                stageE([0, 1, 2, 3])
                stageF([0, 1])
                stageF([2, 3])

            return {"fronta": fronta, "frontb": frontb, "conv": conv, "poolG": poolG, "S12": S12, "S34": S34}

        def wout_A(t):
            for half in range(2):
                bk = newbank()

                def mm(e, half=half, bk=bk):
                    ins = None
                    for kc in range(8):
                        ins = e.matmul(bk.ap[:, :], lhsT=mixedT_a[:, kc, t * 128:(t + 1) * 128],
                                       rhs=w_out_bf[:, kc, half * 512:(half + 1) * 512], start=(kc == 0), stop=(kc == 7))
                    return ins
                P.op("pe", mm, reads=[mixedTb[t], w_out_T], writes=[bk])
                xs_ = X[t].ap[:, half * 512:(half + 1) * 512]
                P.op("dve", lambda e, bk=bk, xs_=xs_: e.tensor_tensor(out=xs_, in0=bk.ap[:, :], in1=xs_, op=ALU.add),
                     reads=[bk, X[t]], writes=[X[t]])
            norm_A(X[t], grep, xn_m[t % 3])

        def wout_B(t):
            norm_B(xn_m[t % 3], h2T_t[:, :, 2 + t * 128: 2 + (t + 1) * 128], h2Tb[t])

        tail_pending = []
        mult_evs = {}

        def flush_tail():
            while tail_pending:
                tail_pending.pop(0)()

        def up_pair(gi, pr, stg, j, segs_p, has_s, nt_p, first_group, last_group, on_last_mm=None):
            cg, cv = pr, 24 + pr
            L = segs_p[0][1]
            N = L + 2
            ig, bg0, bg1 = newbank_pair()
            iv, bv0, bv1 = newbank_pair()
            bks = {0: (bg0, bg1), 1: (bv0, bv1)}
            ib = {0: ig, 1: iv}
            all_tiles = list(range(0, (2 * L) // 128))
            for si, (c0, L_) in enumerate(segs_p):
                tiles = list(range(c0 // 128, (c0 + L) // 128))
                rd = [h2Tb[t] for t in tiles] + list(upb_T[stg])
                rd.append(h2T_hist if c0 == 0 else h2Tb[c0 // 128 - 1])
                for gv in (0, 1):
                    bk = bks[gv][si]

                    def mm(e, gv=gv, bk=bk, c0=c0, N=N, L=L):
                        ins = None
                        for kc in range(8):
                            ins = e.matmul(bk.ap[:, 0:N], lhsT=upb[stg][:, gv, kc, j * 128:(j + 1) * 128],
                                           rhs=h2T_t[:, kc, c0: 2 + c0 + L], start=(kc == 0), stop=(kc == 7))
                        return ins
                    P.op("pe", mm, reads=rd, writes=[bk])
            if on_last_mm is not None and not has_s:
                on_last_mm()
            k_ = acc_rr[0] % NACC
            acc_rr[0] += 1
            acc = accA[k_]
            aT = accT[k_]
            gb = gbuf[k_]
            for gv, ch in ((0, cg), (1, cv)):
                P.op("act", act_fn(acc[:, gv, :, 0:L], ps_t[:, ib[gv]:ib[gv] + 2, 2:2 + L], AF.Identity,
                                   scale=vcol(V_FCW, 2 * 48 + ch), bias=vcol(V_FCB, ch)),
                     reads=list(bks[gv]) + [vecs], writes=[aT[gv]])
            for tap, sh in ((1, 1), (0, 2)):
                for gv, ch in ((0, cg), (1, cv)):
                    o_ = acc[:, gv, :, 0:L]
                    i_ = ps_t[:, ib[gv]:ib[gv] + 2, 2 - sh:2 - sh + L]
                    P.op("dve", lambda e, o_=o_, i_=i_, tap=tap, ch=ch: e.scalar_tensor_tensor(
                        out=o_, in0=i_, scalar=vcol(V_FCW, tap * 48 + ch), in1=o_, op0=ALU.mult, op1=ALU.add),
                        reads=list(bks[gv]) + [vecs], writes=[aT[gv]])
            if last_group:
                for gv, ch in ((0, cg), (1, cv)):
                    bk = bks[gv][1]
                    P.op("dve", lambda e, bk=bk, ch=ch, N=N: e.tensor_copy(out=offn_t[:, ch, 0, :], in_=bk.ap[:, N - 2:N]),
                         reads=[bk], writes=[offn])
            flush_tail()

            def tail(gb=gb, acc=acc, aT=aT, L=L, all_tiles=all_tiles):
                P.op("act", act_fn(gb.ap[:, :, 0:L], acc[:, 0, :, 0:L], AF.Gelu_apprx_tanh), reads=[aT[0]], writes=[gb])
                mult_evs[pr] = P.op("dve", lambda e: e.tensor_tensor(
                    out=actT_a[:, pr, 0:2 * L].rearrange("p (s l) -> p s l", s=2), in0=gb.ap[:, :, 0:L],
                    in1=acc[:, 1, :, 0:L], op=ALU.mult),
                    reads=[gb, aT[1]], writes=[actTb[t] for t in all_tiles])
            tail_pending.append(tail)
            if has_s:
                c0 = nt_p * 128
                L = NS * LS
                t = nt_p
                bg, bv = newbank(), newbank()
                ue = uexts[pr % 2]
                k_ = acc_rr[0] % NACC
                acc_rr[0] += 1
                acc = accA[k_][:, :, 0, :]
                aT = accT[k_]
                gb = T(gbuf[k_].ap[:, 0, :])
                gb_full = gbuf[k_]
                for gv, bk in ((0, bg), (1, bv)):
                    def mm(e, gv=gv, bk=bk, c0=c0, L=L):
                        ins = None
                        for kc in range(8):
                            ins = e.matmul(bk.ap[:, 0:L], lhsT=upb[stg][:, gv, kc, j * 128:(j + 1) * 128],
                                           rhs=h2T_t[:, kc, 2 + c0: 2 + c0 + L], start=(kc == 0), stop=(kc == 7))
                        return ins
                    P.op("pe", mm, reads=[h2Tb[t]] + list(upb_T[stg]), writes=[bk])
                if on_last_mm is not None:
                    on_last_mm()
                ueT = uext_halves[pr % 2]
                for gv, bk, ch in ((0, bg, cg), (1, bv, cv)):
                    P.op("act", act_fn(ue.ap[:, gv, :, 0:2], stffn_t[:, ch, :, :], AF.Copy), reads=[stffn], writes=[ueT[gv]])
                    b3 = bk.ap[:, 0:L].rearrange("p (s l) -> p s l", s=NS)
                    P.op("act", act_fn(ue.ap[:, gv, :, 2:2 + LS], b3, AF.Copy), reads=[bk], writes=[ueT[gv]])
                for gv, bk, ch in ((0, bg, cg), (1, bv, cv)):
                    a3 = acc[:, gv, 0:L].rearrange("p (s l) -> p s l", s=NS)
                    if gv == 0:
                        b3 = bk.ap[:, 0:L].rearrange("p (s l) -> p s l", s=NS)
                        P.op("act", act_fn(a3, b3, AF.Identity, scale=vcol(V_FCW, 2 * 48 + ch), bias=vcol(V_FCB, ch)),
                             reads=[bk, vecs], writes=[aT[gv]])
                    else:
                        P.op("dve", lambda e, gv=gv, ch=ch, a3=a3: e.tensor_scalar(
                            out=a3, in0=ue.ap[:, gv, :, 2:2 + LS], scalar1=vcol(V_FCW, 2 * 48 + ch), scalar2=vcol(V_FCB, ch),
                            op0=ALU.mult, op1=ALU.add), reads=[ueT[gv], vecs], writes=[aT[gv]])
                for tap, sh in ((1, 1), (0, 2)):
                    for gv, bk, ch in ((0, bg, cg), (1, bv, cv)):
                        a3 = acc[:, gv, 0:L].rearrange("p (s l) -> p s l", s=NS)
                        P.op("dve", lambda e, gv=gv, ch=ch, a3=a3, tap=tap, sh=sh: e.scalar_tensor_tensor(
                            out=a3, in0=ue.ap[:, gv, :, 2 - sh:2 - sh + LS], scalar=vcol(V_FCW, tap * 48 + ch), in1=a3,
                            op0=ALU.mult, op1=ALU.add), reads=[ueT[gv], vecs], writes=[aT[gv]])
                for gv, bk, ch in ((0, bg, cg), (1, bv, cv)):
                    P.op("act", act_fn(offn_t[:, ch, 1:17, :], ue.ap[:, gv, :, LS:LS + 2], AF.Copy), reads=[ueT[gv]], writes=[offn])
                flush_tail()

                def tail(gb=gb, gb_full=gb_full, acc=acc, aT=aT, c0=c0, L=L, t=t):
                    P.op("act", act_fn(gb.ap[:, 0:L], acc[:, 0, 0:L], AF.Gelu_apprx_tanh), reads=[aT[0]], writes=[gb_full])
                    P.op("dve", lambda e: e.tensor_tensor(
                        out=actT_a[:, pr, c0:c0 + L], in0=gb.ap[:, 0:L], in1=acc[:, 1, 0:L], op=ALU.mult),
                        reads=[gb_full, aT[1]], writes=[actTb[t]])
                tail_pending.append(tail)

        def down_final(gi, t, yrow0):
            CS = 20
            split = (t == 0 and (CS - 1) in mult_evs)
            bks2 = [newbank(), newbank()]

            def mk(half, bk):
                def mm(e, c_lo=0, c_hi=24):
                    ins = None
                    for c in range(c_lo, c_hi):
                        ins = e.matmul(bk.ap[:, :], lhsT=actT_a[:, c, t * 128:(t + 1) * 128],
                                       rhs=wd_bf[:, c, half * 512:(half + 1) * 512], start=(c == 0), stop=(c == 23))
                    return ins
                return mm
            mms = [mk(0, bks2[0]), mk(1, bks2[1])]
            if split:
                for half in range(2):
                    ev_ = P.op("pe", lambda e, mm=mms[half]: mm(e, c_lo=0, c_hi=CS), reads=wd_T[:CS // 2],
                               writes=[bks2[half]], extra=[mult_evs[CS - 1]])
                    actTb[t].readers.append(ev_)
                for half in range(2):
                    P.op("pe", lambda e, mm=mms[half]: mm(e, c_lo=CS, c_hi=24), reads=[actTb[t]] + wd_T[CS // 2:],
                         writes=[bks2[half]])
            else:
                for half in range(2):
                    P.op("pe", mms[half], reads=[actTb[t]] + wd_T, writes=[bks2[half]])
            for half in range(2):
                bk = bks2[half]
                xs_ = X[t].ap[:, half * 512:(half + 1) * 512]
                P.op("dve", lambda e, bk=bk, xs_=xs_: e.tensor_tensor(out=xs_, in0=bk.ap[:, :], in1=xs_, op=ALU.add),
                     reads=[bk, X[t]], writes=[X[t]])
            ssq, sq, rs = newstat(), newstat(), newstat()
            yb = ybuf[0]
            P.op("act", act_fn(yb.ap, X[t].ap, AF.Square, accum_out=ssq.ap), reads=[X[t]], writes=[yb, ssq])
            P.op("act", act_fn(sq.ap, ssq.ap, AF.Sqrt, scale=1.0 / D, bias=EPS), reads=[ssq], writes=[sq])
            P.op("dve", lambda e: e.reciprocal(out=rs.ap, in_=sq.ap), reads=[sq], writes=[rs])
            P.op("dve", lambda e: e.scalar_tensor_tensor(out=yb.ap, in0=X[t].ap, scalar=rs.ap, in1=grep.ap,
                                                         op0=ALU.mult, op1=ALU.mult),
                 reads=[X[t], rs, grep], writes=[yb])
            P.dma("sp", "d_y", y_d[yrow0:yrow0 + 128, :], yb.ap, reads=[yb])

        deferred_norm1 = []
        load_x(0)
        for gi, (p0, npt, has_s) in enumerate(GROUPS):
            nt_p = npt // 128
            nt = nt_p + (1 if has_s else 0)
            first_group = gi == 0
            last_group = gi == len(GROUPS) - 1
            if gi > 0:
                P.barrier()
            else:
                def p1_tiles(t0, t1):
                    for t in range(t0, t1 + 1):
                        if t < t1:
                            norm_A(X[t], grep, xn_m[t % 2])
                        if t >= t0 + 1:
                            norm_B(xn_m[(t - 1) % 2], h1T_a[:, :, HX + (t - 1) * 128: HX + t * 128], h1Tb[t - 1])
                p1_tiles(0, 3)
                deferred_norm1.append(lambda nt=nt: p1_tiles(3, nt))
            if gi > 0:
                P.op("pool", lambda e: e.tensor_copy(out=h1T_a[:, :, 0:HX], in_=h1hist_t[:, :, 0:HX]),
                     reads=[h1hist], writes=[h1T_hist])
            P.dma("pool", "d_w", w_out_bf[:, :, :], w_out_v[:, :, :], writes=[w_out_T])
            if gi == 0:
                scale_pool_weights()
            segs = []
            c0 = 0
            while c0 < npt:
                L = min(384 if npt % 384 == 0 else 256, npt - c0)
                segs.append((c0, L))
                c0 += L
            sg = []
            for si, (c0, L) in enumerate(segs):
                sg.append(mixer_segment(gi, c0, L, 1, L, list(range(c0 // 128, (c0 + L) // 128)), "p",
                                        first_seq=(first_group and si == 0),
                                        last_prompt=(last_group and si == len(segs) - 1)))
            if has_s:
                sg.append(mixer_segment(gi, nt_p * 128, NS * LS, NS, LS, [nt_p], "s", first_seq=False, last_prompt=False))
            sg[0]["fronta"]()
            sg[0]["frontb"]()
            sg[0]["conv"]()
            sg[0]["poolG"]()
            while deferred_norm1:
                deferred_norm1.pop(0)()
            for k in range(len(sg)):
                sg[k]["S12"]()
                if k + 1 < len(sg):
                    sg[k + 1]["fronta"]()
                    sg[k + 1]["conv"]()
                sg[k]["S34"]()
                if k + 1 < len(sg):
                    sg[k + 1]["frontb"]()
                    sg[k + 1]["poolG"]()
            if not last_group:
                P.op("pool", lambda e, npt=npt: e.tensor_copy(out=h1hist_t[:, :, 0:HX], in_=h1T_a[:, :, npt:npt + HX]),
                     reads=[h1Tb[nt_p - 1]], writes=[h1hist])
            half_p = npt // 2
            segs_p = [(0, half_p), (half_p, half_p)]
            nstages = 12

            def load_stage(s):
                stg = s % NST
                pr0 = s * 2
                P.dma("pool", "d_up", upb[stg][:, 0, :, :], up_v[:, :, pr0 * 128: pr0 * 128 + 256], writes=[upb_T[stg][0]])
                P.dma("pool", "d_up", upb[stg][:, 1, :, :], up_v[:, :, DFF + pr0 * 128: DFF + pr0 * 128 + 256],
                      writes=[upb_T[stg][1]])
            P._wait(P.q["pool"], [(q_.sem_key, q_.cnt) for q_ in P.q.values() if q_.cnt > 0 and q_.name != "pool"])
            for s0_ in range(NST):
                load_stage(s0_)
            P.dma("sp", "d_misc", grep.ap, g2r_d[:, :], writes=[grep])
            for t in range(nt + 2):
                if t < nt:
                    wout_A(t)
                if t >= 2:
                    wout_B(t - 2)
            P.barrier()
            P.dma("sp", "d_misc", grep.ap, gfr_d[:, :], writes=[grep])
            mult_evs.clear()
            for s in range(nstages):
                def prefetch(s=s):
                    if s + NST < nstages:
                        load_stage(s + NST)
                    P.dma("pool", "d_wd", wd_bf[:, s * 2:(s + 1) * 2, :], down_v[:, s * 2:(s + 1) * 2, :], writes=[wd_T[s]])
                up_pair(gi, s * 2, s % NST, 0, segs_p, has_s, nt_p, first_group, last_group)
                up_pair(gi, s * 2 + 1, s % NST, 1, segs_p, has_s, nt_p, first_group, last_group, on_last_mm=prefetch)
            flush_tail()
            if not last_group:
                P.op("dve", lambda e, npt=npt: e.tensor_copy(out=h2T_t[:, :, 0:2], in_=h2T_t[:, :, npt:npt + 2]),
                     reads=[h2Tb[nt_p - 1]], writes=[h2T_hist])
            evs_p4 = [(q_.sem_key, q_.cnt) for q_ in P.q.values() if q_.cnt > 0 and q_.name != "sp"]
            if not last_group:
                for part in (0, 2, 1):
                    P.dma("pool", "d_w", w_in_bf[:, :, part * 512:(part + 1) * 512],
                          w_in_v[:, :, part * 512:(part + 1) * 512], writes=[w_in_T[part]], extra=evs_p4)
                np0, nnpt, nhs = GROUPS[gi + 1]
                nnt_p = nnpt // 128
                nnt = nnt_p + (1 if nhs else 0)
            else:
                nnt = 0
            doneA = doneB = 0

            def load_next(tt):
                if tt < nnt_p:
                    P.dma("sp", "d_x", X[tt].ap, xp[np0 + tt * 128: np0 + (tt + 1) * 128, :], writes=[X[tt]])
                else:
                    P.dma("sp", "d_x", X[tt].ap, xs[:, :], writes=[X[tt]])

            for t in range(nt):
                yrow0 = (p0 + t * 128) if t < nt_p else SEQ
                down_final(gi, t, yrow0)
                if t < nnt:
                    load_next(t)
                if doneB < doneA and doneB <= t - 2:
                    norm1_B(doneB, extra=evs_p4)
                    doneB += 1
                if doneA < nnt and doneA <= t - 1:
                    norm1_A(doneA, extra=evs_p4)
                    doneA += 1
            for tt in range(nt, nnt):
                load_next(tt)
            def norm1_tail(doneA=doneA, doneB=doneB, nnt=nnt, evs_p4=evs_p4):
                while doneB < nnt:
                    if doneA == doneB:
                        norm1_A(doneA, extra=evs_p4)
                        doneA += 1
                    norm1_B(doneB, extra=evs_p4)
                    doneB += 1
            deferred_norm1.append(norm1_tail)

        P.dma("sp", "d_y", olc_d[:, :], olc_t[:, :, :, :].rearrange("p a b c -> p (a b c)"), reads=[olc])
        P.dma("sp", "d_y", oh_d[:, :], oh_t[:, :, :].rearrange("p a b -> p (a b)"), reads=[oh])
        P.dma("sp", "d_y", opool_d[:, :], opool_t[:, :, :, :].rearrange("p a b c -> p (a b c)"), reads=[opool])
        P.dma("sp", "d_y", offn_d[:, :], offn_t[:, :, :, :].rearrange("p a b c -> p (a b c)"), reads=[offn])
        P.final_wait("sp")

        with nc.Block() as block:
            def play(eng, q):
                own = sems[q.sem_key]
                for item in q.ops:
                    if item[0] == "wait":
                        eng.wait_ge(sems[item[1]], item[2])
                    elif item[0] == "op":
                        item[1](eng).then_inc(own, 1)
                    else:
                        eng.dma_start(out=item[1], in_=item[2]).then_inc(sems[item[3]], 16)

            @block.sync
            def _(e):
                play(e, P.q["sp"])

            @block.scalar
            def _(e):
                play(e, P.q["act"])

            @block.vector
            def _(e):
                play(e, P.q["dve"])

            @block.gpsimd
            def _(e):
                play(e, P.q["pool"])

            @block.tensor
            def _(e):
                play(e, P.q["pe"])
    build_program.last_prog = P
    return nc


_NC_CACHE = {}


def _layout_vec(v, nchunk):
    return np.ascontiguousarray(np.asarray(v, np.float32).reshape(nchunk, 128).T)


def kernel(x_prompt, x_sample, state_lru_conv, state_lru_h, state_pool, state_ffn_conv,
           norm1_g, w_in, lru_conv_w, lru_conv_b, lru_wa, lru_ba, lru_wx, lru_bx, lru_lambda,
           pool_w, pool_scale, w_out, norm2_g, ffn_up, ffn_conv_w, ffn_conv_b, ffn_down, final_g):
    f = lambda a: np.ascontiguousarray(np.asarray(a, dtype=np.float32))
    x_prompt, x_sample = f(x_prompt), f(x_sample)
    vecs = np.zeros((128, NV), np.float32)
    cw = f(lru_conv_w)[0]
    for j in range(4):
        vecs[:, V_CW + j * 4: V_CW + (j + 1) * 4] = _layout_vec(cw[j], 4)
    vecs[:, V_CB:V_CB + 4] = _layout_vec(f(lru_conv_b)[0], 4)
    vecs[:, V_BA:V_BA + 4] = _layout_vec(f(lru_ba)[0], 4)
    vecs[:, V_BX:V_BX + 4] = _layout_vec(f(lru_bx)[0], 4)
    vecs[:, V_LAM:V_LAM + 4] = _layout_vec(f(lru_lambda)[0], 4)
    vecs[:, V_PS:V_PS + 4] = _layout_vec(f(pool_scale)[0], 4)
    fcw = f(ffn_conv_w)[0]
    for j in range(3):
        vecs[:, V_FCW + j * 48: V_FCW + (j + 1) * 48] = _layout_vec(fcw[j], 48)
    vecs[:, V_FCB:V_FCB + 48] = _layout_vec(f(ffn_conv_b)[0], 48)
    vecs[:, V_G1:V_G1 + 8] = _layout_vec(f(norm1_g)[0], 8)
    rep = lambda g: np.ascontiguousarray(np.broadcast_to(f(g).reshape(1, D), (128, D)))
    common = {
        "vecs": vecs, "g1r": rep(norm1_g[0]), "g2r": rep(norm2_g[0]), "gfr": rep(final_g),
        "ident": np.eye(128, dtype=np.float32),
        "w_in": f(w_in)[0], "w_out": f(w_out)[0], "ffn_up": f(ffn_up)[0], "ffn_down": f(ffn_down)[0],
        "wa": f(lru_wa)[0], "wx": f(lru_wx)[0], "pw": f(pool_w)[0],
    }
    slc, slh, spl, sff = f(state_lru_conv)[0], f(state_lru_h)[0], f(state_pool)[0], f(state_ffn_conv)[0]
    in_maps = []
    for c in range(NCORES):
        s0, s1 = c * NS, (c + 1) * NS
        m = dict(common)
        m["xp"] = x_prompt[c]
        m["xs"] = x_sample[s0:s1].reshape(NS * LS, D)
        m["stlc"] = np.ascontiguousarray(slc[s0:s1].reshape(NS, 3, 4, 128).transpose(3, 2, 0, 1)).reshape(128, -1)
        m["sth"] = np.ascontiguousarray(slh[s0:s1].reshape(NS, 4, 128).transpose(2, 1, 0)).reshape(128, -1)
        m["stpool"] = np.ascontiguousarray(spl[s0:s1].reshape(NS, 15, 4, 128).transpose(3, 2, 0, 1)).reshape(128, -1)
        m["stffn"] = np.ascontiguousarray(sff[s0:s1].reshape(NS, 2, 48, 128).transpose(3, 2, 0, 1)).reshape(128, -1)
        in_maps.append(m)

    if "nc" not in _NC_CACHE:
        _NC_CACHE["nc"] = build_program()
    nc = _NC_CACHE["nc"]
    res = run_bass_kernel_spmd(nc, in_maps, core_ids=list(range(NCORES)))
    R = res.results

    y_prompt = np.empty((8, SEQ, D), np.float32)
    y_sample = np.empty((128, LS, D), np.float32)
    p_lc = np.empty((1, 8, 3, 512), np.float32)
    p_h = np.empty((1, 8, 512), np.float32)
    p_pool = np.empty((1, 8, 15, 512), np.float32)
    p_ffn = np.empty((1, 8, 2, 6144), np.float32)
    s_lc = np.empty((1, 128, 3, 512), np.float32)
    s_h = np.empty((1, 128, 512), np.float32)
    s_pool = np.empty((1, 128, 15, 512), np.float32)
    s_ffn = np.empty((1, 128, 2, 6144), np.float32)
    for c in range(NCORES):
        s0, s1 = c * NS, (c + 1) * NS
        y = np.asarray(R[c]["y"], np.float32)
        y_prompt[c] = y[:SEQ]
        y_sample[s0:s1] = y[SEQ:].reshape(NS, LS, D)
        olc = np.asarray(R[c]["o_lc"], np.float32).reshape(128, 4, 17, 3).transpose(2, 3, 1, 0).reshape(17, 3, 512)
        ohh = np.asarray(R[c]["o_h"], np.float32).reshape(128, 4, 17).transpose(2, 1, 0).reshape(17, 512)
        opl = np.asarray(R[c]["o_pool"], np.float32).reshape(128, 4, 17, 15).transpose(2, 3, 1, 0).reshape(17, 15, 512)
        off = np.asarray(R[c]["o_ffn"], np.float32).reshape(128, 48, 17, 2).transpose(2, 3, 1, 0).reshape(17, 2, 6144)
        p_lc[0, c], s_lc[0, s0:s1] = olc[0], olc[1:]
        p_h[0, c], s_h[0, s0:s1] = ohh[0], ohh[1:]
        p_pool[0, c], s_pool[0, s0:s1] = opl[0], opl[1:]
        p_ffn[0, c], s_ffn[0, s0:s1] = off[0], off[1:]
    return (y_prompt, y_sample, p_lc, p_h, p_pool, p_ffn, s_lc, s_h, s_pool, s_ffn)
```

```python
from contextlib import ExitStack

import numpy as np
import concourse.bass as bass
import concourse.mybir as mybir
from concourse.bass_utils import run_bass_kernel_spmd

F32 = mybir.dt.float32
BF16 = mybir.dt.bfloat16
AF = mybir.ActivationFunctionType
ALU = mybir.AluOpType

NCORES = 8
D = 1024
SEQ = 2048
NS = 16
LS = 8
DFF = 3072
EPS = 1e-6
POOL_W = (2, 4, 8, 16)

V_CW, V_CB, V_BA, V_BX, V_LAM, V_PS, V_FCW, V_FCB, V_G1, NV = 0, 16, 20, 24, 28, 32, 36, 180, 228, 236

GROUPS = [(0, 768, False), (768, 768, False), (1536, 512, True)]
GMAX = 768
ARENA_WORDS = 33920


class T:
    def __init__(self, ap, excl=False):
        self.ap = ap
        self.lastw = []
        self.readers = []
        self.excl = excl


class Q:
    def __init__(self, name, sem_key):
        self.name = name
        self.sem_key = sem_key
        self.cnt = 0
        self.waited = {}
        self.ops = []


class Prog:
    def __init__(self):
        self.q = {n: Q(n, "s_" + n) for n in ("pe", "act", "dve", "pool", "sp")}
        self.dma_cnt = {}
        self.ring = {}

    def _wait(self, q, deps):
        for (k, v) in deps:
            if q.name == "pe" and k == q.sem_key:
                continue
            if q.waited.get(k, 0) < v:
                q.ops.append(("wait", k, v))
                q.waited[k] = v

    @staticmethod
    def _deps(reads, writes, extra):
        deps = set(extra)
        for b in reads:
            deps.update(b.lastw)
            if b.excl:
                deps.update(b.readers)
        for b in writes:
            deps.update(b.lastw)
            deps.update(b.readers)
        return deps

    @staticmethod
    def _update(ev, reads, writes):
        for b in reads:
            b.readers.append(ev)
        for b in writes:
            b.lastw = [ev]
            b.readers = []

    def op(self, eng, fn, reads=(), writes=(), extra=()):
        q = self.q[eng]
        self._wait(q, self._deps(reads, writes, extra))
        q.cnt += 1
        ev = (q.sem_key, q.cnt)
        q.ops.append(("op", fn))
        self._update(ev, reads, writes)
        return ev

    NRING = 12

    def dma(self, eng, dsem, out, in_, reads=(), writes=(), extra=()):
        q = self.q[eng]
        i = self.ring.get(eng, 0)
        self.ring[eng] = i + 1
        dsem = "r_%s_%d" % (eng, i % self.NRING)
        prev = self.dma_cnt.get(dsem, 0)
        if prev:
            self._wait(q, [(dsem, prev)])
        self._wait(q, self._deps(reads, writes, extra))
        self.dma_cnt[dsem] = self.dma_cnt.get(dsem, 0) + 16
        ev = (dsem, self.dma_cnt[dsem])
        q.ops.append(("dma", out, in_, dsem))
        self._update(ev, reads, writes)
        return ev

    def barrier(self):
        evs = [(q.sem_key, q.cnt) for q in self.q.values() if q.cnt > 0]
        for q in self.q.values():
            self._wait(q, evs)

    def final_wait(self, eng):
        evs = [(q.sem_key, q.cnt) for q in self.q.values() if q.cnt > 0]
        evs += [(k, v) for k, v in self.dma_cnt.items()]
        self._wait(self.q[eng], evs)


def build_program():
    nc = bass.Bass("TRN2", target_bir_lowering=False)
    P = Prog()

    def din(name, shape):
        return nc.dram_tensor(name, list(shape), F32, kind="ExternalInput").ap()

    def dout(name, shape):
        return nc.dram_tensor(name, list(shape), F32, kind="ExternalOutput").ap()

    xp = din("xp", [SEQ, D])
    xs = din("xs", [NS * LS, D])
    vecs_d = din("vecs", [128, NV])
    g1r_d = din("g1r", [128, D])
    g2r_d = din("g2r", [128, D])
    gfr_d = din("gfr", [128, D])
    ident_d = din("ident", [128, 128])
    w_in_d = din("w_in", [D, 1536])
    w_out_d = din("w_out", [D, D])
    up_d = din("ffn_up", [D, 2 * DFF])
    down_d = din("ffn_down", [DFF, D])
    wa_d = din("wa", [8, 64, 64])
    wx_d = din("wx", [8, 64, 64])
    pw_d = din("pw", [4, 128, 128])
    stlc_d = din("stlc", [128, 4 * NS * 3])
    sth_d = din("sth", [128, 4 * NS])
    stpool_d = din("stpool", [128, 4 * NS * 15])
    stffn_d = din("stffn", [128, 48 * NS * 2])

    y_d = dout("y", [SEQ + NS * LS, D])
    olc_d = dout("o_lc", [128, 4 * 17 * 3])
    oh_d = dout("o_h", [128, 4 * 17])
    opool_d = dout("o_pool", [128, 4 * 17 * 15])
    offn_d = dout("o_ffn", [128, 48 * 17 * 2])

    w_in_v = w_in_d.rearrange("(kc p) n -> p kc n", p=128)
    w_out_v = w_out_d.rearrange("(kc p) n -> p kc n", p=128)
    up_v = up_d.rearrange("(kc p) n -> p kc n", p=128)
    down_v = down_d.rearrange("(c p) n -> p c n", p=128)

    es = ExitStack()
    with es:
        def sb(name, shape, dt=F32):
            return es.enter_context(nc.sbuf_tensor("sb_" + name, list(shape), dt))

        sems = {}
        for k in ("s_pe", "s_act", "s_dve", "s_pool", "s_sp"):
            sems[k] = es.enter_context(nc.semaphore(k))
        for eng_ in ("sp", "pool"):
            for i_ in range(Prog.NRING):
                k = "r_%s_%d" % (eng_, i_)
                sems[k] = es.enter_context(nc.semaphore(k))

        X_t = sb("X", [128, 6, D])
        X = [T(X_t[:, i, :]) for i in range(6)]
        h2T_t = sb("h2T", [128, 8, 2 + GMAX], BF16)
        h2T_hist = T(h2T_t[:, :, 0:2])
        h2Tb = [T(h2T_t[:, :, 2 + i * 128: 2 + (i + 1) * 128]) for i in range(6)]
        grep = T(sb("grep", [128, D])[:, :])
        vecs_t = sb("vecs", [128, NV])
        vecs = T(vecs_t[:, :])
        der_t = sb("der", [128, 16])
        der = T(der_t[:, :])
        ident_t = sb("ident", [128, 128], BF16)
        ident = T(ident_t[:, :])
        identf_t = sb("identf", [128, 128], F32)
        identf = T(identf_t[:, :])
        olc_t = sb("olc", [128, 4, 17, 3])
        oh_t = sb("oh", [128, 4, 17])
        opool_t = sb("opool", [128, 4, 17, 15])
        offn_t = sb("offn", [128, 48, 17, 2])
        olc, oh, opool, offn = T(olc_t), T(oh_t), T(opool_t), T(offn_t)
        zxs_t = sb("zxs", [128, 4, NS, 3 + LS])
        zxs = [T(zxs_t[:, c, :, :]) for c in range(4)]
        zps_t = sb("zps", [128, 4, NS, 15 + LS], BF16)
        zps = [T(zps_t[:, g, :, :]) for g in range(4)]
        sth_t = sb("sth", [128, 4, NS])
        stffn_t = sb("stffn", [128, 48, NS, 2])
        sth, stffn = T(sth_t), T(stffn_t)
        stats_t = sb("stats", [128, 192])
        h1hist_t = sb("h1hist", [128, 8, 4], BF16)
        h1hist = T(h1hist_t)
        histx_t = sb("histx", [128, 4, 3])
        histx = [T(histx_t[:, c, :]) for c in range(4)]
        histp_t = sb("histp", [128, 4, 15], BF16)
        histp = [T(histp_t[:, g, :]) for g in range(4)]
        dvec_t = sb("dvec", [128, 4, 16])
        dvec = T(dvec_t)
        ones16_t = sb("ones16", [128, 16])
        ones16 = T(ones16_t)
        wa_bd = sb("wa_bd", [128, 4, 128], BF16)
        wx_bd = sb("wx_bd", [128, 4, 128], BF16)
        wab_T = T(wa_bd)
        wxb_T = T(wx_bd)
        pwb = sb("pwb", [128, 4, 128], BF16)
        pws = sb("pws", [128, 4, 128], BF16)
        pwn = sb("pwn", [128, 4, 128], BF16)
        pwb_T, pws_T, pwn_T = T(pwb), T(pws), T(pwn)
        arena_t = sb("arena", [128, ARENA_WORDS])
        stpool_t = arena_t[:, 31760:31760 + 4 * NS * 15].rearrange("p (a b c) -> p a b c", a=4, b=NS)
        stlc_t = arena_t[:, 20548:20548 + 4 * NS * 3].rearrange("p (a b c) -> p a b c", a=4, b=NS)
        stlc, stpool = T(stlc_t), T(stpool_t)

        ps_t = es.enter_context(nc.psum_tensor("ps", [128, 8, 512], F32))
        banks = [T(ps_t[:, i, :], excl=True) for i in range(8)]
        bank_rr = [0]

        def newbank():
            b = banks[bank_rr[0] % 8]
            bank_rr[0] += 1
            return b

        def newbank_pair():
            if bank_rr[0] % 2:
                bank_rr[0] += 1
            i = bank_rr[0] % 8
            bank_rr[0] += 2
            return i, banks[i], banks[i + 1]

        stat_i = [0]

        def newstat():
            i = stat_i[0]
            stat_i[0] += 1
            return T(stats_t[:, i:i + 1])

        class Carver:
            def __init__(self):
                self.off = 0

            def f32(self, shape):
                n = int(np.prod(shape[1:]))
                a = arena_t[:, self.off:self.off + n]
                self.off += n
                assert self.off <= ARENA_WORDS, self.off
                return self._shape(a, shape)

            def bf16(self, shape):
                n = int(np.prod(shape[1:]))
                nw = (n + 1) // 2
                a = arena_t[:, self.off:self.off + nw].bitcast(BF16)[:, 0:n]
                self.off += nw
                assert self.off <= ARENA_WORDS, self.off
                return self._shape(a, shape)

            @staticmethod
            def _shape(a, shape):
                if len(shape) == 2:
                    return a
                if len(shape) == 3:
                    return a.rearrange("p (a b) -> p a b", a=shape[1])
                if len(shape) == 4:
                    return a.rearrange("p (a b c) -> p a b c", a=shape[1], b=shape[2])
                if len(shape) == 5:
                    return a.rearrange("p (a b c d) -> p a b c d", a=shape[1], b=shape[2], c=shape[3])
                raise ValueError(shape)

        DZ0, DZ1 = 21504, 31760
        cm = Carver()
        cm.off = DZ0
        w_in_bf = cm.bf16([128, 8, 1536])
        w_in_T = [T(w_in_bf[:, :, 0:512]), T(w_in_bf[:, :, 512:1024]), T(w_in_bf[:, :, 1024:1536])]
        HX = 3
        h1T_a = cm.bf16([128, 8, HX + GMAX + 1])
        h1T_hist = T(h1T_a[:, :, 0:HX])
        h1Tb = [T(h1T_a[:, :, HX + i * 128: HX + (i + 1) * 128]) for i in range(6)]
        xn1 = [T(cm.f32([128, D])) for _ in range(1)]
        assert cm.off <= DZ1, cm.off
        cm.off = 0
        w_out_bf = cm.bf16([128, 8, D])
        w_out_T = T(w_out_bf)
        mixedT_a = cm.bf16([128, 8, GMAX])
        mixedTb = [T(mixedT_a[:, :, i * 128:(i + 1) * 128]) for i in range(6)]
        LSEG = 384
        lru_sets = []
        for s_ in range(4):
            d = {}
            d["ext"] = T(cm.f32([128, 1, 3 + LSEG]))
            d["acc"] = T(cm.f32([128, LSEG]))
            d["xcb"] = T(cm.bf16([128, LSEG]))
            d["tr"] = T(cm.f32([128, LSEG]))
            d["ti"] = T(cm.f32([128, LSEG]))
            d["a"] = T(cm.f32([128, LSEG]))
            d["a2"] = T(cm.f32([128, LSEG]))
            lru_sets.append(d)
        gz_one = [T(cm.f32([128, LSEG])) for _ in range(4)]
        gz2 = [gz_one, gz_one]
        zpe = [T(cm.bf16([128, 1, 15 + LSEG])) for _ in range(4)]
        xn_m = [T(cm.bf16([128, D])) for _ in range(3)]
        fixS = T(cm.f32([128, 16]))
        fixSd = T(cm.bf16([128, 16]))
        assert cm.off <= DZ0, cm.off
        mixer_words = DZ1

        cf = Carver()
        actT_a = cf.bf16([128, 24, GMAX])
        actTb = [T(actT_a[:, :, i * 128:(i + 1) * 128]) for i in range(6)]
        wd_bf = cf.bf16([128, 24, D])
        wd_T = [T(wd_bf[:, i * 2:(i + 1) * 2, :]) for i in range(12)]
        NST = 3
        upb = [cf.bf16([128, 2, 8, 256]) for _ in range(NST)]
        upb_T = [(T(u[:, 0, :, :]), T(u[:, 1, :, :])) for u in upb]
        NACC = 2
        accA = [cf.f32([128, 2, 2, 384]) for _ in range(NACC)]
        accT = [(T(a[:, 0, :, :]), T(a[:, 1, :, :])) for a in accA]
        gbuf = [T(cf.f32([128, 2, 384])) for _ in range(NACC)]
        acc_rr = [0]
        uexts = [T(cf.f32([128, 2, NS, 2 + LS])) for _ in range(2)]
        uext_halves = [(T(u.ap[:, 0, :, :]), T(u.ap[:, 1, :, :])) for u in uexts]
        ybuf = [T(cf.f32([128, D])) for _ in range(1)]
        ffn_words = cf.off
        assert max(mixer_words, ffn_words) <= ARENA_WORDS

        def act_fn(out, in_, func, **kw):
            return lambda e: e.activation(out=out, in_=in_, func=func, **kw)

        def vcol(off, c):
            return vecs_t[:, off + c: off + c + 1]

        def dcol(off, c):
            return der_t[:, off + c: off + c + 1]

        D_HBA, D_HBX, D_HC, D_C2 = 0, 4, 8, 12

        P.dma("sp", "d_misc", vecs_t[:, :], vecs_d[:, :], writes=[vecs])
        P.dma("sp", "d_misc", grep.ap, g1r_d[:, :], writes=[grep])
        P.dma("sp", "d_x", X[0].ap, xp[0:128, :], writes=[X[0]])
        P.dma("pool", "d_w", ident_t[:, :], ident_d[:, :], writes=[ident])
        P.dma("pool", "d_w", pwb[:, :, :], pw_d.rearrange("g i j -> i g j"), writes=[pwb_T])
        for part in (0, 2, 1):
            P.dma("pool", "d_w", w_in_bf[:, :, part * 512:(part + 1) * 512], w_in_v[:, :, part * 512:(part + 1) * 512],
                  writes=[w_in_T[part]])
        P.dma("sp", "d_misc", identf_t[:, :], ident_d[:, :], writes=[identf])
        P.dma("sp", "d_misc", stlc_t[:, :, :, :].rearrange("p a b c -> p (a b c)"), stlc_d[:, :], writes=[stlc])
        P.dma("sp", "d_misc", sth_t[:, :, :].rearrange("p a b -> p (a b)"), sth_d[:, :], writes=[sth])
        P.dma("sp", "d_misc", stpool_t[:, :, :, :].rearrange("p a b c -> p (a b c)"), stpool_d[:, :], writes=[stpool])
        P.dma("sp", "d_misc", stffn_t[:, :, :, :].rearrange("p a b c -> p (a b c)"), stffn_d[:, :], writes=[stffn])

        P.op("dve", lambda e: e.tensor_scalar(out=der_t[:, 0:8], in0=vecs_t[:, V_BA:V_BA + 8], scalar1=0.5,
                                              scalar2=None, op0=ALU.mult), reads=[vecs], writes=[der])
        tmp_e4 = T(stats_t[:, 176:180])
        tmp_s4 = T(stats_t[:, 180:184])
        P.op("act", act_fn(stats_t[:, 176:180], vecs_t[:, V_LAM:V_LAM + 4], AF.Exp, scale=-1.0),
             reads=[vecs], writes=[tmp_e4])
        P.op("act", act_fn(stats_t[:, 180:184], stats_t[:, 176:180], AF.Ln, bias=1.0),
             reads=[tmp_e4], writes=[tmp_s4])
        P.op("dve", lambda e: e.tensor_scalar(out=der_t[:, 8:12], in0=stats_t[:, 180:184], scalar1=-4.0,
                                              scalar2=None, op0=ALU.mult), reads=[tmp_s4], writes=[der])
        P.op("dve", lambda e: e.tensor_scalar(out=der_t[:, 12:16], in0=stats_t[:, 180:184], scalar1=-8.0,
                                              scalar2=None, op0=ALU.mult), reads=[tmp_s4], writes=[der])
        P.op("dve", lambda e: e.memset(histx_t[:, :, :], 0.0), writes=histx)
        P.op("dve", lambda e: e.memset(histp_t[:, :, :], 0.0), writes=histp)
        P.op("dve", lambda e: e.memset(ones16_t[:, :], 1.0), writes=[ones16])
        P.op("dve", lambda e: e.memset(dvec_t[:, :, :], 0.0), writes=[dvec])
        for g, w in enumerate(POOL_W):
            for t in range(w - 1):
                val = 1.0 / (t + 1) - 1.0 / w
                P.op("dve", lambda e, g=g, t=t, val=val: e.memset(dvec_t[:, g, t:t + 1], val), writes=[dvec])
        P.op("dve", lambda e: e.memset(h2T_t[:, :, 0:2], 0.0), writes=[h2T_hist])
        for c in range(4):
            P.op("dve", lambda e, c=c: e.tensor_copy(out=zxs_t[:, c, :, 0:3], in_=stlc_t[:, c, :, :]),
                 reads=[stlc], writes=[zxs[c]])
            P.op("dve", lambda e, c=c: e.tensor_copy(out=zps_t[:, c, :, 0:15], in_=stpool_t[:, c, :, :]),
                 reads=[stpool], writes=[zps[c]])
            P.op("dve", lambda e, c=c: e.tensor_copy(out=opool_t[:, c, 1:17, 0:7], in_=stpool_t[:, c, :, 8:15]),
                 reads=[stpool], writes=[opool])

        P.op("dve", lambda e: e.memset(wa_bd[:, :, :], 0.0), writes=[wab_T])
        P.op("dve", lambda e: e.memset(wx_bd[:, :, :], 0.0), writes=[wxb_T])
        for h in range(8):
            r0 = (h % 2) * 64
            P.dma("pool", "d_w", wa_bd[r0:r0 + 64, h // 2, r0:r0 + 64], wa_d[h, :, :], writes=[wab_T])
            P.dma("pool", "d_w", wx_bd[r0:r0 + 64, h // 2, r0:r0 + 64], wx_d[h, :, :], writes=[wxb_T])
        def scale_pool_weights():
            for g, w in enumerate(POOL_W):
                P.op("dve", lambda e, g=g, w=w: e.tensor_scalar(out=pws[:, g, :], in0=pwb[:, g, :], scalar1=1.0 / w,
                                                                scalar2=None, op0=ALU.mult), reads=[pwb_T], writes=[pws_T])
                P.op("dve", lambda e, g=g: e.tensor_scalar(out=pwn[:, g, :], in0=pwb[:, g, :], scalar1=-1.0,
                                                           scalar2=None, op0=ALU.mult), reads=[pwb_T], writes=[pwn_T])

        def load_x(gi):
            p0, npt, has_s = GROUPS[gi]
            nt = npt // 128
            for t in range(1 if gi == 0 else 0, nt):
                P.dma("sp", "d_x", X[t].ap, xp[p0 + t * 128: p0 + (t + 1) * 128, :], writes=[X[t]])
            if has_s:
                P.dma("sp", "d_x", X[nt].ap, xs[:, :], writes=[X[nt]])

        def norm_A(xt, grep_T, xn):
            ssq, sq, rs = newstat(), newstat(), newstat()
            P.op("act", act_fn(xn.ap, xt.ap, AF.Square, accum_out=ssq.ap), reads=[xt], writes=[xn, ssq])
            P.op("act", act_fn(sq.ap, ssq.ap, AF.Sqrt, scale=1.0 / D, bias=EPS), reads=[ssq], writes=[sq])
            P.op("dve", lambda e: e.reciprocal(out=rs.ap, in_=sq.ap), reads=[sq], writes=[rs])
            P.op("dve", lambda e: e.scalar_tensor_tensor(out=xn.ap, in0=xt.ap, scalar=rs.ap, in1=grep_T.ap,
                                                         op0=ALU.mult, op1=ALU.mult),
                 reads=[xt, rs, grep_T], writes=[xn])

        def norm1_A(t, extra=()):
            xt, xn = X[t], xn1[0]
            ssq, sq, rs = newstat(), newstat(), newstat()
            P.op("act", act_fn(xn.ap, xt.ap, AF.Square, accum_out=ssq.ap), reads=[xt], writes=[xn, ssq], extra=extra)
            P.op("act", act_fn(sq.ap, ssq.ap, AF.Sqrt, scale=1.0 / D, bias=EPS), reads=[ssq], writes=[sq])
            P.op("dve", lambda e: e.reciprocal(out=rs.ap, in_=sq.ap), reads=[sq], writes=[rs], extra=extra)
            P.op("act", act_fn(xn.ap, xt.ap, AF.Identity, scale=rs.ap), reads=[xt, rs], writes=[xn])

        def norm1_B(t, extra=()):
            xn = xn1[0]
            for half in range(2):
                bk = newbank()

                def tr(e, half=half, bk=bk):
                    ins = None
                    for j in range(4):
                        kc = half * 4 + j
                        ins = e.transpose(out=bk.ap[:, j * 128:(j + 1) * 128], in_=xn.ap[:, kc * 128:(kc + 1) * 128],
                                          identity=identf_t[:, :])
                    return ins
                P.op("pe", tr, reads=[xn, identf], writes=[bk], extra=extra)
                for j in range(4):
                    kc = half * 4 + j
                    P.op("act", act_fn(h1T_a[:, kc, HX + t * 128: HX + (t + 1) * 128], bk.ap[:, j * 128:(j + 1) * 128], AF.Identity,
                                       scale=vcol(V_G1, kc)), reads=[bk, vecs], writes=[h1Tb[t]])

        def norm_B(xn, dstT_blocks_ap, dst_T):
            bk = newbank()
            bkb = bk.ap[:, :].bitcast(BF16)

            def tr(e):
                ins = None
                for kc in range(8):
                    ins = e.transpose(out=bkb[:, kc * 128:(kc + 1) * 128], in_=xn.ap[:, kc * 128:(kc + 1) * 128],
                                      identity=ident_t[:, :])
                return ins
            P.op("pe", tr, reads=[xn, ident], writes=[bk])
            P.op("act", act_fn(dstT_blocks_ap, bkb.rearrange("p (k n) -> p k n", k=8), AF.Copy),
                 reads=[bk], writes=[dst_T])

        seg_ctr = [0]

        def mixer_segment(gi, col0, L, S, Ls, tiles, kind, first_seq, last_prompt):
            is_s = kind == "s"
            gz = gz2[seg_ctr[0] % 2]
            seg_ctr[0] += 1

            def as3(ap2):
                return ap2.rearrange("p (s l) -> p s l", s=S)

            h1 = [h1Tb[t] for t in tiles]
            mT = [mixedTb[t] for t in tiles]

            H = 0 if (is_s or first_seq) else HX

            def w_in_mm(m, hist=0):
                bk = newbank()

                def mm(e, m=m, bk=bk):
                    ins = None
                    for kc in range(8):
                        ins = e.matmul(bk.ap[:, 0:L + hist], lhsT=w_in_bf[:, kc, m * 128:(m + 1) * 128],
                                       rhs=h1T_a[:, kc, HX + col0 - hist: HX + col0 + L], start=(kc == 0), stop=(kc == 7))
                    return ins
                rd = h1 + [w_in_T[m // 4]]
                if hist:
                    rd.append(h1T_hist if col0 == 0 else h1Tb[col0 // 128 - 1])
                P.op("pe", mm, reads=rd, writes=[bk])
                return bk

            ext3 = {}
            extT = {}

            def fronta():
              if is_s:
                  for c in range(4):
                      bk = w_in_mm(c)
                      P.op("dve", lambda e, c=c, bk=bk: e.tensor_copy(out=zxs_t[:, c, :, 3:3 + LS], in_=as3(bk.ap[:, 0:L])),
                           reads=[bk], writes=[zxs[c]])
                      ext3[c], extT[c] = zxs_t[:, c, :, :], zxs[c]
              else:
                  zb = {}
                  for c in range(4):
                      bk = w_in_mm(c, hist=H)
                      zb[c] = bk
                      st = lru_sets[c]
                      P.op("act", act_fn(st["acc"].ap[:, 0:L], bk.ap[:, H:H + L], AF.Identity, scale=vcol(V_CW, 3 * 4 + c),
                                         bias=vcol(V_CB, c)), reads=[bk, vecs], writes=[st["acc"]])
                      if last_prompt:
                          P.op("act", act_fn(olc_t[:, c, 0, :], bk.ap[:, H + L - 3:H + L], AF.Copy), reads=[bk], writes=[olc])
                  for j in (2, 1, 0):
                      sh = 3 - j
                      for c in range(4):
                          st = lru_sets[c]
                          bk = zb[c]
                          if H:
                              o_, i_ = st["acc"].ap[:, 0:L], bk.ap[:, H - sh:H - sh + L]
                          else:
                              o_, i_ = st["acc"].ap[:, sh:L], bk.ap[:, 0:L - sh]
                          P.op("dve", lambda e, o_=o_, i_=i_, j=j, c=c: e.scalar_tensor_tensor(
                              out=o_, in0=i_, scalar=vcol(V_CW, j * 4 + c), in1=o_, op0=ALU.mult, op1=ALU.add),
                              reads=[bk, vecs], writes=[st["acc"]])
              for g, w in enumerate(POOL_W):
                  bk = w_in_mm(8 + g)
                  if is_s:
                      P.op("act", act_fn(zps_t[:, g, :, 15:15 + LS], as3(bk.ap[:, 0:L]), AF.Copy), reads=[bk], writes=[zps[g]])
                      P.op("act", act_fn(opool_t[:, g, 1:17, 7:15], as3(bk.ap[:, 0:L]), AF.Copy), reads=[bk], writes=[opool])
                  else:
                      P.op("pool", lambda e, g=g: e.tensor_copy(out=zpe[g].ap[:, 0, 0:15], in_=histp_t[:, g, :]),
                           reads=[histp[g]], writes=[zpe[g]])
                      P.op("act", act_fn(zpe[g].ap[:, :, 15:15 + L], as3(bk.ap[:, 0:L]), AF.Copy), reads=[bk], writes=[zpe[g]])
                      P.op("pool", lambda e, g=g: e.tensor_copy(out=histp_t[:, g, :], in_=zpe[g].ap[:, 0, L:L + 15]),
                           reads=[zpe[g]], writes=[histp[g]])
                      if last_prompt:
                          P.op("act", act_fn(opool_t[:, g, 0, :], bk.ap[:, L - 15:L], AF.Copy), reads=[bk], writes=[opool])
            acc3 = {c: lru_sets[c]["acc"].ap[:, 0:L].rearrange("p (s l) -> p s l", s=S) for c in range(4)}

            def stageC(cs):
                if not is_s:
                    return
                for c in cs:
                    st = lru_sets[c]
                    P.op("dve", lambda e, c=c: e.tensor_scalar(out=acc3[c], in0=ext3[c][:, :, 3:3 + Ls],
                                                               scalar1=vcol(V_CW, 3 * 4 + c), scalar2=vcol(V_CB, c),
                                                               op0=ALU.mult, op1=ALU.add),
                         reads=[extT[c], vecs], writes=[st["acc"]])
                for j in (2, 1, 0):
                    for c in cs:
                        st = lru_sets[c]
                        P.op("dve", lambda e, j=j, c=c: e.scalar_tensor_tensor(out=acc3[c], in0=ext3[c][:, :, j:j + Ls],
                                                                               scalar=vcol(V_CW, j * 4 + c), in1=acc3[c],
                                                                               op0=ALU.mult, op1=ALU.add),
                             reads=[extT[c], vecs], writes=[st["acc"]])
                if is_s:
                    for c in cs:
                        P.op("pool", lambda e, c=c: e.tensor_copy(out=olc_t[:, c, 1:17, :], in_=zxs_t[:, c, :, LS:LS + 3]),
                             reads=[zxs[c]], writes=[olc])

            def stageD(cs):
                banks_ax = {}
                for c in cs:
                    st = lru_sets[c]
                    P.op("dve", lambda e, st=st: e.tensor_copy(out=st["xcb"].ap[:, 0:L], in_=st["acc"].ap[:, 0:L]),
                         reads=[st["acc"]], writes=[st["xcb"]])
                    bka, bkx = newbank(), newbank()
                    P.op("pe", lambda e, st=st, c=c, bka=bka: e.matmul(bka.ap[:, 0:L], lhsT=wa_bd[:, c, :], rhs=st["xcb"].ap[:, 0:L],
                                                                       start=True, stop=True),
                         reads=[st["xcb"], wab_T], writes=[bka])
                    P.op("pe", lambda e, st=st, c=c, bkx=bkx: e.matmul(bkx.ap[:, 0:L], lhsT=wx_bd[:, c, :], rhs=st["xcb"].ap[:, 0:L],
                                                                       start=True, stop=True),
                         reads=[st["xcb"], wxb_T], writes=[bkx])
                    banks_ax[c] = (bka, bkx)
                for c in cs:
                    st = lru_sets[c]
                    ba_, bx_ = banks_ax[c]
                    P.op("act", act_fn(st["ti"].ap[:, 0:L], bx_.ap[:, 0:L], AF.Tanh, scale=0.5, bias=dcol(D_HBX, c)),
                         reads=[bx_, der], writes=[st["ti"]])
                for c in cs:
                    st = lru_sets[c]
                    ba_, bx_ = banks_ax[c]
                    P.op("act", act_fn(st["tr"].ap[:, 0:L], ba_.ap[:, 0:L], AF.Tanh, scale=0.5, bias=dcol(D_HBA, c)),
                         reads=[ba_, der], writes=[st["tr"]])

            def stageD2(cs):
                for c in cs:
                    st = lru_sets[c]
                    P.op("act", act_fn(st["a"].ap[:, 0:L], st["tr"].ap[:, 0:L], AF.Exp, scale=dcol(D_HC, c), bias=dcol(D_HC, c)),
                         reads=[st["tr"], der], writes=[st["a"]])
                    P.op("pool", lambda e, st=st: e.tensor_tensor(out=st["a2"].ap[:, 0:L], in0=st["a"].ap[:, 0:L],
                                                                  in1=st["a"].ap[:, 0:L], op=ALU.mult),
                         reads=[st["a"]], writes=[st["a2"]])

            def stageE(cs):
                for c in cs:
                    st = lru_sets[c]
                    P.op("act", act_fn(st["a2"].ap[:, 0:L], st["a2"].ap[:, 0:L], AF.Sqrt, scale=-1.0, bias=1.0),
                         reads=[], writes=[st["a2"]])

            def stageF1(cs):
                for c in cs:
                    st = lru_sets[c]
                    P.op("dve", lambda e, st=st: e.scalar_tensor_tensor(out=st["ti"].ap[:, 0:L], in0=st["ti"].ap[:, 0:L], scalar=1.0,
                                                                        in1=st["acc"].ap[:, 0:L], op0=ALU.add, op1=ALU.mult),
                         reads=[st["acc"]], writes=[st["ti"]])

            def stageF(cs):
                for c in cs:
                    st = lru_sets[c]
                    P.op("dve", lambda e, st=st: e.scalar_tensor_tensor(out=st["ti"].ap[:, 0:L], in0=st["ti"].ap[:, 0:L], scalar=0.5,
                                                                        in1=st["a2"].ap[:, 0:L], op0=ALU.mult, op1=ALU.mult),
                         reads=[st["a2"]], writes=[st["ti"]])
                for c in cs:
                    st = lru_sets[c]
                    hb = st["tr"]
                    if is_s:
                        a3s = st["a"].ap[:, 0:L].rearrange("p (s l) -> p s l", s=NS)
                        b3s = st["ti"].ap[:, 0:L].rearrange("p (s l) -> p s l", s=NS)
                        P.op("dve", lambda e, a3s=a3s, c=c: e.tensor_tensor(out=fixS.ap, in0=a3s[:, :, 0], in1=sth_t[:, c, :],
                                                                           op=ALU.mult),
                             reads=[st["a"], sth], writes=[fixS])
                        P.op("dve", lambda e, b3s=b3s: e.tensor_tensor(out=b3s[:, :, 0], in0=b3s[:, :, 0], in1=fixS.ap, op=ALU.add),
                             reads=[fixS], writes=[st["ti"]])
                        P.op("dve", lambda e, a3s=a3s: e.memset(a3s[:, :, 0:1], 0.0), writes=[st["a"]])
                        P.op("dve", lambda e, st=st, hb=hb: e.tensor_tensor_scan(
                            out=hb.ap[:, 0:L], data0=st["a"].ap[:, 0:L], data1=st["ti"].ap[:, 0:L], initial=0.0,
                            op0=ALU.mult, op1=ALU.add),
                            reads=[st["a"], st["ti"]], writes=[hb])
                        P.op("dve", lambda e, hb=hb, c=c: e.tensor_copy(
                            out=oh_t[:, c, 1:17], in_=hb.ap[:, 0:L].rearrange("p (s l) -> p s l", s=NS)[:, :, LS - 1]),
                            reads=[hb], writes=[oh])
                    else:
                        init = 0.0 if first_seq else oh_t[:, c, 0:1]
                        P.op("dve", lambda e, st=st, hb=hb, init=init: e.tensor_tensor_scan(
                            out=hb.ap[:, 0:L], data0=st["a"].ap[:, 0:L], data1=st["ti"].ap[:, 0:L], initial=init,
                            op0=ALU.mult, op1=ALU.add),
                            reads=[st["a"], st["ti"], oh], writes=[hb])
                        P.op("dve", lambda e, hb=hb, c=c: e.tensor_copy(out=oh_t[:, c, 0:1], in_=hb.ap[:, L - 1:L]),
                             reads=[hb], writes=[oh])
                    P.op("pool", lambda e, hb=hb, c=c: e.tensor_tensor(out=mixedT_a[:, c, col0:col0 + L], in0=hb.ap[:, 0:L],
                                                                       in1=gz[c].ap[:, 0:L], op=ALU.mult),
                         reads=[hb, gz[c]], writes=mT)

            def frontb():
                for c in range(4):
                    bk = w_in_mm(4 + c)
                    P.op("act", act_fn(gz[c].ap[:, 0:L], bk.ap[:, 0:L], AF.Gelu_apprx_tanh), reads=[bk], writes=[gz[c]])

            def conv():
                stageC([0, 1])
                stageC([2, 3])

            def poolG():
              for g, w in enumerate(POOL_W):
                  bk = newbank()
                  src = zps_t[:, g, :, :] if is_s else zpe[g].ap
                  srcT = zps[g] if is_s else zpe[g]
                  do_fix = first_seq and not is_s
                  if do_fix:
                      P.op("dve", lambda e, g=g: e.tensor_tensor_scan(out=fixS.ap, data0=ones16_t[:, :],
                                                                      data1=zpe[g].ap[:, 0, 15:31], initial=0.0,
                                                                      op0=ALU.mult, op1=ALU.add),
                           reads=[zpe[g], ones16], writes=[fixS])
                      P.op("dve", lambda e, g=g: e.tensor_tensor(out=fixSd.ap, in0=fixS.ap, in1=dvec_t[:, g, :], op=ALU.mult),
                           reads=[fixS, dvec], writes=[fixSd])

                  def pm(e, g=g, w=w, bk=bk, src=src, do_fix=do_fix):
                      ins = None
                      out3 = as3(bk.ap[:, 0:L])
                      for k in range(w):
                          ins = e.matmul(out3, lhsT=pws[:, g, :], rhs=src[:, :, 15 - k:15 - k + Ls],
                                         start=(k == 0), stop=False)
                      ins = e.matmul(out3, lhsT=pwn[:, g, :], rhs=src[:, :, 15:15 + Ls], start=False, stop=(not do_fix))
                      if do_fix:
                          ins = e.matmul(bk.ap[:, 0:16], lhsT=pwb[:, g, :], rhs=fixSd.ap, start=False, stop=True)
                      return ins
                  rd = [srcT, pws_T, pwn_T] + ([fixSd, pwb_T] if do_fix else [])
                  P.op("pe", pm, reads=rd, writes=[bk])
                  P.op("act", act_fn(mixedT_a[:, 4 + g, col0:col0 + L], bk.ap[:, 0:L], AF.Identity, scale=vcol(V_PS, g)),
                       reads=[bk, vecs], writes=mT)
            def S12():
                stageD([0, 1, 2, 3])
                stageF1([0, 1, 2, 3])

            def S34():
                stageD2([0, 1, 2, 3])
                stageE([0, 1, 2, 3])
                stageF([0, 1])
                stageF([2, 3])

            return {"fronta": fronta, "frontb": frontb, "conv": conv, "poolG": poolG, "S12": S12, "S34": S34}

        def wout_A(t):
            for half in range(2):
                bk = newbank()

                def mm(e, half=half, bk=bk):
                    ins = None
                    for kc in range(8):
                        ins = e.matmul(bk.ap[:, :], lhsT=mixedT_a[:, kc, t * 128:(t + 1) * 128],
                                       rhs=w_out_bf[:, kc, half * 512:(half + 1) * 512], start=(kc == 0), stop=(kc == 7))
                    return ins
                P.op("pe", mm, reads=[mixedTb[t], w_out_T], writes=[bk])
                xs_ = X[t].ap[:, half * 512:(half + 1) * 512]
                P.op("dve", lambda e, bk=bk, xs_=xs_: e.tensor_tensor(out=xs_, in0=bk.ap[:, :], in1=xs_, op=ALU.add),
                     reads=[bk, X[t]], writes=[X[t]])
            norm_A(X[t], grep, xn_m[t % 3])

        def wout_B(t):
            norm_B(xn_m[t % 3], h2T_t[:, :, 2 + t * 128: 2 + (t + 1) * 128], h2Tb[t])

        tail_pending = []
        mult_evs = {}

        def flush_tail():
            while tail_pending:
                tail_pending.pop(0)()

        def up_pair(gi, pr, stg, j, segs_p, has_s, nt_p, first_group, last_group, on_last_mm=None):
            cg, cv = pr, 24 + pr
            L = segs_p[0][1]
            N = L + 2
            ig, bg0, bg1 = newbank_pair()
            iv, bv0, bv1 = newbank_pair()
            bks = {0: (bg0, bg1), 1: (bv0, bv1)}
            ib = {0: ig, 1: iv}
            all_tiles = list(range(0, (2 * L) // 128))
            for si, (c0, L_) in enumerate(segs_p):
                tiles = list(range(c0 // 128, (c0 + L) // 128))
                rd = [h2Tb[t] for t in tiles] + list(upb_T[stg])
                rd.append(h2T_hist if c0 == 0 else h2Tb[c0 // 128 - 1])
                for gv in (0, 1):
                    bk = bks[gv][si]

                    def mm(e, gv=gv, bk=bk, c0=c0, N=N, L=L):
                        ins = None
                        for kc in range(8):
                            ins = e.matmul(bk.ap[:, 0:N], lhsT=upb[stg][:, gv, kc, j * 128:(j + 1) * 128],
                                           rhs=h2T_t[:, kc, c0: 2 + c0 + L], start=(kc == 0), stop=(kc == 7))
                        return ins
                    P.op("pe", mm, reads=rd, writes=[bk])
            if on_last_mm is not None and not has_s:
                on_last_mm()
            k_ = acc_rr[0] % NACC
            acc_rr[0] += 1
            acc = accA[k_]
            aT = accT[k_]
            gb = gbuf[k_]
            for gv, ch in ((0, cg), (1, cv)):
                P.op("act", act_fn(acc[:, gv, :, 0:L], ps_t[:, ib[gv]:ib[gv] + 2, 2:2 + L], AF.Identity,
                                   scale=vcol(V_FCW, 2 * 48 + ch), bias=vcol(V_FCB, ch)),
                     reads=list(bks[gv]) + [vecs], writes=[aT[gv]])
            for tap, sh in ((1, 1), (0, 2)):
                for gv, ch in ((0, cg), (1, cv)):
                    o_ = acc[:, gv, :, 0:L]
                    i_ = ps_t[:, ib[gv]:ib[gv] + 2, 2 - sh:2 - sh + L]
                    P.op("dve", lambda e, o_=o_, i_=i_, tap=tap, ch=ch: e.scalar_tensor_tensor(
                        out=o_, in0=i_, scalar=vcol(V_FCW, tap * 48 + ch), in1=o_, op0=ALU.mult, op1=ALU.add),
                        reads=list(bks[gv]) + [vecs], writes=[aT[gv]])
            if last_group:
                for gv, ch in ((0, cg), (1, cv)):
                    bk = bks[gv][1]
                    P.op("dve", lambda e, bk=bk, ch=ch, N=N: e.tensor_copy(out=offn_t[:, ch, 0, :], in_=bk.ap[:, N - 2:N]),
                         reads=[bk], writes=[offn])
            flush_tail()

            def tail(gb=gb, acc=acc, aT=aT, L=L, all_tiles=all_tiles):
                P.op("act", act_fn(gb.ap[:, :, 0:L], acc[:, 0, :, 0:L], AF.Gelu_apprx_tanh), reads=[aT[0]], writes=[gb])
                mult_evs[pr] = P.op("dve", lambda e: e.tensor_tensor(
                    out=actT_a[:, pr, 0:2 * L].rearrange("p (s l) -> p s l", s=2), in0=gb.ap[:, :, 0:L],
                    in1=acc[:, 1, :, 0:L], op=ALU.mult),
                    reads=[gb, aT[1]], writes=[actTb[t] for t in all_tiles])
            tail_pending.append(tail)
            if has_s:
                c0 = nt_p * 128
                L = NS * LS
                t = nt_p
                bg, bv = newbank(), newbank()
                ue = uexts[pr % 2]
                k_ = acc_rr[0] % NACC
                acc_rr[0] += 1
                acc = accA[k_][:, :, 0, :]
                aT = accT[k_]
                gb = T(gbuf[k_].ap[:, 0, :])
                gb_full = gbuf[k_]
                for gv, bk in ((0, bg), (1, bv)):
                    def mm(e, gv=gv, bk=bk, c0=c0, L=L):
                        ins = None
                        for kc in range(8):
                            ins = e.matmul(bk.ap[:, 0:L], lhsT=upb[stg][:, gv, kc, j * 128:(j + 1) * 128],
                                           rhs=h2T_t[:, kc, 2 + c0: 2 + c0 + L], start=(kc == 0), stop=(kc == 7))
                        return ins
                    P.op("pe", mm, reads=[h2Tb[t]] + list(upb_T[stg]), writes=[bk])
                if on_last_mm is not None:
                    on_last_mm()
                ueT = uext_halves[pr % 2]
                for gv, bk, ch in ((0, bg, cg), (1, bv, cv)):
                    P.op("act", act_fn(ue.ap[:, gv, :, 0:2], stffn_t[:, ch, :, :], AF.Copy), reads=[stffn], writes=[ueT[gv]])
                    b3 = bk.ap[:, 0:L].rearrange("p (s l) -> p s l", s=NS)
                    P.op("act", act_fn(ue.ap[:, gv, :, 2:2 + LS], b3, AF.Copy), reads=[bk], writes=[ueT[gv]])
                for gv, bk, ch in ((0, bg, cg), (1, bv, cv)):
                    a3 = acc[:, gv, 0:L].rearrange("p (s l) -> p s l", s=NS)
                    if gv == 0:
                        b3 = bk.ap[:, 0:L].rearrange("p (s l) -> p s l", s=NS)
                        P.op("act", act_fn(a3, b3, AF.Identity, scale=vcol(V_FCW, 2 * 48 + ch), bias=vcol(V_FCB, ch)),
                             reads=[bk, vecs], writes=[aT[gv]])
                    else:
                        P.op("dve", lambda e, gv=gv, ch=ch, a3=a3: e.tensor_scalar(
                            out=a3, in0=ue.ap[:, gv, :, 2:2 + LS], scalar1=vcol(V_FCW, 2 * 48 + ch), scalar2=vcol(V_FCB, ch),
                            op0=ALU.mult, op1=ALU.add), reads=[ueT[gv], vecs], writes=[aT[gv]])
                for tap, sh in ((1, 1), (0, 2)):
                    for gv, bk, ch in ((0, bg, cg), (1, bv, cv)):
                        a3 = acc[:, gv, 0:L].rearrange("p (s l) -> p s l", s=NS)
                        P.op("dve", lambda e, gv=gv, ch=ch, a3=a3, tap=tap, sh=sh: e.scalar_tensor_tensor(
                            out=a3, in0=ue.ap[:, gv, :, 2 - sh:2 - sh + LS], scalar=vcol(V_FCW, tap * 48 + ch), in1=a3,
                            op0=ALU.mult, op1=ALU.add), reads=[ueT[gv], vecs], writes=[aT[gv]])
                for gv, bk, ch in ((0, bg, cg), (1, bv, cv)):
                    P.op("act", act_fn(offn_t[:, ch, 1:17, :], ue.ap[:, gv, :, LS:LS + 2], AF.Copy), reads=[ueT[gv]], writes=[offn])
                flush_tail()

                def tail(gb=gb, gb_full=gb_full, acc=acc, aT=aT, c0=c0, L=L, t=t):
                    P.op("act", act_fn(gb.ap[:, 0:L], acc[:, 0, 0:L], AF.Gelu_apprx_tanh), reads=[aT[0]], writes=[gb_full])
                    P.op("dve", lambda e: e.tensor_tensor(
                        out=actT_a[:, pr, c0:c0 + L], in0=gb.ap[:, 0:L], in1=acc[:, 1, 0:L], op=ALU.mult),
                        reads=[gb_full, aT[1]], writes=[actTb[t]])
                tail_pending.append(tail)

        def down_final(gi, t, yrow0):
            CS = 20
            split = (t == 0 and (CS - 1) in mult_evs)
            bks2 = [newbank(), newbank()]

            def mk(half, bk):
                def mm(e, c_lo=0, c_hi=24):
                    ins = None
                    for c in range(c_lo, c_hi):
                        ins = e.matmul(bk.ap[:, :], lhsT=actT_a[:, c, t * 128:(t + 1) * 128],
                                       rhs=wd_bf[:, c, half * 512:(half + 1) * 512], start=(c == 0), stop=(c == 23))
                    return ins
                return mm
            mms = [mk(0, bks2[0]), mk(1, bks2[1])]
            if split:
                for half in range(2):
                    ev_ = P.op("pe", lambda e, mm=mms[half]: mm(e, c_lo=0, c_hi=CS), reads=wd_T[:CS // 2],
                               writes=[bks2[half]], extra=[mult_evs[CS - 1]])
                    actTb[t].readers.append(ev_)
                for half in range(2):
                    P.op("pe", lambda e, mm=mms[half]: mm(e, c_lo=CS, c_hi=24), reads=[actTb[t]] + wd_T[CS // 2:],
                         writes=[bks2[half]])
            else:
                for half in range(2):
                    P.op("pe", mms[half], reads=[actTb[t]] + wd_T, writes=[bks2[half]])
            for half in range(2):
                bk = bks2[half]
                xs_ = X[t].ap[:, half * 512:(half + 1) * 512]
                P.op("dve", lambda e, bk=bk, xs_=xs_: e.tensor_tensor(out=xs_, in0=bk.ap[:, :], in1=xs_, op=ALU.add),
                     reads=[bk, X[t]], writes=[X[t]])
            ssq, sq, rs = newstat(), newstat(), newstat()
            yb = ybuf[0]
            P.op("act", act_fn(yb.ap, X[t].ap, AF.Square, accum_out=ssq.ap), reads=[X[t]], writes=[yb, ssq])
            P.op("act", act_fn(sq.ap, ssq.ap, AF.Sqrt, scale=1.0 / D, bias=EPS), reads=[ssq], writes=[sq])
            P.op("dve", lambda e: e.reciprocal(out=rs.ap, in_=sq.ap), reads=[sq], writes=[rs])
            P.op("dve", lambda e: e.scalar_tensor_tensor(out=yb.ap, in0=X[t].ap, scalar=rs.ap, in1=grep.ap,
                                                         op0=ALU.mult, op1=ALU.mult),
                 reads=[X[t], rs, grep], writes=[yb])
            P.dma("sp", "d_y", y_d[yrow0:yrow0 + 128, :], yb.ap, reads=[yb])

        deferred_norm1 = []
        load_x(0)
        for gi, (p0, npt, has_s) in enumerate(GROUPS):
            nt_p = npt // 128
            nt = nt_p + (1 if has_s else 0)
            first_group = gi == 0
            last_group = gi == len(GROUPS) - 1
            if gi > 0:
                P.barrier()
            else:
                def p1_tiles(t0, t1):
                    for t in range(t0, t1 + 1):
                        if t < t1:
                            norm_A(X[t], grep, xn_m[t % 2])
                        if t >= t0 + 1:
                            norm_B(xn_m[(t - 1) % 2], h1T_a[:, :, HX + (t - 1) * 128: HX + t * 128], h1Tb[t - 1])
                p1_tiles(0, 3)
                deferred_norm1.append(lambda nt=nt: p1_tiles(3, nt))
            if gi > 0:
                P.op("pool", lambda e: e.tensor_copy(out=h1T_a[:, :, 0:HX], in_=h1hist_t[:, :, 0:HX]),
                     reads=[h1hist], writes=[h1T_hist])
            P.dma("pool", "d_w", w_out_bf[:, :, :], w_out_v[:, :, :], writes=[w_out_T])
            if gi == 0:
                scale_pool_weights()
            segs = []
            c0 = 0
            while c0 < npt:
                L = min(384 if npt % 384 == 0 else 256, npt - c0)
                segs.append((c0, L))
                c0 += L
            sg = []
            for si, (c0, L) in enumerate(segs):
                sg.append(mixer_segment(gi, c0, L, 1, L, list(range(c0 // 128, (c0 + L) // 128)), "p",
                                        first_seq=(first_group and si == 0),
                                        last_prompt=(last_group and si == len(segs) - 1)))
            if has_s:
                sg.append(mixer_segment(gi, nt_p * 128, NS * LS, NS, LS, [nt_p], "s", first_seq=False, last_prompt=False))
            sg[0]["fronta"]()
            sg[0]["frontb"]()
            sg[0]["conv"]()
            sg[0]["poolG"]()
            while deferred_norm1:
                deferred_norm1.pop(0)()
            for k in range(len(sg)):
                sg[k]["S12"]()
                if k + 1 < len(sg):
                    sg[k + 1]["fronta"]()
                    sg[k + 1]["conv"]()
                sg[k]["S34"]()
                if k + 1 < len(sg):
                    sg[k + 1]["frontb"]()
                    sg[k + 1]["poolG"]()
            if not last_group:
                P.op("pool", lambda e, npt=npt: e.tensor_copy(out=h1hist_t[:, :, 0:HX], in_=h1T_a[:, :, npt:npt + HX]),
                     reads=[h1Tb[nt_p - 1]], writes=[h1hist])
            half_p = npt // 2
            segs_p = [(0, half_p), (half_p, half_p)]
            nstages = 12

            def load_stage(s):
                stg = s % NST
                pr0 = s * 2
                P.dma("pool", "d_up", upb[stg][:, 0, :, :], up_v[:, :, pr0 * 128: pr0 * 128 + 256], writes=[upb_T[stg][0]])
                P.dma("pool", "d_up", upb[stg][:, 1, :, :], up_v[:, :, DFF + pr0 * 128: DFF + pr0 * 128 + 256],
                      writes=[upb_T[stg][1]])
            P._wait(P.q["pool"], [(q_.sem_key, q_.cnt) for q_ in P.q.values() if q_.cnt > 0 and q_.name != "pool"])
            for s0_ in range(NST):
                load_stage(s0_)
            P.dma("sp", "d_misc", grep.ap, g2r_d[:, :], writes=[grep])
            for t in range(nt + 2):
                if t < nt:
                    wout_A(t)
                if t >= 2:
                    wout_B(t - 2)
            P.barrier()
            P.dma("sp", "d_misc", grep.ap, gfr_d[:, :], writes=[grep])
            mult_evs.clear()
            for s in range(nstages):
                def prefetch(s=s):
                    if s + NST < nstages:
                        load_stage(s + NST)
                    P.dma("pool", "d_wd", wd_bf[:, s * 2:(s + 1) * 2, :], down_v[:, s * 2:(s + 1) * 2, :], writes=[wd_T[s]])
                up_pair(gi, s * 2, s % NST, 0, segs_p, has_s, nt_p, first_group, last_group)
                up_pair(gi, s * 2 + 1, s % NST, 1, segs_p, has_s, nt_p, first_group, last_group, on_last_mm=prefetch)
            flush_tail()
            if not last_group:
                P.op("dve", lambda e, npt=npt: e.tensor_copy(out=h2T_t[:, :, 0:2], in_=h2T_t[:, :, npt:npt + 2]),
                     reads=[h2Tb[nt_p - 1]], writes=[h2T_hist])
            if last_group:
                P.dma("sp", "d_y", olc_d[:, :], olc_t[:, :, :, :].rearrange("p a b c -> p (a b c)"), reads=[olc])
                P.dma("sp", "d_y", oh_d[:, :], oh_t[:, :, :].rearrange("p a b -> p (a b)"), reads=[oh])
                P.dma("sp", "d_y", opool_d[:, :], opool_t[:, :, :, :].rearrange("p a b c -> p (a b c)"), reads=[opool])
                P.dma("sp", "d_y", offn_d[:, :], offn_t[:, :, :, :].rearrange("p a b c -> p (a b c)"), reads=[offn])
            evs_p4 = [(q_.sem_key, q_.cnt) for q_ in P.q.values() if q_.cnt > 0 and q_.name != "sp"]
            if not last_group:
                for part in (0, 2, 1):
                    P.dma("pool", "d_w", w_in_bf[:, :, part * 512:(part + 1) * 512],
                          w_in_v[:, :, part * 512:(part + 1) * 512], writes=[w_in_T[part]], extra=evs_p4)
                np0, nnpt, nhs = GROUPS[gi + 1]
                nnt_p = nnpt // 128
                nnt = nnt_p + (1 if nhs else 0)
            else:
                nnt = 0
            doneA = doneB = 0

            def load_next(tt):
                if tt < nnt_p:
                    P.dma("sp", "d_x", X[tt].ap, xp[np0 + tt * 128: np0 + (tt + 1) * 128, :], writes=[X[tt]])
                else:
                    P.dma("sp", "d_x", X[tt].ap, xs[:, :], writes=[X[tt]])

            for t in range(nt):
                yrow0 = (p0 + t * 128) if t < nt_p else SEQ
                down_final(gi, t, yrow0)
                if t < nnt:
                    load_next(t)
                if doneB < doneA and doneB <= t - 2:
                    norm1_B(doneB, extra=evs_p4)
                    doneB += 1
                if doneA < nnt and doneA <= t - 1:
                    norm1_A(doneA, extra=evs_p4)
                    doneA += 1
            for tt in range(nt, nnt):
                load_next(tt)
            def norm1_tail(doneA=doneA, doneB=doneB, nnt=nnt, evs_p4=evs_p4):
                while doneB < nnt:
                    if doneA == doneB:
                        norm1_A(doneA, extra=evs_p4)
                        doneA += 1
                    norm1_B(doneB, extra=evs_p4)
                    doneB += 1
            deferred_norm1.append(norm1_tail)

        P.final_wait("sp")

        with nc.Block() as block:
            def play(eng, q):
                own = sems[q.sem_key]
                for item in q.ops:
                    if item[0] == "wait":
                        eng.wait_ge(sems[item[1]], item[2])
                    elif item[0] == "op":
                        item[1](eng).then_inc(own, 1)
                    else:
                        eng.dma_start(out=item[1], in_=item[2]).then_inc(sems[item[3]], 16)

            @block.sync
            def _(e):
                play(e, P.q["sp"])

            @block.scalar
            def _(e):
                play(e, P.q["act"])

            @block.vector
            def _(e):
                play(e, P.q["dve"])

            @block.gpsimd
            def _(e):
                play(e, P.q["pool"])

            @block.tensor
            def _(e):
                play(e, P.q["pe"])
    build_program.last_prog = P
    return nc


_NC_CACHE = {}


def _layout_vec(v, nchunk):
    return np.ascontiguousarray(np.asarray(v, np.float32).reshape(nchunk, 128).T)


def kernel(x_prompt, x_sample, state_lru_conv, state_lru_h, state_pool, state_ffn_conv,
           norm1_g, w_in, lru_conv_w, lru_conv_b, lru_wa, lru_ba, lru_wx, lru_bx, lru_lambda,
           pool_w, pool_scale, w_out, norm2_g, ffn_up, ffn_conv_w, ffn_conv_b, ffn_down, final_g):
    f = lambda a: np.ascontiguousarray(np.asarray(a, dtype=np.float32))
    x_prompt, x_sample = f(x_prompt), f(x_sample)
    vecs = np.zeros((128, NV), np.float32)
    cw = f(lru_conv_w)[0]
    for j in range(4):
        vecs[:, V_CW + j * 4: V_CW + (j + 1) * 4] = _layout_vec(cw[j], 4)
    vecs[:, V_CB:V_CB + 4] = _layout_vec(f(lru_conv_b)[0], 4)
    vecs[:, V_BA:V_BA + 4] = _layout_vec(f(lru_ba)[0], 4)
    vecs[:, V_BX:V_BX + 4] = _layout_vec(f(lru_bx)[0], 4)
    vecs[:, V_LAM:V_LAM + 4] = _layout_vec(f(lru_lambda)[0], 4)
    vecs[:, V_PS:V_PS + 4] = _layout_vec(f(pool_scale)[0], 4)
    fcw = f(ffn_conv_w)[0]
    for j in range(3):
        vecs[:, V_FCW + j * 48: V_FCW + (j + 1) * 48] = _layout_vec(fcw[j], 48)
    vecs[:, V_FCB:V_FCB + 48] = _layout_vec(f(ffn_conv_b)[0], 48)
    vecs[:, V_G1:V_G1 + 8] = _layout_vec(f(norm1_g)[0], 8)
    rep = lambda g: np.ascontiguousarray(np.broadcast_to(f(g).reshape(1, D), (128, D)))
    common = {
        "vecs": vecs, "g1r": rep(norm1_g[0]), "g2r": rep(norm2_g[0]), "gfr": rep(final_g),
        "ident": np.eye(128, dtype=np.float32),
        "w_in": f(w_in)[0], "w_out": f(w_out)[0], "ffn_up": f(ffn_up)[0], "ffn_down": f(ffn_down)[0],
        "wa": f(lru_wa)[0], "wx": f(lru_wx)[0], "pw": f(pool_w)[0],
    }
    slc, slh, spl, sff = f(state_lru_conv)[0], f(state_lru_h)[0], f(state_pool)[0], f(state_ffn_conv)[0]
    in_maps = []
    for c in range(NCORES):
        s0, s1 = c * NS, (c + 1) * NS
        m = dict(common)
        m["xp"] = x_prompt[c]
        m["xs"] = x_sample[s0:s1].reshape(NS * LS, D)
        m["stlc"] = np.ascontiguousarray(slc[s0:s1].reshape(NS, 3, 4, 128).transpose(3, 2, 0, 1)).reshape(128, -1)
        m["sth"] = np.ascontiguousarray(slh[s0:s1].reshape(NS, 4, 128).transpose(2, 1, 0)).reshape(128, -1)
        m["stpool"] = np.ascontiguousarray(spl[s0:s1].reshape(NS, 15, 4, 128).transpose(3, 2, 0, 1)).reshape(128, -1)
        m["stffn"] = np.ascontiguousarray(sff[s0:s1].reshape(NS, 2, 48, 128).transpose(3, 2, 0, 1)).reshape(128, -1)
        in_maps.append(m)

    if "nc" not in _NC_CACHE:
        _NC_CACHE["nc"] = build_program()
    nc = _NC_CACHE["nc"]
    res = run_bass_kernel_spmd(nc, in_maps, core_ids=list(range(NCORES)))
    R = res.results

    y_prompt = np.empty((8, SEQ, D), np.float32)
    y_sample = np.empty((128, LS, D), np.float32)
    p_lc = np.empty((1, 8, 3, 512), np.float32)
    p_h = np.empty((1, 8, 512), np.float32)
    p_pool = np.empty((1, 8, 15, 512), np.float32)
    p_ffn = np.empty((1, 8, 2, 6144), np.float32)
    s_lc = np.empty((1, 128, 3, 512), np.float32)
    s_h = np.empty((1, 128, 512), np.float32)
    s_pool = np.empty((1, 128, 15, 512), np.float32)
    s_ffn = np.empty((1, 128, 2, 6144), np.float32)
    for c in range(NCORES):
        s0, s1 = c * NS, (c + 1) * NS
        y = np.asarray(R[c]["y"], np.float32)
        y_prompt[c] = y[:SEQ]
        y_sample[s0:s1] = y[SEQ:].reshape(NS, LS, D)
        olc = np.asarray(R[c]["o_lc"], np.float32).reshape(128, 4, 17, 3).transpose(2, 3, 1, 0).reshape(17, 3, 512)
        ohh = np.asarray(R[c]["o_h"], np.float32).reshape(128, 4, 17).transpose(2, 1, 0).reshape(17, 512)
        opl = np.asarray(R[c]["o_pool"], np.float32).reshape(128, 4, 17, 15).transpose(2, 3, 1, 0).reshape(17, 15, 512)
        off = np.asarray(R[c]["o_ffn"], np.float32).reshape(128, 48, 17, 2).transpose(2, 3, 1, 0).reshape(17, 2, 6144)
        p_lc[0, c], s_lc[0, s0:s1] = olc[0], olc[1:]
        p_h[0, c], s_h[0, s0:s1] = ohh[0], ohh[1:]
        p_pool[0, c], s_pool[0, s0:s1] = opl[0], opl[1:]
        p_ffn[0, c], s_ffn[0, s0:s1] = off[0], off[1:]
    return (y_prompt, y_sample, p_lc, p_h, p_pool, p_ffn, s_lc, s_h, s_pool, s_ffn)
```

```python
from contextlib import ExitStack

import numpy as np
import concourse.bass as bass
import concourse.mybir as mybir
from concourse.bass_utils import run_bass_kernel_spmd

F32 = mybir.dt.float32
BF16 = mybir.dt.bfloat16
AF = mybir.ActivationFunctionType
ALU = mybir.AluOpType

NCORES = 8
D = 1024
SEQ = 2048
NS = 16
LS = 8
DFF = 3072
EPS = 1e-6
POOL_W = (2, 4, 8, 16)

V_CW, V_CB, V_BA, V_BX, V_LAM, V_PS, V_FCW, V_FCB, V_G1, NV = 0, 16, 20, 24, 28, 32, 36, 180, 228, 236

GROUPS = [(0, 768, False), (768, 768, False), (1536, 512, True)]
GMAX = 768
ARENA_WORDS = 33920


class T:
    def __init__(self, ap, excl=False):
        self.ap = ap
        self.lastw = []
        self.readers = []
        self.excl = excl


class Q:
    def __init__(self, name, sem_key):
        self.name = name
        self.sem_key = sem_key
        self.cnt = 0
        self.waited = {}
        self.ops = []


class Prog:
    def __init__(self):
        self.q = {n: Q(n, "s_" + n) for n in ("pe", "act", "dve", "pool", "sp")}
        self.dma_cnt = {}
        self.ring = {}

    def _wait(self, q, deps):
        for (k, v) in deps:
            if q.name == "pe" and k == q.sem_key:
                continue
            if q.waited.get(k, 0) < v:
                q.ops.append(("wait", k, v))
                q.waited[k] = v

    @staticmethod
    def _deps(reads, writes, extra):
        deps = set(extra)
        for b in reads:
            deps.update(b.lastw)
            if b.excl:
                deps.update(b.readers)
        for b in writes:
            deps.update(b.lastw)
            deps.update(b.readers)
        return deps

    @staticmethod
    def _update(ev, reads, writes):
        for b in reads:
            b.readers.append(ev)
        for b in writes:
            b.lastw = [ev]
            b.readers = []

    def op(self, eng, fn, reads=(), writes=(), extra=()):
        q = self.q[eng]
        self._wait(q, self._deps(reads, writes, extra))
        q.cnt += 1
        ev = (q.sem_key, q.cnt)
        q.ops.append(("op", fn))
        self._update(ev, reads, writes)
        return ev

    NRING = 12

    def dma(self, eng, dsem, out, in_, reads=(), writes=(), extra=()):
        q = self.q[eng]
        i = self.ring.get(eng, 0)
        self.ring[eng] = i + 1
        dsem = "r_%s_%d" % (eng, i % self.NRING)
        prev = self.dma_cnt.get(dsem, 0)
        if prev:
            self._wait(q, [(dsem, prev)])
        self._wait(q, self._deps(reads, writes, extra))
        self.dma_cnt[dsem] = self.dma_cnt.get(dsem, 0) + 16
        ev = (dsem, self.dma_cnt[dsem])
        q.ops.append(("dma", out, in_, dsem))
        self._update(ev, reads, writes)
        return ev

    def barrier(self):
        evs = [(q.sem_key, q.cnt) for q in self.q.values() if q.cnt > 0]
        for q in self.q.values():
            if q.name == "pe":
                continue
            self._wait(q, evs)

    def final_wait(self, eng):
        evs = [(q.sem_key, q.cnt) for q in self.q.values() if q.cnt > 0]
        evs += [(k, v) for k, v in self.dma_cnt.items()]
        self._wait(self.q[eng], evs)


def build_program():
    nc = bass.Bass("TRN2", target_bir_lowering=False)
    P = Prog()

    def din(name, shape):
        return nc.dram_tensor(name, list(shape), F32, kind="ExternalInput").ap()

    def dout(name, shape):
        return nc.dram_tensor(name, list(shape), F32, kind="ExternalOutput").ap()

    xp = din("xp", [SEQ, D])
    xs = din("xs", [NS * LS, D])
    vecs_d = din("vecs", [128, NV])
    g1r_d = din("g1r", [128, D])
    g2r_d = din("g2r", [128, D])
    gfr_d = din("gfr", [128, D])
    ident_d = din("ident", [128, 128])
    w_in_d = din("w_in", [D, 1536])
    w_out_d = din("w_out", [D, D])
    up_d = din("ffn_up", [D, 2 * DFF])
    down_d = din("ffn_down", [DFF, D])
    wa_d = din("wa", [8, 64, 64])
    wx_d = din("wx", [8, 64, 64])
    pw_d = din("pw", [4, 128, 128])
    stlc_d = din("stlc", [128, 4 * NS * 3])
    sth_d = din("sth", [128, 4 * NS])
    stpool_d = din("stpool", [128, 4 * NS * 15])
    stffn_d = din("stffn", [128, 48 * NS * 2])

    y_d = dout("y", [SEQ + NS * LS, D])
    olc_d = dout("o_lc", [128, 4 * 17 * 3])
    oh_d = dout("o_h", [128, 4 * 17])
    opool_d = dout("o_pool", [128, 4 * 17 * 15])
    offn_d = dout("o_ffn", [128, 48 * 17 * 2])

    w_in_v = w_in_d.rearrange("(kc p) n -> p kc n", p=128)
    w_out_v = w_out_d.rearrange("(kc p) n -> p kc n", p=128)
    up_v = up_d.rearrange("(kc p) n -> p kc n", p=128)
    down_v = down_d.rearrange("(c p) n -> p c n", p=128)

    es = ExitStack()
    with es:
        def sb(name, shape, dt=F32):
            return es.enter_context(nc.sbuf_tensor("sb_" + name, list(shape), dt))

        sems = {}
        for k in ("s_pe", "s_act", "s_dve", "s_pool", "s_sp"):
            sems[k] = es.enter_context(nc.semaphore(k))
        for eng_ in ("sp", "pool"):
            for i_ in range(Prog.NRING):
                k = "r_%s_%d" % (eng_, i_)
                sems[k] = es.enter_context(nc.semaphore(k))

        X_t = sb("X", [128, 6, D])
        X = [T(X_t[:, i, :]) for i in range(6)]
        h2T_t = sb("h2T", [128, 8, 2 + GMAX], BF16)
        h2T_hist = T(h2T_t[:, :, 0:2])
        h2Tb = [T(h2T_t[:, :, 2 + i * 128: 2 + (i + 1) * 128]) for i in range(6)]
        grep = T(sb("grep", [128, D])[:, :])
        vecs_t = sb("vecs", [128, NV])
        vecs = T(vecs_t[:, :])
        der_t = sb("der", [128, 16])
        der = T(der_t[:, :])
        ident_t = sb("ident", [128, 128], BF16)
        ident = T(ident_t[:, :])
        identf_t = sb("identf", [128, 128], F32)
        identf = T(identf_t[:, :])
        olc_t = sb("olc", [128, 4, 17, 3])
        oh_t = sb("oh", [128, 4, 17])
        opool_t = sb("opool", [128, 4, 17, 15])
        offn_t = sb("offn", [128, 48, 17, 2])
        olc, oh, opool, offn = T(olc_t), T(oh_t), T(opool_t), T(offn_t)
        zxs_t = sb("zxs", [128, 4, NS, 3 + LS])
        zxs = [T(zxs_t[:, c, :, :]) for c in range(4)]
        zps_t = sb("zps", [128, 4, NS, 15 + LS], BF16)
        zps = [T(zps_t[:, g, :, :]) for g in range(4)]
        sth_t = sb("sth", [128, 4, NS])
        stffn_t = sb("stffn", [128, 48, NS, 2])
        sth, stffn = T(sth_t), T(stffn_t)
        stats_t = sb("stats", [128, 192])
        h1hist_t = sb("h1hist", [128, 8, 4], BF16)
        h1hist = T(h1hist_t)
        histx_t = sb("histx", [128, 4, 3])
        histx = [T(histx_t[:, c, :]) for c in range(4)]
        histp_t = sb("histp", [128, 4, 15], BF16)
        histp = [T(histp_t[:, g, :]) for g in range(4)]
        dvec_t = sb("dvec", [128, 4, 16])
        dvec = T(dvec_t)
        ones16_t = sb("ones16", [128, 16])
        ones16 = T(ones16_t)
        wa_bd = sb("wa_bd", [128, 4, 128], BF16)
        wx_bd = sb("wx_bd", [128, 4, 128], BF16)
        wab_T = T(wa_bd)
        wxb_T = T(wx_bd)
        pwb = sb("pwb", [128, 4, 128], BF16)
        pws = sb("pws", [128, 4, 128], BF16)
        pwn = sb("pwn", [128, 4, 128], BF16)
        pwb_T, pws_T, pwn_T = T(pwb), T(pws), T(pwn)
        arena_t = sb("arena", [128, ARENA_WORDS])
        stpool_t = arena_t[:, 31760:31760 + 4 * NS * 15].rearrange("p (a b c) -> p a b c", a=4, b=NS)
        stlc_t = arena_t[:, 20548:20548 + 4 * NS * 3].rearrange("p (a b c) -> p a b c", a=4, b=NS)
        stlc, stpool = T(stlc_t), T(stpool_t)

        ps_t = es.enter_context(nc.psum_tensor("ps", [128, 8, 512], F32))
        banks = [T(ps_t[:, i, :], excl=True) for i in range(8)]
        bank_rr = [0]

        def newbank():
            b = banks[bank_rr[0] % 8]
            bank_rr[0] += 1
            return b

        def newbank_pair():
            if bank_rr[0] % 2:
                bank_rr[0] += 1
            i = bank_rr[0] % 8
            bank_rr[0] += 2
            return i, banks[i], banks[i + 1]

        stat_i = [0]

        def newstat():
            i = stat_i[0]
            stat_i[0] += 1
            return T(stats_t[:, i:i + 1])

        class Carver:
            def __init__(self):
                self.off = 0

            def f32(self, shape):
                n = int(np.prod(shape[1:]))
                a = arena_t[:, self.off:self.off + n]
                self.off += n
                assert self.off <= ARENA_WORDS, self.off
                return self._shape(a, shape)

            def bf16(self, shape):
                n = int(np.prod(shape[1:]))
                nw = (n + 1) // 2
                a = arena_t[:, self.off:self.off + nw].bitcast(BF16)[:, 0:n]
                self.off += nw
                assert self.off <= ARENA_WORDS, self.off
                return self._shape(a, shape)

            @staticmethod
            def _shape(a, shape):
                if len(shape) == 2:
                    return a
                if len(shape) == 3:
                    return a.rearrange("p (a b) -> p a b", a=shape[1])
                if len(shape) == 4:
                    return a.rearrange("p (a b c) -> p a b c", a=shape[1], b=shape[2])
                if len(shape) == 5:
                    return a.rearrange("p (a b c d) -> p a b c d", a=shape[1], b=shape[2], c=shape[3])
                raise ValueError(shape)

        DZ0, DZ1 = 21504, 31760
        cm = Carver()
        cm.off = DZ0
        w_in_bf = cm.bf16([128, 8, 1536])
        w_in_T = [T(w_in_bf[:, :, 0:512]), T(w_in_bf[:, :, 512:1024]), T(w_in_bf[:, :, 1024:1536])]
        HX = 3
        h1T_a = cm.bf16([128, 8, HX + GMAX + 1])
        h1T_hist = T(h1T_a[:, :, 0:HX])
        h1Tb = [T(h1T_a[:, :, HX + i * 128: HX + (i + 1) * 128]) for i in range(6)]
        xn1 = [T(cm.f32([128, D])) for _ in range(1)]
        assert cm.off <= DZ1, cm.off
        cm.off = 0
        w_out_bf = cm.bf16([128, 8, D])
        w_out_T = T(w_out_bf)
        mixedT_a = cm.bf16([128, 8, GMAX])
        mixedTb = [T(mixedT_a[:, :, i * 128:(i + 1) * 128]) for i in range(6)]
        LSEG = 384
        lru_sets = []
        for s_ in range(4):
            d = {}
            d["ext"] = T(cm.f32([128, 1, 3 + LSEG]))
            d["acc"] = T(cm.f32([128, LSEG]))
            d["xcb"] = T(cm.bf16([128, LSEG]))
            d["tr"] = T(cm.f32([128, LSEG]))
            d["ti"] = T(cm.f32([128, LSEG]))
            d["a"] = T(cm.f32([128, LSEG]))
            d["a2"] = T(cm.f32([128, LSEG]))
            lru_sets.append(d)
        gz_one = [T(cm.f32([128, LSEG])) for _ in range(4)]
        gz2 = [gz_one, gz_one]
        zpe = [T(cm.bf16([128, 1, 15 + LSEG])) for _ in range(4)]
        xn_m = [T(cm.bf16([128, D])) for _ in range(3)]
        fixS = T(cm.f32([128, 16]))
        fixSd = T(cm.bf16([128, 16]))
        assert cm.off <= DZ0, cm.off
        mixer_words = DZ1

        cf = Carver()
        actT_a = cf.bf16([128, 24, GMAX])
        actTb = [T(actT_a[:, :, i * 128:(i + 1) * 128]) for i in range(6)]
        wd_bf = cf.bf16([128, 24, D])
        wd_T = [T(wd_bf[:, i * 2:(i + 1) * 2, :]) for i in range(12)]
        NST = 3
        upb = [cf.bf16([128, 2, 8, 256]) for _ in range(NST)]
        upb_T = [(T(u[:, 0, :, :]), T(u[:, 1, :, :])) for u in upb]
        NACC = 2
        accA = [cf.f32([128, 2, 2, 384]) for _ in range(NACC)]
        accT = [(T(a[:, 0, :, :]), T(a[:, 1, :, :])) for a in accA]
        gbuf = [T(cf.f32([128, 2, 384])) for _ in range(NACC)]
        acc_rr = [0]
        uexts = [T(cf.f32([128, 2, NS, 2 + LS])) for _ in range(2)]
        uext_halves = [(T(u.ap[:, 0, :, :]), T(u.ap[:, 1, :, :])) for u in uexts]
        ybuf = [T(cf.f32([128, D])) for _ in range(1)]
        ffn_words = cf.off
        assert max(mixer_words, ffn_words) <= ARENA_WORDS

        def act_fn(out, in_, func, **kw):
            return lambda e: e.activation(out=out, in_=in_, func=func, **kw)

        def vcol(off, c):
            return vecs_t[:, off + c: off + c + 1]

        def dcol(off, c):
            return der_t[:, off + c: off + c + 1]

        D_HBA, D_HBX, D_HC, D_C2 = 0, 4, 8, 12

        P.dma("sp", "d_misc", vecs_t[:, :], vecs_d[:, :], writes=[vecs])
        P.dma("sp", "d_misc", grep.ap, g1r_d[:, :], writes=[grep])
        P.dma("sp", "d_x", X[0].ap, xp[0:128, :], writes=[X[0]])
        P.dma("pool", "d_w", ident_t[:, :], ident_d[:, :], writes=[ident])
        P.dma("pool", "d_w", pwb[:, :, :], pw_d.rearrange("g i j -> i g j"), writes=[pwb_T])
        for part in (0, 2, 1):
            P.dma("pool", "d_w", w_in_bf[:, :, part * 512:(part + 1) * 512], w_in_v[:, :, part * 512:(part + 1) * 512],
                  writes=[w_in_T[part]])
        P.dma("sp", "d_misc", identf_t[:, :], ident_d[:, :], writes=[identf])
        P.dma("sp", "d_misc", stlc_t[:, :, :, :].rearrange("p a b c -> p (a b c)"), stlc_d[:, :], writes=[stlc])
        P.dma("sp", "d_misc", sth_t[:, :, :].rearrange("p a b -> p (a b)"), sth_d[:, :], writes=[sth])
        P.dma("sp", "d_misc", stpool_t[:, :, :, :].rearrange("p a b c -> p (a b c)"), stpool_d[:, :], writes=[stpool])
        P.dma("sp", "d_misc", stffn_t[:, :, :, :].rearrange("p a b c -> p (a b c)"), stffn_d[:, :], writes=[stffn])

        P.op("dve", lambda e: e.tensor_scalar(out=der_t[:, 0:8], in0=vecs_t[:, V_BA:V_BA + 8], scalar1=0.5,
                                              scalar2=None, op0=ALU.mult), reads=[vecs], writes=[der])
        tmp_e4 = T(stats_t[:, 176:180])
        tmp_s4 = T(stats_t[:, 180:184])
        P.op("act", act_fn(stats_t[:, 176:180], vecs_t[:, V_LAM:V_LAM + 4], AF.Exp, scale=-1.0),
             reads=[vecs], writes=[tmp_e4])
        P.op("act", act_fn(stats_t[:, 180:184], stats_t[:, 176:180], AF.Ln, bias=1.0),
             reads=[tmp_e4], writes=[tmp_s4])
        P.op("dve", lambda e: e.tensor_scalar(out=der_t[:, 8:12], in0=stats_t[:, 180:184], scalar1=-4.0,
                                              scalar2=None, op0=ALU.mult), reads=[tmp_s4], writes=[der])
        P.op("dve", lambda e: e.tensor_scalar(out=der_t[:, 12:16], in0=stats_t[:, 180:184], scalar1=-8.0,
                                              scalar2=None, op0=ALU.mult), reads=[tmp_s4], writes=[der])
        P.op("dve", lambda e: e.memset(histx_t[:, :, :], 0.0), writes=histx)
        P.op("dve", lambda e: e.memset(histp_t[:, :, :], 0.0), writes=histp)
        P.op("dve", lambda e: e.memset(ones16_t[:, :], 1.0), writes=[ones16])
        P.op("dve", lambda e: e.memset(dvec_t[:, :, :], 0.0), writes=[dvec])
        for g, w in enumerate(POOL_W):
            for t in range(w - 1):
                val = 1.0 / (t + 1) - 1.0 / w
                P.op("dve", lambda e, g=g, t=t, val=val: e.memset(dvec_t[:, g, t:t + 1], val), writes=[dvec])
        P.op("dve", lambda e: e.memset(h2T_t[:, :, 0:2], 0.0), writes=[h2T_hist])
        for c in range(4):
            P.op("dve", lambda e, c=c: e.tensor_copy(out=zxs_t[:, c, :, 0:3], in_=stlc_t[:, c, :, :]),
                 reads=[stlc], writes=[zxs[c]])
            P.op("dve", lambda e, c=c: e.tensor_copy(out=zps_t[:, c, :, 0:15], in_=stpool_t[:, c, :, :]),
                 reads=[stpool], writes=[zps[c]])
            P.op("dve", lambda e, c=c: e.tensor_copy(out=opool_t[:, c, 1:17, 0:7], in_=stpool_t[:, c, :, 8:15]),
                 reads=[stpool], writes=[opool])

        P.op("dve", lambda e: e.memset(wa_bd[:, :, :], 0.0), writes=[wab_T])
        P.op("dve", lambda e: e.memset(wx_bd[:, :, :], 0.0), writes=[wxb_T])
        for h in range(8):
            r0 = (h % 2) * 64
            P.dma("pool", "d_w", wa_bd[r0:r0 + 64, h // 2, r0:r0 + 64], wa_d[h, :, :], writes=[wab_T])
            P.dma("pool", "d_w", wx_bd[r0:r0 + 64, h // 2, r0:r0 + 64], wx_d[h, :, :], writes=[wxb_T])
        def scale_pool_weights():
            for g, w in enumerate(POOL_W):
                P.op("dve", lambda e, g=g, w=w: e.tensor_scalar(out=pws[:, g, :], in0=pwb[:, g, :], scalar1=1.0 / w,
                                                                scalar2=None, op0=ALU.mult), reads=[pwb_T], writes=[pws_T])
                P.op("dve", lambda e, g=g: e.tensor_scalar(out=pwn[:, g, :], in0=pwb[:, g, :], scalar1=-1.0,
                                                           scalar2=None, op0=ALU.mult), reads=[pwb_T], writes=[pwn_T])

        def load_x(gi):
            p0, npt, has_s = GROUPS[gi]
            nt = npt // 128
            for t in range(1 if gi == 0 else 0, nt):
                P.dma("sp", "d_x", X[t].ap, xp[p0 + t * 128: p0 + (t + 1) * 128, :], writes=[X[t]])
            if has_s:
                P.dma("sp", "d_x", X[nt].ap, xs[:, :], writes=[X[nt]])

        def norm_A(xt, grep_T, xn):
            ssq, sq, rs = newstat(), newstat(), newstat()
            P.op("act", act_fn(xn.ap, xt.ap, AF.Square, accum_out=ssq.ap), reads=[xt], writes=[xn, ssq])
            P.op("act", act_fn(sq.ap, ssq.ap, AF.Sqrt, scale=1.0 / D, bias=EPS), reads=[ssq], writes=[sq])
            P.op("dve", lambda e: e.reciprocal(out=rs.ap, in_=sq.ap), reads=[sq], writes=[rs])
            P.op("dve", lambda e: e.scalar_tensor_tensor(out=xn.ap, in0=xt.ap, scalar=rs.ap, in1=grep_T.ap,
                                                         op0=ALU.mult, op1=ALU.mult),
                 reads=[xt, rs, grep_T], writes=[xn])

        def norm1_A(t, extra=()):
            xt, xn = X[t], xn1[0]
            ssq, sq, rs = newstat(), newstat(), newstat()
            P.op("act", act_fn(xn.ap, xt.ap, AF.Square, accum_out=ssq.ap), reads=[xt], writes=[xn, ssq], extra=extra)
            P.op("act", act_fn(sq.ap, ssq.ap, AF.Sqrt, scale=1.0 / D, bias=EPS), reads=[ssq], writes=[sq])
            P.op("dve", lambda e: e.reciprocal(out=rs.ap, in_=sq.ap), reads=[sq], writes=[rs], extra=extra)
            P.op("act", act_fn(xn.ap, xt.ap, AF.Identity, scale=rs.ap), reads=[xt, rs], writes=[xn])

        def norm1_B(t, extra=()):
            xn = xn1[0]
            for half in range(2):
                bk = newbank()

                def tr(e, half=half, bk=bk):
                    ins = None
                    for j in range(4):
                        kc = half * 4 + j
                        ins = e.transpose(out=bk.ap[:, j * 128:(j + 1) * 128], in_=xn.ap[:, kc * 128:(kc + 1) * 128],
                                          identity=identf_t[:, :])
                    return ins
                P.op("pe", tr, reads=[xn, identf], writes=[bk], extra=extra)
                for j in range(4):
                    kc = half * 4 + j
                    P.op("act", act_fn(h1T_a[:, kc, HX + t * 128: HX + (t + 1) * 128], bk.ap[:, j * 128:(j + 1) * 128], AF.Identity,
                                       scale=vcol(V_G1, kc)), reads=[bk, vecs], writes=[h1Tb[t]])

        def norm_B(xn, dstT_blocks_ap, dst_T):
            bk = newbank()
            bkb = bk.ap[:, :].bitcast(BF16)

            def tr(e):
                ins = None
                for kc in range(8):
                    ins = e.transpose(out=bkb[:, kc * 128:(kc + 1) * 128], in_=xn.ap[:, kc * 128:(kc + 1) * 128],
                                      identity=ident_t[:, :])
                return ins
            P.op("pe", tr, reads=[xn, ident], writes=[bk])
            P.op("act", act_fn(dstT_blocks_ap, bkb.rearrange("p (k n) -> p k n", k=8), AF.Copy),
                 reads=[bk], writes=[dst_T])

        seg_ctr = [0]

        def mixer_segment(gi, col0, L, S, Ls, tiles, kind, first_seq, last_prompt):
            is_s = kind == "s"
            gz = gz2[seg_ctr[0] % 2]
            seg_ctr[0] += 1

            def as3(ap2):
                return ap2.rearrange("p (s l) -> p s l", s=S)

            h1 = [h1Tb[t] for t in tiles]
            mT = [mixedTb[t] for t in tiles]

            H = 0 if (is_s or first_seq) else HX

            def w_in_mm(m, hist=0):
                bk = newbank()

                def mm(e, m=m, bk=bk):
                    ins = None
                    for kc in range(8):
                        ins = e.matmul(bk.ap[:, 0:L + hist], lhsT=w_in_bf[:, kc, m * 128:(m + 1) * 128],
                                       rhs=h1T_a[:, kc, HX + col0 - hist: HX + col0 + L], start=(kc == 0), stop=(kc == 7))
                    return ins
                rd = h1 + [w_in_T[m // 4]]
                if hist:
                    rd.append(h1T_hist if col0 == 0 else h1Tb[col0 // 128 - 1])
                P.op("pe", mm, reads=rd, writes=[bk])
                return bk

            ext3 = {}
            extT = {}

            def fronta():
              if is_s:
                  for c in range(4):
                      bk = w_in_mm(c)
                      P.op("dve", lambda e, c=c, bk=bk: e.tensor_copy(out=zxs_t[:, c, :, 3:3 + LS], in_=as3(bk.ap[:, 0:L])),
                           reads=[bk], writes=[zxs[c]])
                      ext3[c], extT[c] = zxs_t[:, c, :, :], zxs[c]
              else:
                  zb = {}
                  for c in range(4):
                      bk = w_in_mm(c, hist=H)
                      zb[c] = bk
                      st = lru_sets[c]
                      P.op("act", act_fn(st["acc"].ap[:, 0:L], bk.ap[:, H:H + L], AF.Identity, scale=vcol(V_CW, 3 * 4 + c),
                                         bias=vcol(V_CB, c)), reads=[bk, vecs], writes=[st["acc"]])
                      if last_prompt:
                          P.op("act", act_fn(olc_t[:, c, 0, :], bk.ap[:, H + L - 3:H + L], AF.Copy), reads=[bk], writes=[olc])
                  for j in (2, 1, 0):
                      sh = 3 - j
                      for c in range(4):
                          st = lru_sets[c]
                          bk = zb[c]
                          if H:
                              o_, i_ = st["acc"].ap[:, 0:L], bk.ap[:, H - sh:H - sh + L]
                          else:
                              o_, i_ = st["acc"].ap[:, sh:L], bk.ap[:, 0:L - sh]
                          P.op("dve", lambda e, o_=o_, i_=i_, j=j, c=c: e.scalar_tensor_tensor(
                              out=o_, in0=i_, scalar=vcol(V_CW, j * 4 + c), in1=o_, op0=ALU.mult, op1=ALU.add),
                              reads=[bk, vecs], writes=[st["acc"]])
              for g, w in enumerate(POOL_W):
                  bk = w_in_mm(8 + g)
                  if is_s:
                      P.op("act", act_fn(zps_t[:, g, :, 15:15 + LS], as3(bk.ap[:, 0:L]), AF.Copy), reads=[bk], writes=[zps[g]])
                      P.op("act", act_fn(opool_t[:, g, 1:17, 7:15], as3(bk.ap[:, 0:L]), AF.Copy), reads=[bk], writes=[opool])
                  else:
                      P.op("pool", lambda e, g=g: e.tensor_copy(out=zpe[g].ap[:, 0, 0:15], in_=histp_t[:, g, :]),
                           reads=[histp[g]], writes=[zpe[g]])
                      P.op("act", act_fn(zpe[g].ap[:, :, 15:15 + L], as3(bk.ap[:, 0:L]), AF.Copy), reads=[bk], writes=[zpe[g]])
                      P.op("pool", lambda e, g=g: e.tensor_copy(out=histp_t[:, g, :], in_=zpe[g].ap[:, 0, L:L + 15]),
                           reads=[zpe[g]], writes=[histp[g]])
                      if last_prompt:
                          P.op("act", act_fn(opool_t[:, g, 0, :], bk.ap[:, L - 15:L], AF.Copy), reads=[bk], writes=[opool])
            acc3 = {c: lru_sets[c]["acc"].ap[:, 0:L].rearrange("p (s l) -> p s l", s=S) for c in range(4)}

            def stageC(cs):
                if not is_s:
                    return
                for c in cs:
                    st = lru_sets[c]
                    P.op("dve", lambda e, c=c: e.tensor_scalar(out=acc3[c], in0=ext3[c][:, :, 3:3 + Ls],
                                                               scalar1=vcol(V_CW, 3 * 4 + c), scalar2=vcol(V_CB, c),
                                                               op0=ALU.mult, op1=ALU.add),
                         reads=[extT[c], vecs], writes=[st["acc"]])
                for j in (2, 1, 0):
                    for c in cs:
                        st = lru_sets[c]
                        P.op("dve", lambda e, j=j, c=c: e.scalar_tensor_tensor(out=acc3[c], in0=ext3[c][:, :, j:j + Ls],
                                                                               scalar=vcol(V_CW, j * 4 + c), in1=acc3[c],
                                                                               op0=ALU.mult, op1=ALU.add),
                             reads=[extT[c], vecs], writes=[st["acc"]])
                if is_s:
                    for c in cs:
                        P.op("pool", lambda e, c=c: e.tensor_copy(out=olc_t[:, c, 1:17, :], in_=zxs_t[:, c, :, LS:LS + 3]),
                             reads=[zxs[c]], writes=[olc])

            def stageD(cs):
                banks_ax = {}
                for c in cs:
                    st = lru_sets[c]
                    P.op("dve", lambda e, st=st: e.tensor_copy(out=st["xcb"].ap[:, 0:L], in_=st["acc"].ap[:, 0:L]),
                         reads=[st["acc"]], writes=[st["xcb"]])
                    bka, bkx = newbank(), newbank()
                    P.op("pe", lambda e, st=st, c=c, bka=bka: e.matmul(bka.ap[:, 0:L], lhsT=wa_bd[:, c, :], rhs=st["xcb"].ap[:, 0:L],
                                                                       start=True, stop=True),
                         reads=[st["xcb"], wab_T], writes=[bka])
                    P.op("pe", lambda e, st=st, c=c, bkx=bkx: e.matmul(bkx.ap[:, 0:L], lhsT=wx_bd[:, c, :], rhs=st["xcb"].ap[:, 0:L],
                                                                       start=True, stop=True),
                         reads=[st["xcb"], wxb_T], writes=[bkx])
                    banks_ax[c] = (bka, bkx)
                for c in cs:
                    st = lru_sets[c]
                    ba_, bx_ = banks_ax[c]
                    P.op("act", act_fn(st["ti"].ap[:, 0:L], bx_.ap[:, 0:L], AF.Tanh, scale=0.5, bias=dcol(D_HBX, c)),
                         reads=[bx_, der], writes=[st["ti"]])
                for c in cs:
                    st = lru_sets[c]
                    ba_, bx_ = banks_ax[c]
                    P.op("act", act_fn(st["tr"].ap[:, 0:L], ba_.ap[:, 0:L], AF.Tanh, scale=0.5, bias=dcol(D_HBA, c)),
                         reads=[ba_, der], writes=[st["tr"]])

            def stageD2(cs):
                for c in cs:
                    st = lru_sets[c]
                    P.op("act", act_fn(st["a"].ap[:, 0:L], st["tr"].ap[:, 0:L], AF.Exp, scale=dcol(D_HC, c), bias=dcol(D_HC, c)),
                         reads=[st["tr"], der], writes=[st["a"]])
                    P.op("pool", lambda e, st=st: e.tensor_tensor(out=st["a2"].ap[:, 0:L], in0=st["a"].ap[:, 0:L],
                                                                  in1=st["a"].ap[:, 0:L], op=ALU.mult),
                         reads=[st["a"]], writes=[st["a2"]])

            def stageE(cs):
                for c in cs:
                    st = lru_sets[c]
                    P.op("act", act_fn(st["a2"].ap[:, 0:L], st["a2"].ap[:, 0:L], AF.Sqrt, scale=-1.0, bias=1.0),
                         reads=[], writes=[st["a2"]])

            def stageF1(cs):
                for c in cs:
                    st = lru_sets[c]
                    P.op("dve", lambda e, st=st: e.scalar_tensor_tensor(out=st["ti"].ap[:, 0:L], in0=st["ti"].ap[:, 0:L], scalar=1.0,
                                                                        in1=st["acc"].ap[:, 0:L], op0=ALU.add, op1=ALU.mult),
                         reads=[st["acc"]], writes=[st["ti"]])

            def stageF(cs):
                for c in cs:
                    st = lru_sets[c]
                    P.op("dve", lambda e, st=st: e.scalar_tensor_tensor(out=st["ti"].ap[:, 0:L], in0=st["ti"].ap[:, 0:L], scalar=0.5,
                                                                        in1=st["a2"].ap[:, 0:L], op0=ALU.mult, op1=ALU.mult),
                         reads=[st["a2"]], writes=[st["ti"]])
                for c in cs:
                    st = lru_sets[c]
                    hb = st["tr"]
                    if is_s:
                        a3s = st["a"].ap[:, 0:L].rearrange("p (s l) -> p s l", s=NS)
                        b3s = st["ti"].ap[:, 0:L].rearrange("p (s l) -> p s l", s=NS)
                        P.op("dve", lambda e, a3s=a3s, c=c: e.tensor_tensor(out=fixS.ap, in0=a3s[:, :, 0], in1=sth_t[:, c, :],
                                                                           op=ALU.mult),
                             reads=[st["a"], sth], writes=[fixS])
                        P.op("dve", lambda e, b3s=b3s: e.tensor_tensor(out=b3s[:, :, 0], in0=b3s[:, :, 0], in1=fixS.ap, op=ALU.add),
                             reads=[fixS], writes=[st["ti"]])
                        P.op("dve", lambda e, a3s=a3s: e.memset(a3s[:, :, 0:1], 0.0), writes=[st["a"]])
                        P.op("dve", lambda e, st=st, hb=hb: e.tensor_tensor_scan(
                            out=hb.ap[:, 0:L], data0=st["a"].ap[:, 0:L], data1=st["ti"].ap[:, 0:L], initial=0.0,
                            op0=ALU.mult, op1=ALU.add),
                            reads=[st["a"], st["ti"]], writes=[hb])
                        P.op("dve", lambda e, hb=hb, c=c: e.tensor_copy(
                            out=oh_t[:, c, 1:17], in_=hb.ap[:, 0:L].rearrange("p (s l) -> p s l", s=NS)[:, :, LS - 1]),
                            reads=[hb], writes=[oh])
                    else:
                        init = 0.0 if first_seq else oh_t[:, c, 0:1]
                        P.op("dve", lambda e, st=st, hb=hb, init=init: e.tensor_tensor_scan(
                            out=hb.ap[:, 0:L], data0=st["a"].ap[:, 0:L], data1=st["ti"].ap[:, 0:L], initial=init,
                            op0=ALU.mult, op1=ALU.add),
                            reads=[st["a"], st["ti"], oh], writes=[hb])
                        P.op("dve", lambda e, hb=hb, c=c: e.tensor_copy(out=oh_t[:, c, 0:1], in_=hb.ap[:, L - 1:L]),
                             reads=[hb], writes=[oh])
                    P.op("pool", lambda e, hb=hb, c=c: e.tensor_tensor(out=mixedT_a[:, c, col0:col0 + L], in0=hb.ap[:, 0:L],
                                                                       in1=gz[c].ap[:, 0:L], op=ALU.mult),
                         reads=[hb, gz[c]], writes=mT)

            def frontb():
                for c in range(4):
                    bk = w_in_mm(4 + c)
                    P.op("act", act_fn(gz[c].ap[:, 0:L], bk.ap[:, 0:L], AF.Gelu_apprx_tanh), reads=[bk], writes=[gz[c]])

            def conv():
                stageC([0, 1])
                stageC([2, 3])

            def poolG():
              for g, w in enumerate(POOL_W):
                  bk = newbank()
                  src = zps_t[:, g, :, :] if is_s else zpe[g].ap
                  srcT = zps[g] if is_s else zpe[g]
                  do_fix = first_seq and not is_s
                  if do_fix:
                      P.op("dve", lambda e, g=g: e.tensor_tensor_scan(out=fixS.ap, data0=ones16_t[:, :],
                                                                      data1=zpe[g].ap[:, 0, 15:31], initial=0.0,
                                                                      op0=ALU.mult, op1=ALU.add),
                           reads=[zpe[g], ones16], writes=[fixS])
                      P.op("dve", lambda e, g=g: e.tensor_tensor(out=fixSd.ap, in0=fixS.ap, in1=dvec_t[:, g, :], op=ALU.mult),
                           reads=[fixS, dvec], writes=[fixSd])

                  def pm(e, g=g, w=w, bk=bk, src=src, do_fix=do_fix):
                      ins = None
                      out3 = as3(bk.ap[:, 0:L])
                      for k in range(w):
                          ins = e.matmul(out3, lhsT=pws[:, g, :], rhs=src[:, :, 15 - k:15 - k + Ls],
                                         start=(k == 0), stop=False)
                      ins = e.matmul(out3, lhsT=pwn[:, g, :], rhs=src[:, :, 15:15 + Ls], start=False, stop=(not do_fix))
                      if do_fix:
                          ins = e.matmul(bk.ap[:, 0:16], lhsT=pwb[:, g, :], rhs=fixSd.ap, start=False, stop=True)
                      return ins
                  rd = [srcT, pws_T, pwn_T] + ([fixSd, pwb_T] if do_fix else [])
                  P.op("pe", pm, reads=rd, writes=[bk])
                  P.op("act", act_fn(mixedT_a[:, 4 + g, col0:col0 + L], bk.ap[:, 0:L], AF.Identity, scale=vcol(V_PS, g)),
                       reads=[bk, vecs], writes=mT)
            def S12():
                stageD([0, 1, 2, 3])
                stageF1([0, 1, 2, 3])

            def S34():
                stageD2([0, 1, 2, 3])
                stageE([0, 1, 2, 3])
                stageF([0, 1])
                stageF([2, 3])

            return {"fronta": fronta, "frontb": frontb, "conv": conv, "poolG": poolG, "S12": S12, "S34": S34}

        def wout_A(t):
            for half in range(2):
                bk = newbank()

                def mm(e, half=half, bk=bk):
                    ins = None
                    for kc in range(8):
                        ins = e.matmul(bk.ap[:, :], lhsT=mixedT_a[:, kc, t * 128:(t + 1) * 128],
                                       rhs=w_out_bf[:, kc, half * 512:(half + 1) * 512], start=(kc == 0), stop=(kc == 7))
                    return ins
                P.op("pe", mm, reads=[mixedTb[t], w_out_T], writes=[bk])
                xs_ = X[t].ap[:, half * 512:(half + 1) * 512]
                P.op("dve", lambda e, bk=bk, xs_=xs_: e.tensor_tensor(out=xs_, in0=bk.ap[:, :], in1=xs_, op=ALU.add),
                     reads=[bk, X[t]], writes=[X[t]])
            norm_A(X[t], grep, xn_m[t % 3])

        def wout_B(t):
            norm_B(xn_m[t % 3], h2T_t[:, :, 2 + t * 128: 2 + (t + 1) * 128], h2Tb[t])

        tail_pending = []
        mult_evs = {}

        def flush_tail():
            while tail_pending:
                tail_pending.pop(0)()

        def up_pair(gi, pr, stg, j, segs_p, has_s, nt_p, first_group, last_group, on_last_mm=None):
            cg, cv = pr, 24 + pr
            L = segs_p[0][1]
            N = L + 2
            ig, bg0, bg1 = newbank_pair()
            iv, bv0, bv1 = newbank_pair()
            bks = {0: (bg0, bg1), 1: (bv0, bv1)}
            ib = {0: ig, 1: iv}
            all_tiles = list(range(0, (2 * L) // 128))
            for si, (c0, L_) in enumerate(segs_p):
                tiles = list(range(c0 // 128, (c0 + L) // 128))
                rd = [h2Tb[t] for t in tiles] + list(upb_T[stg])
                rd.append(h2T_hist if c0 == 0 else h2Tb[c0 // 128 - 1])
                for gv in (0, 1):
                    bk = bks[gv][si]

                    def mm(e, gv=gv, bk=bk, c0=c0, N=N, L=L):
                        ins = None
                        for kc in range(8):
                            ins = e.matmul(bk.ap[:, 0:N], lhsT=upb[stg][:, gv, kc, j * 128:(j + 1) * 128],
                                           rhs=h2T_t[:, kc, c0: 2 + c0 + L], start=(kc == 0), stop=(kc == 7))
                        return ins
                    P.op("pe", mm, reads=rd, writes=[bk])
            if on_last_mm is not None and not has_s:
                on_last_mm()
            k_ = acc_rr[0] % NACC
            acc_rr[0] += 1
            acc = accA[k_]
            aT = accT[k_]
            gb = gbuf[k_]
            for gv, ch in ((0, cg), (1, cv)):
                P.op("act", act_fn(acc[:, gv, :, 0:L], ps_t[:, ib[gv]:ib[gv] + 2, 2:2 + L], AF.Identity,
                                   scale=vcol(V_FCW, 2 * 48 + ch), bias=vcol(V_FCB, ch)),
                     reads=list(bks[gv]) + [vecs], writes=[aT[gv]])
            for tap, sh in ((1, 1), (0, 2)):
                for gv, ch in ((0, cg), (1, cv)):
                    o_ = acc[:, gv, :, 0:L]
                    i_ = ps_t[:, ib[gv]:ib[gv] + 2, 2 - sh:2 - sh + L]
                    P.op("dve", lambda e, o_=o_, i_=i_, tap=tap, ch=ch: e.scalar_tensor_tensor(
                        out=o_, in0=i_, scalar=vcol(V_FCW, tap * 48 + ch), in1=o_, op0=ALU.mult, op1=ALU.add),
                        reads=list(bks[gv]) + [vecs], writes=[aT[gv]])
            if last_group:
                for gv, ch in ((0, cg), (1, cv)):
                    bk = bks[gv][1]
                    P.op("dve", lambda e, bk=bk, ch=ch, N=N: e.tensor_copy(out=offn_t[:, ch, 0, :], in_=bk.ap[:, N - 2:N]),
                         reads=[bk], writes=[offn])
            flush_tail()

            def tail(gb=gb, acc=acc, aT=aT, L=L, all_tiles=all_tiles):
                P.op("act", act_fn(gb.ap[:, :, 0:L], acc[:, 0, :, 0:L], AF.Gelu_apprx_tanh), reads=[aT[0]], writes=[gb])
                mult_evs[pr] = P.op("dve", lambda e: e.tensor_tensor(
                    out=actT_a[:, pr, 0:2 * L].rearrange("p (s l) -> p s l", s=2), in0=gb.ap[:, :, 0:L],
                    in1=acc[:, 1, :, 0:L], op=ALU.mult),
                    reads=[gb, aT[1]], writes=[actTb[t] for t in all_tiles])
            tail_pending.append(tail)
            if has_s:
                c0 = nt_p * 128
                L = NS * LS
                t = nt_p
                bg, bv = newbank(), newbank()
                ue = uexts[pr % 2]
                k_ = acc_rr[0] % NACC
                acc_rr[0] += 1
                acc = accA[k_][:, :, 0, :]
                aT = accT[k_]
                gb = T(gbuf[k_].ap[:, 0, :])
                gb_full = gbuf[k_]
                for gv, bk in ((0, bg), (1, bv)):
                    def mm(e, gv=gv, bk=bk, c0=c0, L=L):
                        ins = None
                        for kc in range(8):
                            ins = e.matmul(bk.ap[:, 0:L], lhsT=upb[stg][:, gv, kc, j * 128:(j + 1) * 128],
                                           rhs=h2T_t[:, kc, 2 + c0: 2 + c0 + L], start=(kc == 0), stop=(kc == 7))
                        return ins
                    P.op("pe", mm, reads=[h2Tb[t]] + list(upb_T[stg]), writes=[bk])
                if on_last_mm is not None:
                    on_last_mm()
                ueT = uext_halves[pr % 2]
                for gv, bk, ch in ((0, bg, cg), (1, bv, cv)):
                    P.op("act", act_fn(ue.ap[:, gv, :, 0:2], stffn_t[:, ch, :, :], AF.Copy), reads=[stffn], writes=[ueT[gv]])
                    b3 = bk.ap[:, 0:L].rearrange("p (s l) -> p s l", s=NS)
                    P.op("act", act_fn(ue.ap[:, gv, :, 2:2 + LS], b3, AF.Copy), reads=[bk], writes=[ueT[gv]])
                for gv, bk, ch in ((0, bg, cg), (1, bv, cv)):
                    a3 = acc[:, gv, 0:L].rearrange("p (s l) -> p s l", s=NS)
                    if gv == 0:
                        b3 = bk.ap[:, 0:L].rearrange("p (s l) -> p s l", s=NS)
                        P.op("act", act_fn(a3, b3, AF.Identity, scale=vcol(V_FCW, 2 * 48 + ch), bias=vcol(V_FCB, ch)),
                             reads=[bk, vecs], writes=[aT[gv]])
                    else:
                        P.op("dve", lambda e, gv=gv, ch=ch, a3=a3: e.tensor_scalar(
                            out=a3, in0=ue.ap[:, gv, :, 2:2 + LS], scalar1=vcol(V_FCW, 2 * 48 + ch), scalar2=vcol(V_FCB, ch),
                            op0=ALU.mult, op1=ALU.add), reads=[ueT[gv], vecs], writes=[aT[gv]])
                for tap, sh in ((1, 1), (0, 2)):
                    for gv, bk, ch in ((0, bg, cg), (1, bv, cv)):
                        a3 = acc[:, gv, 0:L].rearrange("p (s l) -> p s l", s=NS)
                        P.op("dve", lambda e, gv=gv, ch=ch, a3=a3, tap=tap, sh=sh: e.scalar_tensor_tensor(
                            out=a3, in0=ue.ap[:, gv, :, 2 - sh:2 - sh + LS], scalar=vcol(V_FCW, tap * 48 + ch), in1=a3,
                            op0=ALU.mult, op1=ALU.add), reads=[ueT[gv], vecs], writes=[aT[gv]])
                for gv, bk, ch in ((0, bg, cg), (1, bv, cv)):
                    P.op("act", act_fn(offn_t[:, ch, 1:17, :], ue.ap[:, gv, :, LS:LS + 2], AF.Copy), reads=[ueT[gv]], writes=[offn])
                flush_tail()

                def tail(gb=gb, gb_full=gb_full, acc=acc, aT=aT, c0=c0, L=L, t=t):
                    P.op("act", act_fn(gb.ap[:, 0:L], acc[:, 0, 0:L], AF.Gelu_apprx_tanh), reads=[aT[0]], writes=[gb_full])
                    P.op("dve", lambda e: e.tensor_tensor(
                        out=actT_a[:, pr, c0:c0 + L], in0=gb.ap[:, 0:L], in1=acc[:, 1, 0:L], op=ALU.mult),
                        reads=[gb_full, aT[1]], writes=[actTb[t]])
                tail_pending.append(tail)

        def down_final(gi, t, yrow0):
            CS = 20
            split = (t == 0 and (CS - 1) in mult_evs)
            bks2 = [newbank(), newbank()]

            def mk(half, bk):
                def mm(e, c_lo=0, c_hi=24):
                    ins = None
                    for c in range(c_lo, c_hi):
                        ins = e.matmul(bk.ap[:, :], lhsT=actT_a[:, c, t * 128:(t + 1) * 128],
                                       rhs=wd_bf[:, c, half * 512:(half + 1) * 512], start=(c == 0), stop=(c == 23))
                    return ins
                return mm
            mms = [mk(0, bks2[0]), mk(1, bks2[1])]
            if split:
                for half in range(2):
                    ev_ = P.op("pe", lambda e, mm=mms[half]: mm(e, c_lo=0, c_hi=CS), reads=wd_T[:CS // 2],
                               writes=[bks2[half]], extra=[mult_evs[CS - 1]])
                    actTb[t].readers.append(ev_)
                for half in range(2):
                    P.op("pe", lambda e, mm=mms[half]: mm(e, c_lo=CS, c_hi=24), reads=[actTb[t]] + wd_T[CS // 2:],
                         writes=[bks2[half]])
            else:
                for half in range(2):
                    P.op("pe", mms[half], reads=[actTb[t]] + wd_T, writes=[bks2[half]])
            for half in range(2):
                bk = bks2[half]
                xs_ = X[t].ap[:, half * 512:(half + 1) * 512]
                P.op("dve", lambda e, bk=bk, xs_=xs_: e.tensor_tensor(out=xs_, in0=bk.ap[:, :], in1=xs_, op=ALU.add),
                     reads=[bk, X[t]], writes=[X[t]])
            ssq, sq, rs = newstat(), newstat(), newstat()
            yb = ybuf[0]
            P.op("act", act_fn(yb.ap, X[t].ap, AF.Square, accum_out=ssq.ap), reads=[X[t]], writes=[yb, ssq])
            P.op("act", act_fn(sq.ap, ssq.ap, AF.Sqrt, scale=1.0 / D, bias=EPS), reads=[ssq], writes=[sq])
            P.op("dve", lambda e: e.reciprocal(out=rs.ap, in_=sq.ap), reads=[sq], writes=[rs])
            P.op("dve", lambda e: e.scalar_tensor_tensor(out=yb.ap, in0=X[t].ap, scalar=rs.ap, in1=grep.ap,
                                                         op0=ALU.mult, op1=ALU.mult),
                 reads=[X[t], rs, grep], writes=[yb])
            P.dma("sp", "d_y", y_d[yrow0:yrow0 + 128, :], yb.ap, reads=[yb])

        deferred_norm1 = []
        load_x(0)
        for gi, (p0, npt, has_s) in enumerate(GROUPS):
            nt_p = npt // 128
            nt = nt_p + (1 if has_s else 0)
            first_group = gi == 0
            last_group = gi == len(GROUPS) - 1
            if gi > 0:
                P.op("pool", lambda e: e.tensor_copy(out=h1T_a[:, :, 0:HX], in_=h1hist_t[:, :, 0:HX]),
                     reads=[h1hist], writes=[h1T_hist], extra=evs_p4)
                P.barrier()
            else:
                def p1_tiles(t0, t1):
                    for t in range(t0, t1 + 1):
                        if t < t1:
                            norm_A(X[t], grep, xn_m[t % 2])
                        if t >= t0 + 1:
                            norm_B(xn_m[(t - 1) % 2], h1T_a[:, :, HX + (t - 1) * 128: HX + t * 128], h1Tb[t - 1])
                p1_tiles(0, 3)
                deferred_norm1.append(lambda nt=nt: p1_tiles(3, nt))
            P.dma("pool", "d_w", w_out_bf[:, :, :], w_out_v[:, :, :], writes=[w_out_T])
            if gi == 0:
                scale_pool_weights()
            segs = []
            c0 = 0
            while c0 < npt:
                L = min(384 if npt % 384 == 0 else 256, npt - c0)
                segs.append((c0, L))
                c0 += L
            sg = []
            for si, (c0, L) in enumerate(segs):
                sg.append(mixer_segment(gi, c0, L, 1, L, list(range(c0 // 128, (c0 + L) // 128)), "p",
                                        first_seq=(first_group and si == 0),
                                        last_prompt=(last_group and si == len(segs) - 1)))
            if has_s:
                sg.append(mixer_segment(gi, nt_p * 128, NS * LS, NS, LS, [nt_p], "s", first_seq=False, last_prompt=False))
            sg[0]["fronta"]()
            sg[0]["frontb"]()
            sg[0]["conv"]()
            sg[0]["poolG"]()
            while deferred_norm1:
                deferred_norm1.pop(0)()
            for k in range(len(sg)):
                sg[k]["S12"]()
                if k + 1 < len(sg):
                    sg[k + 1]["fronta"]()
                    sg[k + 1]["conv"]()
                sg[k]["S34"]()
                if k + 1 < len(sg):
                    sg[k + 1]["frontb"]()
                    sg[k + 1]["poolG"]()
            if not last_group:
                P.op("pool", lambda e, npt=npt: e.tensor_copy(out=h1hist_t[:, :, 0:HX], in_=h1T_a[:, :, npt:npt + HX]),
                     reads=[h1Tb[nt_p - 1]], writes=[h1hist])
            half_p = npt // 2
            segs_p = [(0, half_p), (half_p, half_p)]
            nstages = 12

            def load_stage(s):
                stg = s % NST
                pr0 = s * 2
                P.dma("pool", "d_up", upb[stg][:, 0, :, :], up_v[:, :, pr0 * 128: pr0 * 128 + 256], writes=[upb_T[stg][0]])
                P.dma("pool", "d_up", upb[stg][:, 1, :, :], up_v[:, :, DFF + pr0 * 128: DFF + pr0 * 128 + 256],
                      writes=[upb_T[stg][1]])
            P._wait(P.q["pool"], [(q_.sem_key, q_.cnt) for q_ in P.q.values() if q_.cnt > 0 and q_.name != "pool"])
            for s0_ in range(NST):
                load_stage(s0_)
            P.dma("sp", "d_misc", grep.ap, g2r_d[:, :], writes=[grep])
            for t in range(nt + 2):
                if t < nt:
                    wout_A(t)
                if t >= 2:
                    wout_B(t - 2)
            P.barrier()
            P.dma("sp", "d_misc", grep.ap, gfr_d[:, :], writes=[grep])
            mult_evs.clear()
            for s in range(nstages):
                def prefetch(s=s):
                    if s + NST < nstages:
                        load_stage(s + NST)
                    P.dma("pool", "d_wd", wd_bf[:, s * 2:(s + 1) * 2, :], down_v[:, s * 2:(s + 1) * 2, :], writes=[wd_T[s]])
                up_pair(gi, s * 2, s % NST, 0, segs_p, has_s, nt_p, first_group, last_group)
                up_pair(gi, s * 2 + 1, s % NST, 1, segs_p, has_s, nt_p, first_group, last_group, on_last_mm=prefetch)
            flush_tail()
            if not last_group:
                P.op("dve", lambda e, npt=npt: e.tensor_copy(out=h2T_t[:, :, 0:2], in_=h2T_t[:, :, npt:npt + 2]),
                     reads=[h2Tb[nt_p - 1]], writes=[h2T_hist])
            if last_group:
                P.dma("sp", "d_y", olc_d[:, :], olc_t[:, :, :, :].rearrange("p a b c -> p (a b c)"), reads=[olc])
                P.dma("sp", "d_y", oh_d[:, :], oh_t[:, :, :].rearrange("p a b -> p (a b)"), reads=[oh])
                P.dma("sp", "d_y", opool_d[:, :], opool_t[:, :, :, :].rearrange("p a b c -> p (a b c)"), reads=[opool])
                P.dma("sp", "d_y", offn_d[:, :], offn_t[:, :, :, :].rearrange("p a b c -> p (a b c)"), reads=[offn])
            evs_p4 = [(q_.sem_key, q_.cnt) for q_ in P.q.values() if q_.cnt > 0 and q_.name != "sp"]
            if not last_group:
                for part in (0, 2, 1):
                    P.dma("pool", "d_w", w_in_bf[:, :, part * 512:(part + 1) * 512],
                          w_in_v[:, :, part * 512:(part + 1) * 512], writes=[w_in_T[part]], extra=evs_p4)
                np0, nnpt, nhs = GROUPS[gi + 1]
                nnt_p = nnpt // 128
                nnt = nnt_p + (1 if nhs else 0)
            else:
                nnt = 0
            doneA = doneB = 0

            def load_next(tt):
                if tt < nnt_p:
                    P.dma("sp", "d_x", X[tt].ap, xp[np0 + tt * 128: np0 + (tt + 1) * 128, :], writes=[X[tt]])
                else:
                    P.dma("sp", "d_x", X[tt].ap, xs[:, :], writes=[X[tt]])

            for t in range(nt):
                yrow0 = (p0 + t * 128) if t < nt_p else SEQ
                down_final(gi, t, yrow0)
                if t < nnt:
                    load_next(t)
                if doneB < doneA and doneB <= t - 2:
                    norm1_B(doneB, extra=evs_p4)
                    doneB += 1
                if doneA < nnt and doneA <= t - 1:
                    norm1_A(doneA, extra=evs_p4)
                    doneA += 1
            for tt in range(nt, nnt):
                load_next(tt)
            def norm1_tail(doneA=doneA, doneB=doneB, nnt=nnt, evs_p4=evs_p4):
                while doneB < nnt:
                    if doneA == doneB:
                        norm1_A(doneA, extra=evs_p4)
                        doneA += 1
                    norm1_B(doneB, extra=evs_p4)
                    doneB += 1
            deferred_norm1.append(norm1_tail)

        P.final_wait("sp")

        with nc.Block() as block:
            def play(eng, q):
                own = sems[q.sem_key]
                for item in q.ops:
                    if item[0] == "wait":
                        eng.wait_ge(sems[item[1]], item[2])
                    elif item[0] == "op":
                        item[1](eng).then_inc(own, 1)
                    else:
                        eng.dma_start(out=item[1], in_=item[2]).then_inc(sems[item[3]], 16)

            @block.sync
            def _(e):
                play(e, P.q["sp"])

            @block.scalar
            def _(e):
                play(e, P.q["act"])

            @block.vector
            def _(e):
                play(e, P.q["dve"])

            @block.gpsimd
            def _(e):
                play(e, P.q["pool"])

            @block.tensor
            def _(e):
                play(e, P.q["pe"])
    build_program.last_prog = P
    return nc


_NC_CACHE = {}


def _layout_vec(v, nchunk):
    return np.ascontiguousarray(np.asarray(v, np.float32).reshape(nchunk, 128).T)


def kernel(x_prompt, x_sample, state_lru_conv, state_lru_h, state_pool, state_ffn_conv,
           norm1_g, w_in, lru_conv_w, lru_conv_b, lru_wa, lru_ba, lru_wx, lru_bx, lru_lambda,
           pool_w, pool_scale, w_out, norm2_g, ffn_up, ffn_conv_w, ffn_conv_b, ffn_down, final_g):
    f = lambda a: np.ascontiguousarray(np.asarray(a, dtype=np.float32))
    x_prompt, x_sample = f(x_prompt), f(x_sample)
    vecs = np.zeros((128, NV), np.float32)
    cw = f(lru_conv_w)[0]
    for j in range(4):
        vecs[:, V_CW + j * 4: V_CW + (j + 1) * 4] = _layout_vec(cw[j], 4)
    vecs[:, V_CB:V_CB + 4] = _layout_vec(f(lru_conv_b)[0], 4)
    vecs[:, V_BA:V_BA + 4] = _layout_vec(f(lru_ba)[0], 4)
    vecs[:, V_BX:V_BX + 4] = _layout_vec(f(lru_bx)[0], 4)
    vecs[:, V_LAM:V_LAM + 4] = _layout_vec(f(lru_lambda)[0], 4)
    vecs[:, V_PS:V_PS + 4] = _layout_vec(f(pool_scale)[0], 4)
    fcw = f(ffn_conv_w)[0]
    for j in range(3):
        vecs[:, V_FCW + j * 48: V_FCW + (j + 1) * 48] = _layout_vec(fcw[j], 48)
    vecs[:, V_FCB:V_FCB + 48] = _layout_vec(f(ffn_conv_b)[0], 48)
    vecs[:, V_G1:V_G1 + 8] = _layout_vec(f(norm1_g)[0], 8)
    rep = lambda g: np.ascontiguousarray(np.broadcast_to(f(g).reshape(1, D), (128, D)))
    common = {
        "vecs": vecs, "g1r": rep(norm1_g[0]), "g2r": rep(norm2_g[0]), "gfr": rep(final_g),
        "ident": np.eye(128, dtype=np.float32),
        "w_in": f(w_in)[0], "w_out": f(w_out)[0], "ffn_up": f(ffn_up)[0], "ffn_down": f(ffn_down)[0],
        "wa": f(lru_wa)[0], "wx": f(lru_wx)[0], "pw": f(pool_w)[0],
    }
    slc, slh, spl, sff = f(state_lru_conv)[0], f(state_lru_h)[0], f(state_pool)[0], f(state_ffn_conv)[0]
    in_maps = []
    for c in range(NCORES):
        s0, s1 = c * NS, (c + 1) * NS
        m = dict(common)
        m["xp"] = x_prompt[c]
        m["xs"] = x_sample[s0:s1].reshape(NS * LS, D)
        m["stlc"] = np.ascontiguousarray(slc[s0:s1].reshape(NS, 3, 4, 128).transpose(3, 2, 0, 1)).reshape(128, -1)
        m["sth"] = np.ascontiguousarray(slh[s0:s1].reshape(NS, 4, 128).transpose(2, 1, 0)).reshape(128, -1)
        m["stpool"] = np.ascontiguousarray(spl[s0:s1].reshape(NS, 15, 4, 128).transpose(3, 2, 0, 1)).reshape(128, -1)
        m["stffn"] = np.ascontiguousarray(sff[s0:s1].reshape(NS, 2, 48, 128).transpose(3, 2, 0, 1)).reshape(128, -1)
        in_maps.append(m)

    if "nc" not in _NC_CACHE:
        _NC_CACHE["nc"] = build_program()
    nc = _NC_CACHE["nc"]
    res = run_bass_kernel_spmd(nc, in_maps, core_ids=list(range(NCORES)))
    R = res.results

    y_prompt = np.empty((8, SEQ, D), np.float32)
    y_sample = np.empty((128, LS, D), np.float32)
    p_lc = np.empty((1, 8, 3, 512), np.float32)
    p_h = np.empty((1, 8, 512), np.float32)
    p_pool = np.empty((1, 8, 15, 512), np.float32)
    p_ffn = np.empty((1, 8, 2, 6144), np.float32)
    s_lc = np.empty((1, 128, 3, 512), np.float32)
    s_h = np.empty((1, 128, 512), np.float32)
    s_pool = np.empty((1, 128, 15, 512), np.float32)
    s_ffn = np.empty((1, 128, 2, 6144), np.float32)
    for c in range(NCORES):
        s0, s1 = c * NS, (c + 1) * NS
        y = np.asarray(R[c]["y"], np.float32)
        y_prompt[c] = y[:SEQ]
        y_sample[s0:s1] = y[SEQ:].reshape(NS, LS, D)
        olc = np.asarray(R[c]["o_lc"], np.float32).reshape(128, 4, 17, 3).transpose(2, 3, 1, 0).reshape(17, 3, 512)
        ohh = np.asarray(R[c]["o_h"], np.float32).reshape(128, 4, 17).transpose(2, 1, 0).reshape(17, 512)
        opl = np.asarray(R[c]["o_pool"], np.float32).reshape(128, 4, 17, 15).transpose(2, 3, 1, 0).reshape(17, 15, 512)
        off = np.asarray(R[c]["o_ffn"], np.float32).reshape(128, 48, 17, 2).transpose(2, 3, 1, 0).reshape(17, 2, 6144)
        p_lc[0, c], s_lc[0, s0:s1] = olc[0], olc[1:]
        p_h[0, c], s_h[0, s0:s1] = ohh[0], ohh[1:]
        p_pool[0, c], s_pool[0, s0:s1] = opl[0], opl[1:]
        p_ffn[0, c], s_ffn[0, s0:s1] = off[0], off[1:]
    return (y_prompt, y_sample, p_lc, p_h, p_pool, p_ffn, s_lc, s_h, s_pool, s_ffn)
```
